# Optimizing a Trainium2 kernel written in Bass

```python
import jax
import jax.numpy as jnp
from jax import lax
import numpy as np

D_MODEL = 1024
BATCH = 16
SEQ = 2048
DEPTH = 2
DEC_BATCH = 128
DEC_SEQ = 8
PAST_LEN = 16384
PAGE_SIZE = 128

N_GROUPS = 4
GROUP_W = D_MODEL // N_GROUPS
HEAD_DIM = 64
GROUP_HEADS = GROUP_W // HEAD_DIM

MLA_HEADS = GROUP_HEADS
Q_LORA = 3 * GROUP_W // 4
KV_LORA = GROUP_W // 2
QK_NOPE = HEAD_DIM
QK_ROPE = HEAD_DIM // 2
V_DIM = HEAD_DIM
ROPE_BASE = 10000.0
Q_BLOCK = 128
MLA_SCALE = (QK_NOPE + QK_ROPE) ** -0.5

CONV_W = 31
CONV_DIM = GROUP_W
CONV_GROUPS = GROUP_HEADS

CHUNK = 128
SGU_DIM = GROUP_W
SGU_HEADS = GROUP_HEADS

RW_DIM = GROUP_W
RW_HEADS = GROUP_HEADS
RW_N = HEAD_DIM
DECAY_LORA = 64
AAA_LORA = 64
GATE_LORA = 128

N_MEM = 256
X_HEADS = 4
X_HEAD_DIM = 128

D_FF = ((8 * D_MODEL + 3 * 256 - 1) // (3 * 256)) * 256

MLA_COLS = Q_LORA + KV_LORA + QK_ROPE
CONV_COLS = 2 * CONV_DIM
SGU_COLS = 2 * SGU_DIM
RW_COLS = 3 * RW_DIM + DECAY_LORA + AAA_LORA + GATE_LORA
IN_COLS = MLA_COLS + CONV_COLS + SGU_COLS + RW_COLS

EPS = 1e-6
LN_EPS = 1e-5
RW_LN_EPS = 64e-5
NEG = -1e30

kernel_name = 'hymba_mla_conformer_gmlp_rwkv7_step'


def rmsnorm(x, g, eps=EPS):
    xf = x.astype(jnp.float32)
    y = xf * lax.rsqrt(jnp.mean(xf * xf, axis=-1, keepdims=True) + eps)
    return (y * g.astype(jnp.float32)).astype(x.dtype)


def layernorm(x, g, b, eps=LN_EPS):
    xf = x.astype(jnp.float32)
    xc = xf - jnp.mean(xf, axis=-1, keepdims=True)
    y = xc * lax.rsqrt(jnp.mean(xc * xc, axis=-1, keepdims=True) + eps)
    return (y * g.astype(jnp.float32) + b.astype(jnp.float32)).astype(x.dtype)


def rope(x, pos):
    half = x.shape[-1] // 2
    inv = jnp.power(ROPE_BASE, -jnp.arange(half, dtype=jnp.float32) / half)
    ang = pos.astype(jnp.float32)[:, None] * inv[None, :]
    cos = jnp.cos(ang)[None, :, None, :]
    sin = jnp.sin(ang)[None, :, None, :]
    xf = x.astype(jnp.float32)
    x1, x2 = xf[..., :half], xf[..., half:]
    return jnp.concatenate([x1 * cos - x2 * sin, x1 * sin + x2 * cos], axis=-1).astype(x.dtype)


def mla_queries_latent(pa, pos, p):
    B, T, _ = pa.shape
    cq = rmsnorm(pa[..., :Q_LORA], p['mla_q_norm'])
    ckv = rmsnorm(pa[..., Q_LORA:Q_LORA + KV_LORA], p['mla_kv_norm'])
    kpe = pa[..., Q_LORA + KV_LORA:].reshape(B, T, 1, QK_ROPE)
    q = (cq @ p['mla_w_uq']).reshape(B, T, MLA_HEADS, QK_NOPE + QK_ROPE)
    q_nope = rmsnorm(q[..., :QK_NOPE], p['mla_gq_nope'])
    q_pe = rope(rmsnorm(q[..., QK_NOPE:], p['mla_gq_rope']), pos)
    kpe = rope(rmsnorm(kpe, p['mla_gk_rope']), pos)[:, :, 0]
    return q_nope, q_pe, ckv, kpe


def mla_expand(ckv, p):
    lead = ckv.shape[:-1]
    k_nope = rmsnorm((ckv @ p['mla_w_uk']).reshape(*lead, MLA_HEADS, QK_NOPE), p['mla_gk_nope'])
    v = (ckv @ p['mla_w_uv']).reshape(*lead, MLA_HEADS, V_DIM)
    return k_nope, v


def mla_scores(q_nope, q_pe, k_nope, kpe):
    s = jnp.einsum('bqhd,bkhd->bhqk', q_nope, k_nope) + jnp.einsum('bqhr,bkr->bhqk', q_pe, kpe)
    return s.astype(jnp.float32) * MLA_SCALE


def mla_attend_prompt(q_nope, q_pe, k_nope, v, kpe):
    B, T = q_nope.shape[:2]
    nb = T // Q_BLOCK
    qn = q_nope.reshape(B, nb, Q_BLOCK, MLA_HEADS, QK_NOPE).swapaxes(0, 1)
    qr = q_pe.reshape(B, nb, Q_BLOCK, MLA_HEADS, QK_ROPE).swapaxes(0, 1)
    kpos = jnp.arange(T)

    def block(args):
        i, qn_b, qr_b = args
        s = mla_scores(qn_b, qr_b, k_nope, kpe)
        qpos = i * Q_BLOCK + jnp.arange(Q_BLOCK)
        s = jnp.where(kpos[None, :] <= qpos[:, None], s, NEG)
        pr = jax.nn.softmax(s, axis=-1)
        return jnp.einsum('bhqk,bkhd->bqhd', pr.astype(v.dtype), v)

    o = lax.map(block, (jnp.arange(nb), qn, qr))
    return o.swapaxes(0, 1).reshape(B, T, MLA_HEADS * V_DIM)


def mla_attend_sample(q_nope, q_pe, k_nope, v, kpe, cache_ckv, cache_kpe, page_table, layer, p):
    B, Q = q_nope.shape[:2]
    s = mla_scores(q_nope, q_pe, k_nope, kpe)
    s = jnp.where(jnp.tril(jnp.ones((Q, Q), dtype=bool)), s, NEG)
    m = jnp.max(s, axis=-1, keepdims=True)
    e = jnp.exp(s - m)
    l = jnp.sum(e, axis=-1, keepdims=True)
    acc = jnp.einsum('bhqk,bkhd->bhqd', e, v.astype(jnp.float32))

    def page_step(carry, phys):
        m, l, acc = carry
        ckv_p = cache_ckv[phys, layer]
        kpe_p = cache_kpe[phys, layer]
        kn_p, v_p = mla_expand(ckv_p, p)
        sp = mla_scores(q_nope, q_pe, kn_p, kpe_p)
        m_new = jnp.maximum(m, jnp.max(sp, axis=-1, keepdims=True))
        corr = jnp.exp(m - m_new)
        ep = jnp.exp(sp - m_new)
        l = l * corr + jnp.sum(ep, axis=-1, keepdims=True)
        acc = acc * corr + jnp.einsum('bhqk,bkhd->bhqd', ep, v_p.astype(jnp.float32))
        return (m_new, l, acc), None

    (m, l, acc), _ = lax.scan(page_step, (m, l, acc), page_table.T)
    o = (acc / l).astype(q_nope.dtype)
    return o.transpose(0, 2, 1, 3).reshape(B, Q, MLA_HEADS * V_DIM)


def conv_module(pb, conv_state, p):
    a, gate = pb[..., :CONV_DIM], pb[..., CONV_DIM:]
    glu = a * jax.nn.sigmoid(gate)
    xin = jnp.concatenate([conv_state.astype(glu.dtype), glu], axis=1)
    y = lax.conv_general_dilated(xin, p['conv_w'].astype(xin.dtype)[:, None, :],
                                 window_strides=(1,), padding='VALID',
                                 dimension_numbers=('NWC', 'WIO', 'NWC'),
                                 feature_group_count=CONV_DIM)
    y = y + p['conv_b']
    B, T, _ = y.shape
    gw = CONV_DIM // CONV_GROUPS
    y = layernorm(y.reshape(B, T, CONV_GROUPS, gw), p['conv_norm_g'].reshape(CONV_GROUPS, gw),
                  p['conv_norm_b'].reshape(CONV_GROUPS, gw)).reshape(B, T, CONV_DIM)
    y = jax.nn.silu(y) @ p['conv_pw']
    return y, xin[:, -(CONV_W - 1):]


def spatial_gating(pc, p):
    z = jax.nn.gelu(pc)
    u, v = z[..., :SGU_DIM], z[..., SGU_DIM:]
    v = layernorm(v, p['sgu_norm_g'], p['sgu_norm_b'])
    B, T, _ = v.shape
    L = min(T, CHUNK)
    nc = T // L
    w = p['sgu_w'][:, :L, :L] * jnp.tril(jnp.ones((L, L), p['sgu_w'].dtype))
    vc = v.reshape(B, nc, L, SGU_HEADS, SGU_DIM // SGU_HEADS)
    sv = jnp.einsum('hij,bcjhd->bcihd', w, vc) + p['sgu_b'][:, :L].T[None, None, :, :, None]
    return u * sv.reshape(B, T, SGU_DIM), v


def rwkv_mix(pd, shift_state, wkv_state, p):
    B, T, _ = pd.shape
    prev = jnp.concatenate([shift_state[:, None].astype(pd.dtype), pd[:, :-1]], axis=1)
    xs = pd + (prev - pd) * p['rw_mu']
    o1, o2, o3 = RW_DIM, 2 * RW_DIM, 3 * RW_DIM
    o4 = o3 + DECAY_LORA
    o5 = o4 + AAA_LORA
    r, k, v = xs[..., :o1], xs[..., o1:o2], xs[..., o2:o3]
    xw, xa, xg = xs[..., o3:o4], xs[..., o4:o5], xs[..., o5:]
    w = -jax.nn.softplus(-(p['rw_w0'] + jnp.tanh(xw) @ p['rw_w2'])) - 0.5
    decay = jnp.exp(-jnp.exp(w.astype(jnp.float32)))
    a = jax.nn.sigmoid(p['rw_a0'] + xa @ p['rw_a2'])
    g = jax.nn.sigmoid(xg) @ p['rw_g2']

    def hs(t):
        return t.reshape(B, T, RW_HEADS, RW_N)

    kk = hs(k * p['rw_kk']).astype(jnp.float32)
    kk = kk * lax.rsqrt(jnp.sum(kk * kk, axis=-1, keepdims=True) + 1e-12)
    k = k * (1.0 + (a - 1.0) * p['rw_ka'])
    rh, kh, vh, ah, dh = hs(r), hs(k), hs(v), hs(a), hs(decay)
    seqs = tuple(t.astype(jnp.float32).transpose(1, 0, 2, 3) for t in (rh, dh, kh, vh, kk, ah))

    def step(S, inp):
        r_t, w_t, k_t, v_t, kk_t, a_t = inp
        sa = jnp.einsum('bhij,bhj->bhi', S, -kk_t)
        S = (S * w_t[:, :, None, :] + sa[..., None] * (kk_t * a_t)[:, :, None, :]
             + v_t[..., None] * k_t[:, :, None, :])
        return S, jnp.einsum('bhij,bhj->bhi', S, r_t)

    S_fin, y = lax.scan(step, wkv_state.astype(jnp.float32), seqs)
    y = y.transpose(1, 0, 2, 3)
    y = layernorm(y, p['rw_ln_g'].reshape(RW_HEADS, RW_N), p['rw_ln_b'].reshape(RW_HEADS, RW_N),
                  eps=RW_LN_EPS)
    bonus = jnp.sum((rh * kh * p['rw_rk']).astype(jnp.float32), axis=-1, keepdims=True)
    y = y + bonus * vh.astype(jnp.float32)
    y = (y.reshape(B, T, RW_DIM) * g).astype(pd.dtype)
    return y, pd[:, -1], S_fin


def memory_kv(mem, p):
    B, M, _ = mem.shape
    mn = rmsnorm(mem, p['mem_norm'])
    k = rmsnorm((mn @ p['wk_x']).reshape(B, M, X_HEADS, X_HEAD_DIM), p['xk_norm'])
    v = (mn @ p['wv_x']).reshape(B, M, X_HEADS, X_HEAD_DIM)
    return k, v


def trunk_layer(x, pos, attend, mem_k, mem_v, conv_state, shift_state, wkv_state, p):
    B, T, _ = x.shape
    proj = rmsnorm(x, p['norm_mix']) @ p['w_in']
    c1 = MLA_COLS
    c2 = c1 + CONV_COLS
    c3 = c2 + SGU_COLS
    pa, pb, pc, pd = proj[..., :c1], proj[..., c1:c2], proj[..., c2:c3], proj[..., c3:]
    q_nope, q_pe, ckv, kpe = mla_queries_latent(pa, pos, p)
    k_nope, v = mla_expand(ckv, p)
    oa = attend(q_nope, q_pe, k_nope, v, kpe)
    ob, conv_new = conv_module(pb, conv_state, p)
    oc, v_sgu = spatial_gating(pc, p)
    od, shift_new, wkv_new = rwkv_mix(pd, shift_state, wkv_state, p)
    o = jnp.concatenate([oa, ob, oc, od], axis=-1).reshape(B, T, N_GROUPS, GROUP_W)
    o = rmsnorm(o, p['out_norm'].reshape(N_GROUPS, GROUP_W)).reshape(B, T, D_MODEL)
    x = x + o @ p['w_out']
    q = rmsnorm((rmsnorm(x, p['norm_x']) @ p['wq_x']).reshape(B, T, X_HEADS, X_HEAD_DIM), p['xq_norm'])
    s = jnp.einsum('bthd,bmhd->bhtm', q, mem_k).astype(jnp.float32) * (X_HEAD_DIM ** -0.5)
    pr = jax.nn.softmax(s, axis=-1)
    xo = jnp.einsum('bhtm,bmhd->bthd', pr.astype(mem_v.dtype), mem_v).reshape(B, T, X_HEADS * X_HEAD_DIM)
    x = x + xo @ p['wo_x']
    hf = rmsnorm(x, p['norm_ffn']) @ p['w_ffn_in']
    x = x + (jax.nn.silu(hf[..., :D_FF]) * hf[..., D_FF:]) @ p['w_ffn_out']
    return x, ckv, kpe, conv_new, shift_new, wkv_new, v_sgu


def _normal(k, shape, scale):
    return jax.random.normal(k, shape, jnp.float32) * scale


def setup_inputs(seed: int = 0) -> dict:
    key = jax.random.key(seed)
    ks = iter(jax.random.split(key, 80))
    n_pages = PAST_LEN // PAGE_SIZE
    n_phys = (DEC_BATCH * n_pages * 5) // 4
    L = DEPTH

    def gain(shape):
        return 1.0 + _normal(next(ks), shape, 0.1)

    d = {}
    d['x_prompt'] = _normal(next(ks), (BATCH, SEQ, D_MODEL), 1.0)
    d['x_sample'] = _normal(next(ks), (DEC_BATCH, DEC_SEQ, D_MODEL), 1.0)
    d['mem_prompt'] = _normal(next(ks), (BATCH, N_MEM, D_MODEL), 1.0)
    d['cache_ckv'] = _normal(next(ks), (n_phys, DEPTH, PAGE_SIZE, KV_LORA), 1.0)
    d['cache_kpe'] = _normal(next(ks), (n_phys, DEPTH, PAGE_SIZE, QK_ROPE), 1.0)
    d['cache_mem_k'] = _normal(next(ks), (DEPTH, DEC_BATCH, N_MEM, X_HEADS, X_HEAD_DIM), 1.0)
    d['cache_mem_v'] = _normal(next(ks), (DEPTH, DEC_BATCH, N_MEM, X_HEADS, X_HEAD_DIM), 1.0)
    d['state_conv'] = _normal(next(ks), (DEPTH, DEC_BATCH, CONV_W - 1, CONV_DIM), 0.5)
    d['state_shift'] = _normal(next(ks), (DEPTH, DEC_BATCH, RW_COLS), 1.0)
    d['state_wkv'] = _normal(next(ks), (DEPTH, DEC_BATCH, RW_HEADS, RW_N, RW_N), 0.3)
    perm = jax.random.permutation(next(ks), n_phys)[:DEC_BATCH * n_pages]
    d['page_table'] = perm.reshape(DEC_BATCH, n_pages).astype(jnp.int32)
    d['norm_mix'] = gain((L, D_MODEL))
    d['w_in'] = _normal(next(ks), (L, D_MODEL, IN_COLS), D_MODEL ** -0.5)
    d['mla_q_norm'] = gain((L, Q_LORA))
    d['mla_kv_norm'] = gain((L, KV_LORA))
    d['mla_w_uq'] = _normal(next(ks), (L, Q_LORA, MLA_HEADS * (QK_NOPE + QK_ROPE)), Q_LORA ** -0.5)
    d['mla_w_uk'] = _normal(next(ks), (L, KV_LORA, MLA_HEADS * QK_NOPE), KV_LORA ** -0.5)
    d['mla_w_uv'] = _normal(next(ks), (L, KV_LORA, MLA_HEADS * V_DIM), KV_LORA ** -0.5)
    d['mla_gq_nope'] = gain((L, QK_NOPE))
    d['mla_gq_rope'] = gain((L, QK_ROPE))
    d['mla_gk_nope'] = gain((L, QK_NOPE))
    d['mla_gk_rope'] = gain((L, QK_ROPE))
    d['conv_w'] = _normal(next(ks), (L, CONV_W, CONV_DIM), CONV_W ** -0.5)
    d['conv_b'] = _normal(next(ks), (L, CONV_DIM), 0.02)
    d['conv_norm_g'] = gain((L, CONV_DIM))
    d['conv_norm_b'] = _normal(next(ks), (L, CONV_DIM), 0.02)
    d['conv_pw'] = _normal(next(ks), (L, CONV_DIM, CONV_DIM), CONV_DIM ** -0.5)
    d['sgu_norm_g'] = gain((L, SGU_DIM))
    d['sgu_norm_b'] = _normal(next(ks), (L, SGU_DIM), 0.02)
    d['sgu_w'] = _normal(next(ks), (L, SGU_HEADS, CHUNK, CHUNK), CHUNK ** -0.5)
    d['sgu_b'] = gain((L, SGU_HEADS, CHUNK))
    d['rw_mu'] = jax.random.uniform(next(ks), (L, RW_COLS), jnp.float32)
    d['rw_w0'] = _normal(next(ks), (L, RW_DIM), 0.5)
    d['rw_w2'] = _normal(next(ks), (L, DECAY_LORA, RW_DIM), 0.5 * DECAY_LORA ** -0.5)
    d['rw_a0'] = _normal(next(ks), (L, RW_DIM), 0.5)
    d['rw_a2'] = _normal(next(ks), (L, AAA_LORA, RW_DIM), 0.5 * AAA_LORA ** -0.5)
    d['rw_g2'] = _normal(next(ks), (L, GATE_LORA, RW_DIM), GATE_LORA ** -0.5)
    d['rw_kk'] = 0.85 + _normal(next(ks), (L, RW_DIM), 0.1)
    d['rw_ka'] = gain((L, RW_DIM))
    d['rw_rk'] = _normal(next(ks), (L, RW_HEADS, RW_N), 0.1)
    d['rw_ln_g'] = gain((L, RW_DIM))
    d['rw_ln_b'] = _normal(next(ks), (L, RW_DIM), 0.02)
    d['out_norm'] = gain((L, D_MODEL))
    d['w_out'] = _normal(next(ks), (L, D_MODEL, D_MODEL), D_MODEL ** -0.5)
    d['norm_x'] = gain((L, D_MODEL))
    d['mem_norm'] = gain((L, D_MODEL))
    d['wq_x'] = _normal(next(ks), (L, D_MODEL, X_HEADS * X_HEAD_DIM), D_MODEL ** -0.5)
    d['wk_x'] = _normal(next(ks), (L, D_MODEL, X_HEADS * X_HEAD_DIM), D_MODEL ** -0.5)
    d['wv_x'] = _normal(next(ks), (L, D_MODEL, X_HEADS * X_HEAD_DIM), D_MODEL ** -0.5)
    d['xq_norm'] = gain((L, X_HEAD_DIM))
    d['xk_norm'] = gain((L, X_HEAD_DIM))
    d['wo_x'] = _normal(next(ks), (L, X_HEADS * X_HEAD_DIM, D_MODEL), (X_HEADS * X_HEAD_DIM) ** -0.5)
    d['norm_ffn'] = gain((L, D_MODEL))
    d['w_ffn_in'] = _normal(next(ks), (L, D_MODEL, 2 * D_FF), D_MODEL ** -0.5)
    d['w_ffn_out'] = _normal(next(ks), (L, D_FF, D_MODEL), D_FF ** -0.5)
    return d


def reference(x_prompt, x_sample, mem_prompt, cache_ckv, cache_kpe, cache_mem_k, cache_mem_v,
              state_conv, state_shift, state_wkv, page_table,
              norm_mix, w_in, mla_q_norm, mla_kv_norm, mla_w_uq, mla_w_uk, mla_w_uv,
              mla_gq_nope, mla_gq_rope, mla_gk_nope, mla_gk_rope,
              conv_w, conv_b, conv_norm_g, conv_norm_b, conv_pw,
              sgu_norm_g, sgu_norm_b, sgu_w, sgu_b,
              rw_mu, rw_w0, rw_w2, rw_a0, rw_a2, rw_g2, rw_kk, rw_ka, rw_rk, rw_ln_g, rw_ln_b,
              out_norm, w_out,
              norm_x, mem_norm, wq_x, wk_x, wv_x, xq_norm, xk_norm, wo_x,
              norm_ffn, w_ffn_in, w_ffn_out):
    Bp, Tp, _ = x_prompt.shape
    Bs, Ts, _ = x_sample.shape
    pos_p = jnp.arange(Tp, dtype=jnp.int32)
    pos_s = PAST_LEN + jnp.arange(Ts, dtype=jnp.int32)
    y_p, y_s = x_prompt, x_sample
    ckv_p_l, kpe_p_l, memk_l, memv_l, conv_p_l, shift_p_l, wkv_p_l = [], [], [], [], [], [], []
    ckv_s_l, kpe_s_l, conv_s_l, shift_s_l, wkv_s_l, sgu_s_l = [], [], [], [], [], []
    for l in range(DEPTH):
        p = {
            'norm_mix': norm_mix[l], 'w_in': w_in[l],
            'mla_q_norm': mla_q_norm[l], 'mla_kv_norm': mla_kv_norm[l],
            'mla_w_uq': mla_w_uq[l], 'mla_w_uk': mla_w_uk[l], 'mla_w_uv': mla_w_uv[l],
            'mla_gq_nope': mla_gq_nope[l], 'mla_gq_rope': mla_gq_rope[l],
            'mla_gk_nope': mla_gk_nope[l], 'mla_gk_rope': mla_gk_rope[l],
            'conv_w': conv_w[l], 'conv_b': conv_b[l], 'conv_norm_g': conv_norm_g[l],
            'conv_norm_b': conv_norm_b[l], 'conv_pw': conv_pw[l],
            'sgu_norm_g': sgu_norm_g[l], 'sgu_norm_b': sgu_norm_b[l], 'sgu_w': sgu_w[l], 'sgu_b': sgu_b[l],
            'rw_mu': rw_mu[l], 'rw_w0': rw_w0[l], 'rw_w2': rw_w2[l], 'rw_a0': rw_a0[l], 'rw_a2': rw_a2[l],
            'rw_g2': rw_g2[l], 'rw_kk': rw_kk[l], 'rw_ka': rw_ka[l], 'rw_rk': rw_rk[l],
            'rw_ln_g': rw_ln_g[l], 'rw_ln_b': rw_ln_b[l],
            'out_norm': out_norm[l], 'w_out': w_out[l],
            'norm_x': norm_x[l], 'mem_norm': mem_norm[l], 'wq_x': wq_x[l], 'wk_x': wk_x[l],
            'wv_x': wv_x[l], 'xq_norm': xq_norm[l], 'xk_norm': xk_norm[l], 'wo_x': wo_x[l],
            'norm_ffn': norm_ffn[l], 'w_ffn_in': w_ffn_in[l], 'w_ffn_out': w_ffn_out[l],
        }
        mk, mv = memory_kv(mem_prompt, p)
        conv0 = jnp.zeros((Bp, CONV_W - 1, CONV_DIM), x_prompt.dtype)
        shift0 = jnp.zeros((Bp, RW_COLS), x_prompt.dtype)
        wkv0 = jnp.zeros((Bp, RW_HEADS, RW_N, RW_N), jnp.float32)
        y_p, ckv_p, kpe_p, conv_p, shift_p, wkv_p, _ = trunk_layer(
            y_p, pos_p, mla_attend_prompt, mk, mv, conv0, shift0, wkv0, p)
        ckv_p_l.append(ckv_p)
        kpe_p_l.append(kpe_p)
        memk_l.append(mk)
        memv_l.append(mv)
        conv_p_l.append(conv_p)
        shift_p_l.append(shift_p)
        wkv_p_l.append(wkv_p)

        def attend_s(qn, qr, kn, vv, kp, layer=l, prm=p):
            return mla_attend_sample(qn, qr, kn, vv, kp, cache_ckv, cache_kpe, page_table, layer, prm)

        y_s, ckv_s, kpe_s, conv_s, shift_s, wkv_s, sgu_s = trunk_layer(
            y_s, pos_s, attend_s, cache_mem_k[l], cache_mem_v[l],
            state_conv[l], state_shift[l], state_wkv[l], p)
        ckv_s_l.append(ckv_s)
        kpe_s_l.append(kpe_s)
        conv_s_l.append(conv_s)
        shift_s_l.append(shift_s)
        wkv_s_l.append(wkv_s)
        sgu_s_l.append(sgu_s)

    n_pp = Tp // PAGE_SIZE
    ckv_prompt = jnp.stack(ckv_p_l, axis=1).reshape(Bp, DEPTH, n_pp, PAGE_SIZE, KV_LORA).transpose(0, 2, 1, 3, 4)
    kpe_prompt = jnp.stack(kpe_p_l, axis=1).reshape(Bp, DEPTH, n_pp, PAGE_SIZE, QK_ROPE).transpose(0, 2, 1, 3, 4)
    ckv_sample = jnp.stack(ckv_s_l, axis=1)
    kpe_sample = jnp.stack(kpe_s_l, axis=1)
    mem_k_prompt = jnp.stack(memk_l, axis=0)
    mem_v_prompt = jnp.stack(memv_l, axis=0)
    conv_prompt = jnp.stack(conv_p_l, axis=0)
    conv_sample = jnp.stack(conv_s_l, axis=0)
    shift_prompt = jnp.stack(shift_p_l, axis=0)
    shift_sample = jnp.stack(shift_s_l, axis=0)
    wkv_prompt = jnp.stack(wkv_p_l, axis=0)
    wkv_sample = jnp.stack(wkv_s_l, axis=0)
    sgu_v_sample = jnp.stack(sgu_s_l, axis=0)
    return (y_p, y_s, ckv_prompt, kpe_prompt, ckv_sample, kpe_sample, mem_k_prompt, mem_v_prompt,
            conv_prompt, conv_sample, shift_prompt, shift_sample, wkv_prompt, wkv_sample, sgu_v_sample)
```

```python
import numpy as np
import concourse.bass as bass
import concourse.mybir as mybir
from concourse.bass_utils import run_bass_kernel_spmd
from contextlib import ExitStack

F32 = mybir.dt.float32
BF16 = mybir.dt.bfloat16
I32 = mybir.dt.int32
ALU = mybir.AluOpType
AF = mybir.ActivationFunctionType
AX = mybir.AxisListType

D = 1024
KC = 8
DEPTH = 2
EPS = 1e-6
LN_EPS = 1e-5
RW_LN_EPS = 64e-5
MLA_SCALE = 96.0 ** -0.5
CONV_W = 31
D_FF = 2816
FC = 22


WEIGHT_KEYS = ["w_in", "mla_w_uq", "mla_w_uk", "mla_w_uv", "conv_pw", "sgu_w", "rw_w2", "rw_a2", "rw_g2", "w_out",
               "wq_x", "wk_x", "wv_x", "wo_x", "w_ffn_in", "w_ffn_out"]


class Buf:
    def __init__(self, t, name):
        self.t = t
        self.name = name
        self.w = None
        self.r = []

    def __getitem__(self, k):
        return self.t[k]


class Tok:
    __slots__ = ("sem", "val", "closed", "idx", "dma")

    def __init__(self, sem, val, idx=0, dma=False):
        self.sem = sem
        self.val = val
        self.closed = False
        self.idx = idx
        self.dma = dma


class Sched:
    ENG = ("pe", "dve", "act", "pool", "sp")

    def __init__(self, nc, es):
        self.nc = nc
        self.es = es
        self.q = {e: [] for e in self.ENG}
        self.sem = {e: es.enter_context(nc.semaphore("s_" + e)) for e in self.ENG}
        self.cnt = {e: 0 for e in self.ENG}
        self.waited = {e: {} for e in self.ENG}
        self.tracks = {}
        self.nops = 0
        self.n = 0
        self.limit = 1 << 60
        self.marks = []

    def mark(self, name):
        self.marks.append((name, self.n))

    def track(self, name):
        if name not in self.tracks:
            self.tracks[name] = [self.es.enter_context(self.nc.semaphore("t_" + name)), 0, None, 0]
        return self.tracks[name]

    def _wait(self, eng, tok):
        key = id(tok.sem)
        if tok.dma:
            if self.waited[eng].get(key, -1) >= tok.idx:
                return
            tok.closed = True
            self.waited[eng][key] = tok.idx
        else:
            if self.waited[eng].get(key, 0) >= tok.val:
                return
            self.waited[eng][key] = tok.val
        self.q[eng].append(("w", tok))

    def _deps(self, eng, R, W, skip=None):
        own = self.sem[eng] if eng == "pe" else None
        for b in R:
            if b.w is not None and b.w.sem is not own and b.w is not skip:
                self._wait(eng, b.w)
        for b in W:
            if b.w is not None and b.w.sem is not own and b.w is not skip:
                self._wait(eng, b.w)
            for t in b.r:
                if t.sem is not own:
                    self._wait(eng, t)

    def _post(self, tok, R, W):
        for b in R:
            b.r.append(tok)
        for b in W:
            b.w = tok
            b.r = []

    def op(self, eng, fn, R=(), W=()):
        self.n += 1
        if self.n > self.limit:
            return None
        self._deps(eng, R, W)
        self.cnt[eng] += 1
        tok = Tok(self.sem[eng], self.cnt[eng])
        self.q[eng].append(("o", fn, self.sem[eng], 1))
        self._post(tok, R, W)
        self.nops += 1
        return tok

    def dma(self, eng, out, in_, R=(), W=(), track=None, **kw):
        self.n += 1
        if self.n > self.limit:
            return None
        tr = self.track(track or ("q_" + eng))
        if tr[2] is None or tr[2].closed:
            if tr[2] is not None:
                self._wait(eng, tr[2])
            tr[2] = Tok(tr[0], tr[1], idx=tr[3], dma=True)
            tr[3] += 1
        self._deps(eng, R, W, skip=tr[2])
        if tr[2].closed:
            self._wait(eng, tr[2])
            tr[2] = Tok(tr[0], tr[1], idx=tr[3], dma=True)
            tr[3] += 1
        tr[1] += 16
        tr[2].val = tr[1]
        tok = tr[2]
        self.q[eng].append(("o", (lambda e, o=out, i=in_, k=kw: e.dma_start(out=o, in_=i, **k)), tr[0], 16))
        self._post(tok, R, W)
        return tok

    def dmaf(self, eng, fn, R=(), W=(), track=None):
        self.n += 1
        if self.n > self.limit:
            return None
        tr = self.track(track or ("q_" + eng))
        if tr[2] is None or tr[2].closed:
            if tr[2] is not None:
                self._wait(eng, tr[2])
            tr[2] = Tok(tr[0], tr[1], idx=tr[3], dma=True)
            tr[3] += 1
        self._deps(eng, R, W, skip=tr[2])
        if tr[2].closed:
            self._wait(eng, tr[2])
            tr[2] = Tok(tr[0], tr[1], idx=tr[3], dma=True)
            tr[3] += 1
        tr[1] += 16
        tr[2].val = tr[1]
        tok = tr[2]
        self.q[eng].append(("o", fn, tr[0], 16))
        self._post(tok, R, W)
        return tok

    def barrier(self):
        toks = [Tok(self.sem[e], self.cnt[e]) for e in self.ENG if self.cnt[e] > 0]
        toks += [t[2] for t in self.tracks.values() if t[2] is not None]
        for e in self.ENG:
            for t in toks:
                self._wait(e, t)

    def emit(self):
        nc = self.nc
        q = self.q
        self.q = {e: [] for e in self.ENG}

        def run(e, items):
            for it in items:
                if it[0] == "w":
                    e.wait_ge(it[1].sem, it[1].val)
                else:
                    it[1](e).then_inc(it[2], it[3])

        with nc.Block() as block:
            @block.tensor
            def _(e):
                run(e, q["pe"])

            @block.vector
            def _(e):
                run(e, q["dve"])

            @block.scalar
            def _(e):
                run(e, q["act"])

            @block.gpsimd
            def _(e):
                run(e, q["pool"])

            @block.sync
            def _(e):
                run(e, q["sp"])


def bc(ap, shape):
    return ap.to_broadcast(list(shape))


VB_SPEC = [("norm_mix", 1024), ("mla_q_norm", 192), ("mla_kv_norm", 128), ("mla_gq_nope", 64),
           ("mla_gq_rope", 32), ("mla_gk_nope", 64), ("mla_gk_rope", 32), ("sgu_norm_g", 256),
           ("sgu_norm_b", 256), ("out_norm", 1024), ("norm_x", 1024), ("norm_ffn", 1024),
           ("mem_norm", 1024), ("xq_norm", 128), ("xk_norm", 128)]
VB_OFF = {}
_o = 0
for _n, _s in VB_SPEC:
    VB_OFF[_n] = (_o, _s)
    _o += _s
NB = _o

VC_SPEC = [("conv_w", 62), ("conv_b", 2), ("conv_norm_g", 2), ("conv_norm_b", 2), ("rw_mu", 8),
           ("rw_w0", 2), ("rw_a0", 2), ("rw_kk", 2), ("rw_ka", 2), ("rw_rk", 2), ("rw_ln_g", 2),
           ("rw_ln_b", 2), ("sgu_b", 4), ("gk_col", 1)]
VC_OFF = {}
_o = 0
for _n, _s in VC_SPEC:
    VC_OFF[_n] = (_o, _s)
    _o += _s
NCOL = _o


def pack_small(inp, L):
    vb = np.zeros((L, NB), np.float32)
    vc = np.zeros((L, 128, NCOL), np.float32)
    for l in range(L):
        for n, s in VB_SPEC:
            o = VB_OFF[n][0]
            vb[l, o:o + s] = np.asarray(inp[n][l]).reshape(-1)
        for n, s in VC_SPEC:
            o = VC_OFF[n][0]
            if n == "gk_col":
                vc[l, 0:64, o] = np.asarray(inp["mla_gk_nope"][l]).reshape(-1)
                continue
            a = np.asarray(inp[n][l])
            if n == "conv_w":
                vc[l, :, o:o + 62] = a.reshape(31, 2, 128).transpose(2, 1, 0).reshape(128, 62)
            elif n == "sgu_b":
                vc[l, :, o:o + 4] = a.T
            else:
                vc[l, :, o:o + s] = a.reshape(s, 128).T
    return vb, vc


def make_consts(NT, past=16384):
    c = {}
    c["ident"] = np.eye(128, dtype=np.float32)
    blk = np.kron(np.eye(2, dtype=np.float32), np.ones((64, 64), np.float32))
    c["blkone"] = blk
    c["blkavg"] = blk / 64.0
    p = np.arange(128)
    c["mask"] = (p[:, None] <= p[None, :]).astype(np.float32)
    inv = np.power(np.float32(10000.0), -np.arange(16, dtype=np.float32) / np.float32(16)).astype(np.float32)
    pos = (np.arange(NT * 128, dtype=np.float32)).reshape(NT, 128).T
    ang = (pos[:, :, None] * inv[None, None, :]).astype(np.float32)
    c["cosp"] = np.cos(ang).astype(np.float32).reshape(128, NT * 16)
    c["sinp"] = np.sin(ang).astype(np.float32).reshape(128, NT * 16)
    poss = (past + (p % 8)).astype(np.float32)
    angs = (poss[:, None] * inv[None, :]).astype(np.float32)
    c["coss"] = np.cos(angs).astype(np.float32)
    c["sins"] = np.sin(angs).astype(np.float32)
    c["pidx"] = p.astype(np.float32).reshape(128, 1)
    bq = p // 8
    tq = p % 8
    ms = np.zeros((128, 16, 4, 8), np.float32)
    for b in range(16):
        for t in range(8):
            ms[:, b, :, t] = ((bq == b) & (tq <= t)).astype(np.float32)[:, None]
    c["maskS"] = ms.reshape(128, 512)
    names = ["ident", "blkone", "blkavg", "mask", "cosp", "sinp", "coss", "sins", "pidx", "maskS"]
    offs = {}
    o = 0
    for n in names:
        offs[n] = (o, c[n].shape[1])
        o += c[n].shape[1]
    arr = np.concatenate([c[n] for n in names], axis=1).astype(np.float32)
    return arr, offs


def build(cfg):
    NSEQ = cfg["NSEQ"]
    NT = cfg["NT"]
    L = cfg.get("L", DEPTH)
    STAGE = cfg.get("STAGE", 99)
    T = NT * 128
    carr, COFF = make_consts(NT, cfg.get('NPG', 128) * 128)
    NCONST = carr.shape[1]

    nc = bass.Bass("TRN2", target_bir_lowering=False)
    es = ExitStack()
    S = Sched(nc, es)
    S.limit = cfg.get('MAXOPS', 1 << 60)

    def din(name, shape, dt=F32):
        return nc.dram_tensor(name, list(shape), dt, kind="ExternalInput").ap()

    def dout(name, shape, dt=F32):
        return nc.dram_tensor(name, list(shape), dt, kind="ExternalOutput").ap()

    def dscr(name, shape, dt=F32):
        return nc.dram_tensor(name, list(shape), dt, kind="Internal").ap()

    x_prompt = din("x_prompt", [NSEQ, T, D])
    consts_d = din("consts", [128, NCONST])
    vb_d = din("vb", [L, NB])
    vc_d = din("vc", [L, 128, NCOL])
    w_in = din("w_in", [L, D, 2400])
    w_uq = din("mla_w_uq", [L, 192, 384])
    w_uk = din("mla_w_uk", [L, 128, 256])
    w_uv = din("mla_w_uv", [L, 128, 256])
    conv_pw = din("conv_pw", [L, 256, 256])
    sgu_w = din("sgu_w", [L, 4, 128, 128])
    rw_w2 = din("rw_w2", [L, 64, 256])
    rw_a2 = din("rw_a2", [L, 64, 256])
    rw_g2 = din("rw_g2", [L, 128, 256])
    w_out = din("w_out", [L, D, D])
    mem_prompt = din("mem_prompt", [NSEQ, 256, D])
    wq_x = din("wq_x", [L, D, 512])
    wk_x = din("wk_x", [L, D, 512])
    wv_x = din("wv_x", [L, D, 512])
    wo_x = din("wo_x", [L, 512, D])
    w_ffn_in = din("w_ffn_in", [L, D, 2 * D_FF])
    w_ffn_out = din("w_ffn_out", [L, D_FF, D])
    mem_k_prompt = dout("mem_k_prompt", [L, NSEQ, 256, 512])
    mem_v_prompt = dout("mem_v_prompt", [L, NSEQ, 256, 512])

    y_prompt = dout("y_prompt", [NSEQ, T, D])
    ckv_prompt = dout("ckv_prompt", [NSEQ, NT, L, 128, 128])
    kpe_prompt = dout("kpe_prompt", [NSEQ, NT, L, 128, 32])
    conv_prompt = dout("conv_prompt", [L, NSEQ, 30, 256])
    shift_prompt = dout("shift_prompt", [L, NSEQ, 1024])
    wkv_prompt = dout("wkv_prompt", [L, NSEQ, 4, 64, 64])

    SAMPLE = cfg.get("SAMPLE", True)
    NPG = cfg.get("NPG", 128)
    NPH = cfg.get("NPH", 2560)
    AG = cfg.get("AG", 8)
    x_sample = din("x_sample", [128, D])
    cache_ckv_in = din("cache_ckv", [NPH * L * 128, 128])
    cache_kpe_in = din("cache_kpe", [NPH * L * 128, 32])
    cache_mem_k = din("cache_mem_k", [L, 16, 256, 512])
    cache_mem_v = din("cache_mem_v", [L, 16, 256, 512])
    state_conv = din("state_conv", [L, 16, 30, 256])
    state_shift = din("state_shift", [L, 16, 1024])
    state_wkv = din("state_wkv", [L, 16, 4, 64, 64])
    page_table = din("page_table", [16, NPG], I32)
    y_sample = dout("y_sample", [128, D])
    ckv_s_out = dout("ckv_s_out", [L, 128, 128])
    kpe_s_out = dout("kpe_s_out", [L, 128, 32])
    conv_old_out = dout("conv_old_out", [L, 16, 22, 256])
    conv_new_out = dout("conv_new_out", [L, 128, 256])
    shift_s_out = dout("shift_s_out", [L, 16, 1024])
    wkv_s_out = dout("wkv_s_out", [L, 16, 4, 64, 64])
    sgv_s_out = dout("sgv_s_out", [L, 128, 256])
    xres_s = dscr("xres_s", [128, D])
    if AG > 1:
        cck_loc = dscr("cck_loc", [NPH * L * 128, 128])
        ckp_loc = dscr("ckp_loc", [NPH * L * 128, 32])
        cache_ckv_f = dscr("cache_ckv_f", [AG * NPH * L * 128, 128])
        cache_kpe_f = dscr("cache_kpe_f", [AG * NPH * L * 128, 32])
    else:
        cache_ckv_f = cache_ckv_in
        cache_kpe_f = cache_kpe_in
    xres = dscr("xres", [NSEQ, T, D])
    v_scr = dscr("v_scr", [NSEQ, 128, 256])

    cur = [es]
    uniq = [0]

    alloc_log = []

    def sb(name, shape, dt=F32):
        uniq[0] += 1
        alloc_log.append((name, int(np.prod(shape[1:])) * (4 if dt in (F32, I32) else 2)))
        return Buf(cur[0].enter_context(nc.sbuf_tensor("sb%d_%s" % (uniq[0], name), list(shape), dt)), name)

    def ps(name, shape, dt=F32):
        return Buf(es.enter_context(nc.psum_tensor("ps_" + name, list(shape), dt)), name)

    cst = sb("cst", [128, NCONST])
    S.dma("sp", cst[:], consts_d, W=[cst], track="cst")

    def C(name):
        o, n = COFF[name]
        return cst[:, o:o + n]

    ident_bf = sb("ident_bf", [128, 128], BF16)
    mask_bf = sb("mask_bf", [128, 128], BF16)
    S.op("dve", lambda e: e.tensor_copy(out=ident_bf[:], in_=C("ident")), R=[cst], W=[ident_bf])
    S.op("dve", lambda e: e.tensor_copy(out=mask_bf[:], in_=C("mask")), R=[cst], W=[mask_bf])

    cfull = Buf(None, "cfull")
    cloc = Buf(None, "cloc")
    if SAMPLE and AG > 1:
        NCH = cfg.get("NCH", 10)
        rows_c = (NPH // NCH) * L * 128
        rg = [list(range(AG))]
        for ch in range(NCH):
            r0 = ch * rows_c
            for (src, loc, full, w) in ((cache_ckv_in, cck_loc, cache_ckv_f, 128), (cache_kpe_in, ckp_loc, cache_kpe_f, 32)):
                g = 2048 // w
                S.dma("sp", loc[r0:r0 + rows_c, :].rearrange("(a b) c -> a (b c)", b=g), src[r0:r0 + rows_c, :].rearrange("(a b) c -> a (b c)", b=g),
                      W=[cloc], track="agcp")
            for (loc, full) in ((cck_loc, cache_ckv_f), (ckp_loc, cache_kpe_f)):
                S.dmaf("pool", lambda e, loc=loc, full=full, r0=r0: e.collective_compute(
                    "AllGather", ALU.bypass, replica_groups=rg, ins=[loc[r0:r0 + rows_c, :]],
                    outs=[full[AG * r0:AG * (r0 + rows_c), :]]), R=[cloc], W=[cfull], track="ag")

    PA = ps("PA", [128, 512])
    PC = ps("PC", [128, 512])
    PF = [ps("PF%d" % i, [128, 512]) for i in range(3)]
    PT = ps("PT", [128, 1024], BF16)
    PS_ = ps("PSc", [128, 512])
    PY = ps("PY", [128, 512])

    def mixer_phase(l, s0, NSEQ, kind="p"):
        NTk = NT if kind == "p" else 1
        pes = ExitStack()
        cur[0] = pes
        NB1 = 3072
        vb = sb("vb", [128, NB1])
        vc = sb("vc", [128, NCOL])
        Win_tm = sb("Win_tm", [128, KC, 864], BF16)
        Win_fm = sb("Win_fm", [128, KC, 1536], BF16)
        Wuq = sb("Wuq", [128, 2, 384], BF16)
        Wuk = sb("Wuk", [128, 256], BF16)
        Wuv = sb("Wuv", [128, 256], BF16)
        Wpw = sb("Wpw", [128, 2, 256], BF16)
        Wsg_raw = sb("Wsg_raw", [128, 4, 128], BF16)
        WsgT = sb("WsgT", [128, 4, 128], BF16)
        W2w = sb("W2w", [128, 256])
        W2a = sb("W2a", [128, 256])
        G2 = sb("G2", [128, 256])
        Wout = sb("Wout", [128, KC, D], BF16) if kind == "p" else Win_fm
        omka = sb("omka", [128, 2])

        def VB(name):
            o, n = VB_OFF[name]
            return vb[:, o:o + n]

        def VC(name):
            o, n = VC_OFF[name]
            return vc[:, o:o + n]

        x_t = [sb("x_t%d" % s, [128, D]) for s in range(NSEQ)]
        junk = sb("junk", [128, D])
        ss = sb("ss", [128, 8])
        rstd = sb("rstd", [128, 8])
        h_bf = sb("h_bf", [128, D], BF16)
        hT = sb("hT", [128, KC, 128], BF16)
        pa = sb("pa", [128, 352])
        cqn = sb("cqn", [128, 192], BF16)
        cqT = sb("cqT", [128, 2, 128], BF16)
        qsb = sb("qsb", [128, 4, 96])
        qfull = sb("qfull", [128, 4, 96], BF16)
        qT = sb("qT", [96, 4, 128], BF16)
        tmpr = sb("tmpr", [128, 4, 32])
        tr1 = sb("tr1", [128, 4, 16])
        tr2 = sb("tr2", [128, 4, 16])
        ckv_sb = sb("ckv_sb", [128, 128])
        ckv_bf = sb("ckv_bf", [128, 128], BF16)
        ckvT = sb("ckvT", [128, 128], BF16)
        kfull = sb("kfull", [128, 4, 96], BF16)
        kpe_sb = sb("kpe_sb", [128, 32])
        kT = [sb("kT%d" % s, [96, 4, T if kind == "p" else 128], BF16) for s in range(NSEQ)]
        Vaug = [sb("Vaug%d" % s, [128, NT if kind == "p" else 1, 4, 65], BF16) for s in range(NSEQ)]
        PTs = sb("PTs", [128, 512], BF16)
        rl = sb("rl", [128, 4])
        omix = [sb("omix%d" % s, [128, D]) for s in range(NSEQ)]
        zc = sb("zc", [128, 512])
        t5 = sb("t5", [128, 512])
        vn = sb("vn", [128, 256])
        vn_bf = sb("vn_bf", [128, 256], BF16)
        xin = [sb("xin%d" % s, [128, 2, 158]) for s in range(NSEQ)]
        sgm = sb("sgm", [128, 2, 128])
        cacc = sb("cacc", [128, 2, 128])
        csq = sb("csq", [128, 2, 128])
        cmean = sb("cmean", [128, 2, 128])
        cvar = sb("cvar", [128, 2, 128])
        csl = sb("csl", [128, 2, 128], BF16)
        ctr = sb("ctr", [30, 256])
        PD = [sb("PD%d" % s, [128, 8, 129 if kind == "p" else 2]) for s in range(NSEQ)]
        xs = [sb("xs%d" % s, [128, 8, 128]) for s in range(NSEQ)]
        tw = sb("tw", [128, 128])
        lastc = sb("lastc", [128, 8])
        lastT = sb("lastT", [8, 128])
        dec = [sb("dec%d" % s, [128, 2, 128]) for s in range(NSEQ)]
        aa = sb("aa", [128, 2, 128])
        sgg = sb("sgg", [128, 128])
        gT = [sb("gT%d" % s, [128, 2, 128]) for s in range(NSEQ)]
        kk = sb("kk", [128, 2, 128])
        kk2 = sb("kk2", [128, 2, 128])
        nkk = [sb("nkk%d" % s, [128, 2, 128]) for s in range(NSEQ)]
        kka = [sb("kka%d" % s, [128, 2, 128]) for s in range(NSEQ)]
        kfin = [sb("kfin%d" % s, [128, 2, 128]) for s in range(NSEQ)]
        bon = [sb("bon%d" % s, [128, 2, 128]) for s in range(NSEQ)]
        tk = sb("tk", [128, 2, 128])
        vtm = sb("vtm", [128, 256])
        NG = 2 * NSEQ
        Sst = sb("Sst", [128, NG, 64])
        T1 = sb("T1", [128, NG, 64])
        CH = 8
        vbc = sb("vbc", [128, CH if kind == "p" else 1, NG, 64])
        KV = sb("KV", [128, CH if kind == "p" else 1, NG, 64])
        ysb = [sb("ysb%d" % s, [128, 2, 128]) for s in range(NSEQ)]
        ysq = sb("ysq", [128, 2, 128])
        odT = sb("odT", [128, 2, 128])
        on_bf = sb("on_bf", [128, D], BF16)
        onT = sb("onT", [128, KC, 128], BF16)
        wkv_nat = sb("wkv_nat", [128, NG, 64])
        if kind == "s":
            WsgTs = sb("WsgTs", [128, 4, 128], BF16)
            sgub_s = sb("sgub_s", [128, 4])
            WukTg = sb("WukTg", [64, 4, 128], BF16)
            qabsT = sb("qabsT", [128, 4, 128], BF16)
            qpeT = sb("qpeT", [32, 4, 128], BF16)
            OT = sb("OT", [128, 4, 128], BF16)
            maskS_bf = sb("maskS_bf", [128, 512], BF16)
            pt_i = sb("pt_i", [128, 16 * NPG], I32)
            idx_i = pt_i
            GP = min(4, NPG)
            pg_c = [sb("pg_c%d" % i, [128, GP, 128]) for i in range(2)]
            pg_k = [sb("pg_k%d" % i, [128, GP, 32]) for i in range(2)]
            caug = sb("caug", [128, GP, 129], BF16)
            kpb = sb("kpb", [128, GP, 32], BF16)
            ckvTp = sb("ckvTp", [128, GP * 128], BF16)
            kpeTp = sb("kpeTp", [32, GP * 128], BF16)
            ssq = sb("ssq", [128, 16])
            rsk = sb("rsk", [128, 16])
            sc = sb("sc", [128, GP, 32])
            PTp = sb("PTp", [128, GP, 32], BF16)
            accs = sb("accs", [32, 129])
            ol_bf = sb("ol_bf", [32, 128], BF16)
            rls = sb("rls", [32, 1])
            xin_s = sb("xin_s", [128, 2, 16, 38])
            stc = sb("stc", [120, 256])
            glu_c = sb("glu_c", [128, 2, 128])
            ctr_s = sb("ctr_s", [128, 256])
            PDs = sb("PDs", [128, 8, 16, 9])
            sst = sb("sst", [16, 1024])
            lastc_s = sb("lastc_s", [128, 8, 16])
            lastT_s = sb("lastT_s", [16, 1024])
            Sst_s = sb("Sst_s", [128, 2, 16, 64])
            BQ = 2
            T1s = sb("T1s", [128, 2, BQ, 64])
            vbc_s = sb("vbc_s", [128, 2, BQ * 8, 64])

        def do_rsqrt(dstb, dst, srcb, src, scale, eps):
            S.op("dve", lambda e: e.tensor_scalar(out=dst, in0=src, scalar1=scale, scalar2=eps, op0=ALU.mult, op1=ALU.add), R=[srcb], W=[dstb])
            S.op("act", lambda e: e.sqrt(out=dst, in_=dst), R=[dstb], W=[dstb])
            S.op("dve", lambda e: e.reciprocal(out=dst, in_=dst), R=[dstb], W=[dstb])

        def rope(srcb, src4, dstb, dst4, cos, sin, nh):
            cb = bc(cos.unsqueeze(1), [128, nh, 16])
            sn = bc(sin.unsqueeze(1), [128, nh, 16])
            x1 = src4[:, :, 0:16]
            x2 = src4[:, :, 16:32]
            a1 = tr1[:, 0:nh, :]
            a2 = tr2[:, 0:nh, :]
            S.op("dve", lambda e: e.tensor_tensor(out=a1, in0=x1, in1=cb, op=ALU.mult), R=[srcb, cst], W=[tr1])
            S.op("dve", lambda e: e.tensor_tensor(out=a2, in0=x2, in1=sn, op=ALU.mult), R=[srcb, cst], W=[tr2])
            S.op("dve", lambda e: e.tensor_tensor(out=dst4[:, :, 0:16], in0=a1, in1=a2, op=ALU.subtract), R=[tr1, tr2], W=[dstb])
            S.op("dve", lambda e: e.tensor_tensor(out=a1, in0=x1, in1=sn, op=ALU.mult), R=[srcb, cst, dstb], W=[tr1])
            S.op("dve", lambda e: e.tensor_tensor(out=a2, in0=x2, in1=cb, op=ALU.mult), R=[srcb, cst, dstb], W=[tr2])
            S.op("dve", lambda e: e.tensor_tensor(out=dst4[:, :, 16:32], in0=a1, in1=a2, op=ALU.add), R=[tr1, tr2], W=[dstb])

        def norm_T(xb, gname):
            S.op("act", lambda e: e.activation(out=junk[:], in_=xb[:], func=AF.Square), R=[xb], W=[junk])
            S.op("dve", lambda e: e.reduce_sum(out=ss[:, 0:1], in_=junk[:], axis=AX.X), R=[junk], W=[ss])
            do_rsqrt(rstd, rstd[:, 0:1], ss, ss[:, 0:1], 1.0 / D, EPS)
            S.op("dve", lambda e: e.scalar_tensor_tensor(out=h_bf[:], in0=xb[:], scalar=rstd[:, 0:1], in1=VB(gname),
                                                         op0=ALU.mult, op1=ALU.mult), R=[xb, rstd, vb], W=[h_bf])
            for kc in range(KC):
                S.op("pe", lambda e, kc=kc: e.transpose(out=PT[:, kc * 128:(kc + 1) * 128], in_=h_bf[:, kc * 128:(kc + 1) * 128],
                                                        identity=ident_bf[:]), R=[h_bf, ident_bf], W=[PT])
            S.op("act", lambda e: e.copy(out=hT[:].rearrange("p k t -> p (k t)"), in_=PT[:]), R=[PT], W=[hT])

        S.mark('---------------- parameter loads')
        S.dma("sp", vb[:], bc(vb_d[l:l + 1, 0:NB1], [128, NB1]), W=[vb], track="par")
        S.dma("sp", vc[:], vc_d[l], W=[vc], track="par")
        wl = w_in[l].rearrange("(kc p) n -> p kc n", p=128)
        S.dma("pool", Win_tm[:, :, 0:352], wl[:, :, 0:352], W=[Win_tm], track="w1")
        S.dma("pool", Win_tm[:, :, 352:864], wl[:, :, 864:1376], W=[Win_tm], track="w1")
        S.dma("pool", Win_fm[:, :, 0:512], wl[:, :, 352:864], W=[Win_fm], track="w1")
        S.dma("pool", Win_fm[:, :, 512:1536], wl[:, :, 1376:2400], W=[Win_fm], track="w1")
        S.dma("pool", Wuq[:, 0, :], w_uq[l, 0:128, :], W=[Wuq], track="w1")
        S.dma("pool", Wuq[0:64, 1, :], w_uq[l, 128:192, :], W=[Wuq], track="w1")
        S.dma("pool", Wuk[:], w_uk[l], W=[Wuk], track="w1")
        S.dma("pool", Wuv[:], w_uv[l], W=[Wuv], track="w1")
        S.dma("pool", Wpw[:], conv_pw[l].rearrange("(h p) n -> p h n", p=128), W=[Wpw], track="w1")
        S.dma("pool", Wsg_raw[:], sgu_w[l].rearrange("h i j -> i h j"), W=[Wsg_raw], track="w1")
        S.op("pool", lambda e: e.memset(W2w[:], 0.0), W=[W2w])
        S.op("pool", lambda e: e.memset(W2a[:], 0.0), W=[W2a])
        S.dma("sp", W2w[0:64, :], rw_w2[l], W=[W2w], track="par2")
        S.dma("sp", W2a[64:128, :], rw_a2[l], W=[W2a], track="par2")
        S.dma("sp", G2[:], rw_g2[l], W=[G2], track="par")
        if kind == "p":
            S.dma("pool", Wout[:], w_out[l].rearrange("(kc p) n -> p kc n", p=128), W=[Wout], track="w1")
        o1, n1 = VB_OFF["mla_gq_nope"]
        S.op("act", lambda e: e.mul(out=vb[:, o1:o1 + 96], in_=vb[:, o1:o1 + 96], mul=MLA_SCALE), R=[vb], W=[vb])
        S.op("dve", lambda e: e.tensor_scalar(out=omka[:], in0=VC("rw_ka"), scalar1=-1.0, scalar2=1.0, op0=ALU.mult, op1=ALU.add),
             R=[vc], W=[omka])
        for h in range(4):
            S.op("pe", lambda e, h=h: e.transpose(out=PT[:, h * 128:(h + 1) * 128], in_=Wsg_raw[:, h, :], identity=ident_bf[:]),
                 R=[Wsg_raw, ident_bf], W=[PT])
        S.op("dve", lambda e: e.tensor_tensor(out=WsgT[:], in0=PT[:, 0:512].rearrange("p (h i) -> p h i", h=4),
                                              in1=bc(mask_bf[:].unsqueeze(1), [128, 4, 128]), op=ALU.mult),
             R=[PT, mask_bf], W=[WsgT])
        S.op("pool", lambda e: e.memset(Sst[:], 0.0), W=[Sst])
        for s in range(NSEQ):
            S.op("pool", lambda e, s=s: e.memset(xin[s][:], 0.0), W=[xin[s]])
            S.op("pool", lambda e, s=s: e.memset(PD[s][:], 0.0), W=[PD[s]])
            S.op("pool", lambda e, s=s: e.memset(Vaug[s][:], 1.0), W=[Vaug[s]])


        def sample_prep():
            S.mark("sample_prep")
            S.op("pool", lambda e: e.memset(WsgTs[:], 0.0), W=[WsgTs])
            for b in range(16):
                S.dma("sp", WsgTs[b * 8:(b + 1) * 8, :, b * 8:(b + 1) * 8], WsgT[0:8, :, 0:8], R=[WsgT], W=[WsgTs], track="wsg")
                S.dma("sp", sgub_s[b * 8:(b + 1) * 8, :], VC("sgu_b")[0:8, :], R=[vc], W=[sgub_s], track="wsg2")
            for h in range(4):
                S.op("pe", lambda e, h=h: e.transpose(out=PT[0:64, h * 128:(h + 1) * 128], in_=Wuk[:, h * 64:(h + 1) * 64], identity=ident_bf[:]),
                     R=[Wuk, ident_bf], W=[PT])
            S.op("dve", lambda e: e.tensor_scalar(out=WukTg[:].rearrange("p h l -> p (h l)"), in0=PT[0:64, 0:512], scalar1=VC("gk_col")[0:64, 0:1],
                                                  scalar2=None, op0=ALU.mult), R=[PT, vc], W=[WukTg])
            S.op("dve", lambda e: e.tensor_copy(out=maskS_bf[:], in_=C("maskS")), R=[cst], W=[maskS_bf])
            S.dma("sp", pt_i[:], bc(page_table.rearrange("b j -> (b j)").unsqueeze(0), [128, 16 * NPG]), W=[pt_i], track="par")
            ptf = pt_i[:].bitcast(F32)
            S.op("dve", lambda e: e.tensor_copy(out=ptf, in_=pt_i[:]), R=[pt_i], W=[pt_i])
            S.op("dve", lambda e: e.tensor_scalar(out=ptf, in0=ptf, scalar1=float(L * 128), scalar2=C("pidx")[:, 0:1], op0=ALU.mult, op1=ALU.add),
                 R=[pt_i, cst], W=[pt_i])
            S.op("dve", lambda e: e.tensor_scalar(out=ptf, in0=ptf, scalar1=float(l * 128), scalar2=None, op0=ALU.add), R=[pt_i], W=[pt_i])
            S.op("dve", lambda e: e.tensor_copy(out=pt_i[:], in_=ptf), R=[pt_i], W=[pt_i])
            for g4 in range(4):
                S.dma("sp", stc[:], state_conv[l, 4 * g4:4 * g4 + 4].rearrange("b w c -> (b w) c"), W=[stc], track="stc")
                for hf in range(2):
                    S.op("pe", lambda e, hf=hf: e.transpose(out=PS_[:, hf * 128:hf * 128 + 120], in_=stc[:, hf * 128:(hf + 1) * 128],
                                                            identity=C("ident")[0:120, 0:120]), R=[stc, cst], W=[PS_])
                for hf in range(2):
                    S.op("act", lambda e, hf=hf, g4=g4: e.copy(out=xin_s[:, hf, 4 * g4:4 * g4 + 4, 0:30],
                                                              in_=PS_[:, hf * 128:hf * 128 + 120].rearrange("p (b w) -> p b w", w=30)), R=[PS_], W=[xin_s])
            S.dma("sp", sst[:], state_shift[l], W=[sst], track="stc")
            for c in range(8):
                S.op("pe", lambda e, c=c: e.transpose(out=PS_[:, c * 16:(c + 1) * 16], in_=sst[:, c * 128:(c + 1) * 128], identity=C("ident")[0:16, 0:16]),
                     R=[sst, cst], W=[PS_])
            S.op("act", lambda e: e.copy(out=PDs[:, :, :, 0], in_=PS_[:, 0:128].rearrange("p (c b) -> p c b", c=8)), R=[PS_], W=[PDs])
            for p2 in range(2):
                for cc in range(2):
                    S.dma("sp", vbc_s[p2 * 64:(p2 + 1) * 64, cc, :, :], state_wkv[l, :, cc * 2 + p2].rearrange("b i j -> i b j"), W=[vbc_s], track="stw")
            for cc in range(2):
                for bh in range(2):
                    for bb in range(8):
                        for p2 in range(2):
                            pp = slice(p2 * 64, (p2 + 1) * 64)
                            S.op("pe", lambda e, cc=cc, bh=bh, bb=bb, pp=pp: e.matmul(PS_[pp, bb * 64:(bb + 1) * 64], lhsT=vbc_s[pp, cc, bh * 8 + bb, :],
                                                                                      rhs=C("ident")[pp, pp], start=True, stop=True), R=[vbc_s, cst], W=[PS_])
                    S.op("act", lambda e, cc=cc, bh=bh: e.copy(out=Sst_s[:, cc, bh * 8:(bh + 1) * 8, :], in_=PS_[:].rearrange("p (b i) -> p b i", i=64)),
                         R=[PS_], W=[Sst_s])

        def sample_attention():
            S.mark("sample_attention")
            ckv_rows = cache_ckv_f
            kpe_rows = cache_kpe_f
            for h in range(4):
                S.op("pe", lambda e, h=h: e.matmul(PF[0][:, h * 128:(h + 1) * 128], lhsT=WukTg[0:64, h, :], rhs=qT[0:64, h, :], start=True, stop=True),
                     R=[WukTg, qT], W=[PF[0]])
            S.op("act", lambda e: e.copy(out=qabsT[:].rearrange("p h t -> p (h t)"), in_=PF[0][:]), R=[PF[0]], W=[qabsT])
            for h in range(4):
                S.op("pe", lambda e, h=h: e.transpose(out=PT[0:32, h * 128:(h + 1) * 128], in_=qfull[:, h, 64:96], identity=ident_bf[:]),
                     R=[qfull, ident_bf], W=[PT])
            S.op("act", lambda e: e.copy(out=qpeT[:].rearrange("p h t -> p (h t)"), in_=PT[0:32, 0:512]), R=[PT], W=[qpeT])
            S.op("pool", lambda e: e.memset(caug[:], 1.0), W=[caug])

            def group(b, npg, cb, cap, kb, kap, masked, first, last):
                S.op("act", lambda e: e.copy(out=caug[:, 0:npg, 0:128], in_=cap), R=[cb], W=[caug])
                S.op("dve", lambda e: e.tensor_copy(out=kpb[:, 0:npg, :], in_=kap), R=[kb], W=[kpb])
                for pg in range(npg):
                    S.op("pe", lambda e, pg=pg: e.transpose(out=PT[:, pg * 128:(pg + 1) * 128], in_=caug[:, pg, 0:128], identity=ident_bf[:]),
                         R=[caug, ident_bf], W=[PT])
                    S.op("pe", lambda e, pg=pg: e.transpose(out=PT[0:32, 512 + pg * 128:512 + (pg + 1) * 128], in_=kpb[:, pg, :], identity=ident_bf[:]),
                         R=[kpb, ident_bf], W=[PT])
                S.op("act", lambda e: e.copy(out=ckvTp[:, 0:npg * 128], in_=PT[:, 0:npg * 128]), R=[PT], W=[ckvTp])
                S.op("act", lambda e: e.copy(out=kpeTp[:, 0:npg * 128], in_=PT[0:32, 512:512 + npg * 128]), R=[PT], W=[kpeTp])
                for pg in range(npg):
                    S.op("pe", lambda e, pg=pg: e.matmul(PF[pg // 2][:, (pg % 2) * 256:(pg % 2 + 1) * 256], lhsT=ckvTp[:, pg * 128:(pg + 1) * 128], rhs=Wuk[:],
                                                         start=True, stop=True), R=[ckvTp, Wuk], W=[PF[pg // 2]])
                for hb in range((npg + 1) // 2):
                    w = min(2, npg - 2 * hb) * 256
                    S.op("act", lambda e, hb=hb, w=w: e.activation(out=junk[:, hb * 512:hb * 512 + w], in_=PF[hb][:, 0:w], func=AF.Square), R=[PF[hb]], W=[junk])
                S.op("dve", lambda e: e.reduce_sum(out=ssq[:, 0:npg * 4], in_=junk[:, 0:npg * 256].rearrange("p (g d) -> p g d", d=64), axis=AX.X),
                     R=[junk], W=[ssq])
                do_rsqrt(rsk, rsk[:, 0:npg * 4], ssq, ssq[:, 0:npg * 4], 1.0 / 64, EPS)
                for pg in range(npg):
                    S.op("pe", lambda e, pg=pg: e.matmul(PS_[:, pg * 32:(pg + 1) * 32].rearrange("p (h q) -> p h q", h=4), lhsT=ckvTp[:, pg * 128:(pg + 1) * 128],
                                                         rhs=qabsT[:, :, b * 8:(b + 1) * 8], start=True, stop=True), R=[ckvTp, qabsT], W=[PS_])
                    S.op("pe", lambda e, pg=pg: e.matmul(PS_[:, 128 + pg * 32:128 + (pg + 1) * 32].rearrange("p (h q) -> p h q", h=4),
                                                         lhsT=kpeTp[0:32, pg * 128:(pg + 1) * 128], rhs=qpeT[0:32, :, b * 8:(b + 1) * 8], start=True, stop=True),
                         R=[kpeTp, qpeT], W=[PS_])
                scv = sc[:, 0:npg, :].rearrange("p g (h q) -> p (g h) q", h=4)
                S.op("dve", lambda e: e.tensor_tensor(out=scv, in0=PS_[:, 0:npg * 32].rearrange("p (g q) -> p g q", q=8),
                                                      in1=bc(rsk[:, 0:npg * 4].unsqueeze(2), [128, npg * 4, 8]), op=ALU.mult), R=[PS_, rsk], W=[sc])
                S.op("dve", lambda e: e.tensor_tensor(out=sc[:, 0:npg, :], in0=sc[:, 0:npg, :],
                                                      in1=PS_[:, 128:128 + npg * 32].rearrange("p (g q) -> p g q", q=32), op=ALU.add), R=[PS_, sc], W=[sc])
                S.op("act", lambda e: e.activation(out=PTp[:, 0:npg, :], in_=sc[:, 0:npg, :], func=AF.Exp), R=[sc], W=[PTp])
                if masked:
                    S.op("dve", lambda e: e.tensor_tensor(out=PTp[:, 0, :], in0=PTp[:, 0, :], in1=maskS_bf[:, b * 32:(b + 1) * 32], op=ALU.mult),
                         R=[PTp, maskS_bf], W=[PTp])
                for pg in range(npg):
                    S.op("pe", lambda e, pg=pg: e.matmul(PY[0:32, 0:129], lhsT=PTp[:, pg, :], rhs=caug[:, pg, :], start=(first and pg == 0),
                                                         stop=(last and pg == npg - 1)), R=[PTp, caug], W=[PY])

            gi = 0
            ngr = NPG // GP
            for b in range(16):
                group(b, 1, ckv_sb, ckv_sb[:].unsqueeze(1), kpe_sb, kpe_sb[:].unsqueeze(1), True, True, ngr == 0)
                for gq in range(ngr):
                    bufi = gi % 2
                    gi += 1
                    for pg in range(GP):
                        col = b * NPG + gq * GP + pg
                        S.dmaf("pool", lambda e, bufi=bufi, pg=pg, col=col: e.indirect_dma_start(
                            out=pg_c[bufi][:, pg, :], out_offset=None, in_=ckv_rows,
                            in_offset=bass.IndirectOffsetOnAxis(ap=idx_i[:, col:col + 1], axis=0)), R=[idx_i, cfull], W=[pg_c[bufi]], track="pgc%d" % bufi)
                        S.dmaf("pool", lambda e, bufi=bufi, pg=pg, col=col: e.indirect_dma_start(
                            out=pg_k[bufi][:, pg, :], out_offset=None, in_=kpe_rows,
                            in_offset=bass.IndirectOffsetOnAxis(ap=idx_i[:, col:col + 1], axis=0)), R=[idx_i, cfull], W=[pg_k[bufi]], track="pgk%d" % bufi)
                    group(b, GP, pg_c[bufi], pg_c[bufi][:], pg_k[bufi], pg_k[bufi][:], False, False, gq == ngr - 1)
                S.op("act", lambda e: e.copy(out=accs[:], in_=PY[0:32, 0:129]), R=[PY], W=[accs])
                S.op("dve", lambda e: e.reciprocal(out=rls[:], in_=accs[:, 128:129]), R=[accs], W=[rls])
                S.op("dve", lambda e: e.tensor_scalar(out=ol_bf[:], in0=accs[:, 0:128], scalar1=rls[:, 0:1], scalar2=None, op0=ALU.mult), R=[accs, rls], W=[ol_bf])
                S.op("pe", lambda e: e.transpose(out=PT[:, 0:32], in_=ol_bf[:], identity=ident_bf[0:32, 0:32]), R=[ol_bf, ident_bf], W=[PT])
                S.op("act", lambda e, b=b: e.copy(out=OT[:, :, b * 8:(b + 1) * 8], in_=PT[:, 0:32].rearrange("p (h q) -> p h q", h=4)), R=[PT], W=[OT])
            for h in range(4):
                S.op("pe", lambda e, h=h: e.matmul(PA[:, h * 64:(h + 1) * 64], lhsT=OT[:, h, :], rhs=Wuv[:, h * 64:(h + 1) * 64], start=True, stop=True),
                     R=[OT, Wuv], W=[PA])
            S.op("act", lambda e: e.copy(out=omix[0][:, 0:256], in_=PA[:, 0:256]), R=[PA], W=[omix[0]])

        def sample_scan():
            S.mark("sample_scan")
            vtok = S.track("vscr0")[2]
            kka4 = kka[0][:].rearrange("p c (b t) -> p c b t", t=8)
            dec4 = dec[0][:].rearrange("p c (b t) -> p c b t", t=8)
            for bq in range(16 // BQ):
                b0 = bq * BQ
                for p2 in range(2):
                    for cc in range(2):
                        hh = cc * 2 + p2
                        src = bc(v_scr[0, b0 * 8:(b0 + BQ) * 8, hh * 64:(hh + 1) * 64].unsqueeze(0), [64, BQ * 8, 64])
                        if vtok is not None and S.n < S.limit:
                            S._wait("sp", vtok)
                        S.dma("sp", vbc_s[p2 * 64:(p2 + 1) * 64, cc, :, :], src, W=[vbc_s], track="vbc")
                S.op("pool", lambda e, b0=b0: e.tensor_tensor(out=vbc_s[:], in0=vbc_s[:], in1=bc(kfin[0][:, :, b0 * 8:(b0 + BQ) * 8].unsqueeze(3), [128, 2, BQ * 8, 64]),
                                                              op=ALU.mult), R=[vbc_s, kfin[0]], W=[vbc_s])
                KV5 = vbc_s[:].rearrange("p c (b t) i -> p c b t i", t=8)
                Sv = Sst_s[:, :, b0:b0 + BQ, :]
                for t in range(8):
                    for cc in range(2):
                        for bb in range(BQ):
                            tok = (b0 + bb) * 8 + t
                            for p2 in range(2):
                                pp = slice(p2 * 64, (p2 + 1) * 64)
                                S.op("pe", lambda e, cc=cc, bb=bb, tok=tok, pp=pp, b0=b0: e.matmul(
                                    PS_[pp, (cc * BQ + bb) * 64:(cc * BQ + bb + 1) * 64], lhsT=bc(nkk[0][pp, cc, tok:tok + 1], [64, 64]),
                                    rhs=Sst_s[pp, cc, b0 + bb, :], start=True, stop=True), R=[nkk[0], Sst_s], W=[PS_])
                    sa = PS_[:, 0:2 * BQ * 64].rearrange("p (c b i) -> p c b i", c=2, b=BQ)
                    S.op("dve", lambda e, t=t, b0=b0, sa=sa: e.tensor_tensor(out=T1s[:], in0=sa, in1=bc(kka4[:, :, b0:b0 + BQ, t].unsqueeze(3), [128, 2, BQ, 64]),
                                                                             op=ALU.mult), R=[PS_, kka[0]], W=[T1s])
                    S.op("dve", lambda e, t=t, b0=b0, Sv=Sv: e.tensor_tensor(out=Sv, in0=Sv, in1=bc(dec4[:, :, b0:b0 + BQ, t].unsqueeze(3), [128, 2, BQ, 64]),
                                                                             op=ALU.mult), R=[Sst_s, dec[0]], W=[Sst_s])
                    S.op("dve", lambda e, Sv=Sv: e.tensor_tensor(out=Sv, in0=Sv, in1=T1s[:], op=ALU.add), R=[Sst_s, T1s], W=[Sst_s])
                    S.op("dve", lambda e, Sv=Sv, t=t, KV5=KV5: e.tensor_tensor(out=Sv, in0=Sv, in1=KV5[:, :, :, t, :], op=ALU.add), R=[Sst_s, vbc_s], W=[Sst_s])
                    for cc in range(2):
                        for bb in range(BQ):
                            tok = (b0 + bb) * 8 + t
                            for p2 in range(2):
                                pp = slice(p2 * 64, (p2 + 1) * 64)
                                S.op("pe", lambda e, cc=cc, bb=bb, tok=tok, pp=pp, b0=b0: e.matmul(
                                    PY[pp, cc * 128 + tok:cc * 128 + tok + 1], lhsT=Sst_s[pp, cc, b0 + bb, :], rhs=xs[0][pp, cc, tok:tok + 1],
                                    start=True, stop=True), R=[Sst_s, xs[0]], W=[PY])
            for cc in range(2):
                for bh in range(2):
                    for bb in range(8):
                        for p2 in range(2):
                            pp = slice(p2 * 64, (p2 + 1) * 64)
                            S.op("pe", lambda e, cc=cc, bh=bh, bb=bb, pp=pp: e.matmul(PS_[pp, bb * 64:(bb + 1) * 64], lhsT=Sst_s[pp, cc, bh * 8 + bb, :],
                                                                                      rhs=C("ident")[pp, pp], start=True, stop=True), R=[Sst_s, cst], W=[PS_])
                    S.op("act", lambda e, cc=cc, bh=bh: e.copy(out=vbc_s[:, cc, bh * 8:(bh + 1) * 8, :], in_=PS_[:].rearrange("p (b i) -> p b i", i=64)),
                         R=[PS_], W=[vbc_s])
            for p2 in range(2):
                for cc in range(2):
                    S.dma("sp", wkv_s_out[l, :, cc * 2 + p2].rearrange("b i j -> i b j"), vbc_s[p2 * 64:(p2 + 1) * 64, cc, :, :], R=[vbc_s], track="o_misc")

        xsrc = x_prompt if l == 0 else xres
        xdst = xres
        if kind == "s":
            sample_prep()

        for n in range(NTk):
            for s in range(NSEQ):
                xb = x_t[s]
                if kind == "p":
                    S.dma("sp", xb[:], xsrc[s0 + s, n * 128:(n + 1) * 128, :], W=[xb], track="x%d" % s)
                else:
                    S.dma("sp", xb[:], (x_sample if l == 0 else xres_s), W=[xb], track="x%d" % s)
                norm_T(xb, "norm_mix")
                for (pb, c0, nn) in ((PA, 0, 352), (PC, 352, 512)):
                    for kc in range(KC):
                        S.op("pe", lambda e, pb=pb, c0=c0, nn=nn, kc=kc: e.matmul(
                            pb[:, 0:nn], lhsT=hT[:, kc, :], rhs=Win_tm[:, kc, c0:c0 + nn], start=(kc == 0), stop=(kc == KC - 1)),
                            R=[hT, Win_tm], W=[pb])
                for ch in range(12):
                    pb = PF[ch // 4]
                    for kc in range(KC):
                        S.op("pe", lambda e, pb=pb, ch=ch, kc=kc: e.matmul(
                            pb[:, (ch % 4) * 128:(ch % 4 + 1) * 128], lhsT=Win_fm[:, kc, ch * 128:(ch + 1) * 128], rhs=hT[:, kc, :],
                            start=(kc == 0), stop=(kc == KC - 1)), R=[hT, Win_fm], W=[pb])
                S.mark('================= A: MLA')
                if kind == "s":
                    S.dma("pool", Win_fm[:, :, 0:D], w_out[l].rearrange("(kc p) n -> p kc n", p=128), R=[], W=[Win_fm], track="w2")
                S.op("act", lambda e: e.copy(out=pa[:], in_=PA[:, 0:352]), R=[PA], W=[pa])
                S.op("act", lambda e: e.activation(out=junk[:, 0:352], in_=pa[:], func=AF.Square), R=[pa], W=[junk])
                S.op("dve", lambda e: e.reduce_sum(out=ss[:, 1:2], in_=junk[:, 0:192], axis=AX.X), R=[junk], W=[ss])
                S.op("dve", lambda e: e.reduce_sum(out=ss[:, 2:3], in_=junk[:, 192:320], axis=AX.X), R=[junk], W=[ss])
                S.op("dve", lambda e: e.reduce_sum(out=ss[:, 3:4], in_=junk[:, 320:352], axis=AX.X), R=[junk], W=[ss])
                do_rsqrt(rstd, rstd[:, 1:2], ss, ss[:, 1:2], 1.0 / 192, EPS)
                do_rsqrt(rstd, rstd[:, 2:3], ss, ss[:, 2:3], 1.0 / 128, EPS)
                do_rsqrt(rstd, rstd[:, 3:4], ss, ss[:, 3:4], 1.0 / 32, EPS)
                S.op("dve", lambda e: e.scalar_tensor_tensor(out=cqn[:], in0=pa[:, 0:192], scalar=rstd[:, 1:2], in1=VB("mla_q_norm"),
                                                             op0=ALU.mult, op1=ALU.mult), R=[pa, rstd, vb], W=[cqn])
                S.op("dve", lambda e: e.scalar_tensor_tensor(out=ckv_sb[:], in0=pa[:, 192:320], scalar=rstd[:, 2:3], in1=VB("mla_kv_norm"),
                                                             op0=ALU.mult, op1=ALU.mult), R=[pa, rstd, vb], W=[ckv_sb])
                S.dma("sp", (ckv_prompt[s0 + s, n, l] if kind == "p" else ckv_s_out[l]), ckv_sb[:], R=[ckv_sb], track="o_ckv")
                S.op("act", lambda e: e.copy(out=ckv_bf[:], in_=ckv_sb[:]), R=[ckv_sb], W=[ckv_bf])
                S.op("dve", lambda e: e.scalar_tensor_tensor(out=tmpr[:, 0, :], in0=pa[:, 320:352], scalar=rstd[:, 3:4], in1=VB("mla_gk_rope"),
                                                             op0=ALU.mult, op1=ALU.mult), R=[pa, rstd, vb], W=[tmpr])
                cosn = C("cosp")[:, n * 16:(n + 1) * 16] if kind == "p" else C("coss")
                sinn = C("sinp")[:, n * 16:(n + 1) * 16] if kind == "p" else C("sins")
                rope(tmpr, tmpr[:, 0:1, :], kpe_sb, kpe_sb[:].unsqueeze(1), cosn, sinn, 1)
                S.dma("sp", (kpe_prompt[s0 + s, n, l] if kind == "p" else kpe_s_out[l]), kpe_sb[:], R=[kpe_sb], track="o_kpe")
                S.op("dve", lambda e: e.tensor_copy(out=kfull[:, :, 64:96], in_=bc(kpe_sb[:].unsqueeze(1), [128, 4, 32])),
                     R=[kpe_sb], W=[kfull])
                S.op("pe", lambda e: e.transpose(out=PT[:, 0:128], in_=cqn[:, 0:128], identity=ident_bf[:]), R=[cqn, ident_bf], W=[PT])
                S.op("pe", lambda e: e.transpose(out=PT[0:64, 128:256], in_=cqn[:, 128:192], identity=ident_bf[:]), R=[cqn, ident_bf], W=[PT])
                S.op("pe", lambda e: e.transpose(out=PT[:, 256:384], in_=ckv_bf[:], identity=ident_bf[:]), R=[ckv_bf, ident_bf], W=[PT])
                S.op("act", lambda e: e.copy(out=cqT[:, 0, :], in_=PT[:, 0:128]), R=[PT], W=[cqT])
                S.op("act", lambda e: e.copy(out=cqT[0:64, 1, :], in_=PT[0:64, 128:256]), R=[PT], W=[cqT])
                S.op("act", lambda e: e.copy(out=ckvT[:], in_=PT[:, 256:384]), R=[PT], W=[ckvT])
                S.op("pe", lambda e: e.matmul(PA[:, 0:384], lhsT=cqT[:, 0, :], rhs=Wuq[:, 0, :], start=True, stop=False), R=[cqT, Wuq], W=[PA])
                S.op("pe", lambda e: e.matmul(PA[:, 0:384], lhsT=cqT[0:64, 1, :], rhs=Wuq[0:64, 1, :], start=False, stop=True), R=[cqT, Wuq], W=[PA])
                qv = qsb[:].rearrange("p h d -> p (h d)")
                S.op("act", lambda e: e.copy(out=qv, in_=PA[:, 0:384]), R=[PA], W=[qsb])
                S.op("act", lambda e: e.activation(out=junk[:, 0:384], in_=qv, func=AF.Square), R=[qsb], W=[junk])
                j4 = junk[:, 0:384].rearrange("p (h d) -> p h d", h=4)
                S.op("dve", lambda e: e.reduce_sum(out=ss[:, 4:8], in_=j4[:, :, 0:64], axis=AX.X), R=[junk], W=[ss])
                do_rsqrt(rstd, rstd[:, 4:8], ss, ss[:, 4:8], 1.0 / 64, EPS)
                S.op("dve", lambda e: e.tensor_tensor(out=qsb[:, :, 0:64], in0=qsb[:, :, 0:64], in1=bc(rstd[:, 4:8].unsqueeze(2), [128, 4, 64]),
                                                      op=ALU.mult), R=[qsb, rstd], W=[qsb])
                S.op("dve", lambda e: e.tensor_tensor(out=qfull[:, :, 0:64], in0=qsb[:, :, 0:64],
                                                      in1=bc(VB("mla_gq_nope").unsqueeze(1), [128, 4, 64]), op=ALU.mult),
                     R=[qsb, vb], W=[qfull])
                S.op("dve", lambda e: e.reduce_sum(out=ss[:, 4:8], in_=j4[:, :, 64:96], axis=AX.X), R=[junk], W=[ss])
                do_rsqrt(rstd, rstd[:, 4:8], ss, ss[:, 4:8], 1.0 / 32, EPS)
                S.op("dve", lambda e: e.tensor_tensor(out=tmpr[:], in0=qsb[:, :, 64:96], in1=bc(rstd[:, 4:8].unsqueeze(2), [128, 4, 32]),
                                                      op=ALU.mult), R=[qsb, rstd], W=[tmpr])
                S.op("dve", lambda e: e.tensor_tensor(out=tmpr[:], in0=tmpr[:], in1=bc(VB("mla_gq_rope").unsqueeze(1), [128, 4, 32]),
                                                      op=ALU.mult), R=[tmpr, vb], W=[tmpr])
                rope(tmpr, tmpr[:], qfull, qfull[:, :, 64:96], cosn, sinn, 4)
                S.op("pe", lambda e: e.matmul(PA[:, 0:256], lhsT=ckvT[:], rhs=Wuk[:], start=True, stop=True), R=[ckvT, Wuk, qsb], W=[PA])
                S.op("pe", lambda e: e.matmul(PA[:, 256:512], lhsT=ckvT[:], rhs=Wuv[:], start=True, stop=True), R=[ckvT, Wuv], W=[PA])
                S.op("act", lambda e: e.activation(out=junk[:, 0:256], in_=PA[:, 0:256], func=AF.Square), R=[PA], W=[junk])
                S.op("dve", lambda e: e.reduce_sum(out=ss[:, 4:8], in_=junk[:, 0:256].rearrange("p (h d) -> p h d", h=4), axis=AX.X),
                     R=[junk], W=[ss])
                do_rsqrt(rstd, rstd[:, 4:8], ss, ss[:, 4:8], 1.0 / 64, EPS)
                S.op("dve", lambda e: e.tensor_tensor(out=qsb[:, :, 0:64], in0=PA[:, 0:256].rearrange("p (h d) -> p h d", h=4),
                                                      in1=bc(rstd[:, 4:8].unsqueeze(2), [128, 4, 64]), op=ALU.mult),
                     R=[PA, rstd, qfull], W=[qsb])
                S.op("dve", lambda e: e.tensor_tensor(out=kfull[:, :, 0:64], in0=qsb[:, :, 0:64],
                                                      in1=bc(VB("mla_gk_nope").unsqueeze(1), [128, 4, 64]), op=ALU.mult),
                     R=[qsb, vb], W=[kfull])
                S.op("act", lambda e, s=s, n=n: e.copy(out=Vaug[s][:, n, :, 0:64], in_=PA[:, 256:512].rearrange("p (h d) -> p h d", h=4)),
                     R=[PA], W=[Vaug[s]])
                for h in range(4):
                    S.op("pe", lambda e, h=h: e.transpose(out=PT[0:96, h * 128:(h + 1) * 128], in_=qfull[:, h, :], identity=ident_bf[:]),
                         R=[qfull, ident_bf], W=[PT])
                    S.op("pe", lambda e, h=h: e.transpose(out=PT[0:96, 512 + h * 128:512 + (h + 1) * 128], in_=kfull[:, h, :], identity=ident_bf[:]),
                         R=[kfull, ident_bf], W=[PT])
                S.op("act", lambda e: e.copy(out=qT[:].rearrange("p h t -> p (h t)"), in_=PT[0:96, 0:512]), R=[PT], W=[qT])
                S.op("act", lambda e, s=s, n=n: e.copy(out=kT[s][:, :, n * 128:(n + 1) * 128],
                                                      in_=PT[0:96, 512:1024].rearrange("p (h t) -> p h t", h=4)), R=[PT], W=[kT[s]])
                for kt in (range(n + 1) if kind == "p" else []):
                    for h in range(4):
                        S.op("pe", lambda e, h=h, kt=kt, s=s: e.matmul(PS_[:, h * 128:(h + 1) * 128], lhsT=kT[s][:, h, kt * 128:(kt + 1) * 128],
                                                                      rhs=qT[:, h, :], start=True, stop=True), R=[kT[s], qT], W=[PS_])
                    S.op("act", lambda e: e.activation(out=PTs[:], in_=PS_[:], func=AF.Exp), R=[PS_], W=[PTs])
                    if kt == n:
                        S.op("dve", lambda e: e.tensor_tensor(out=PTs[:].rearrange("p (h t) -> p h t", h=4),
                                                              in0=PTs[:].rearrange("p (h t) -> p h t", h=4),
                                                              in1=bc(mask_bf[:].unsqueeze(1), [128, 4, 128]), op=ALU.mult),
                             R=[PTs, mask_bf], W=[PTs])
                    for h in range(4):
                        S.op("pe", lambda e, h=h, kt=kt, s=s, n=n: e.matmul(PY[:, h * 65:(h + 1) * 65], lhsT=PTs[:, h * 128:(h + 1) * 128],
                                                                           rhs=Vaug[s][:, kt, h, :], start=(kt == 0 and h == 0), stop=(kt == n and h == 3)),
                             R=[PTs, Vaug[s]], W=[PY])
                py4 = PY[:, 0:260].rearrange("p (h d) -> p h d", h=4)
                if kind == "p":
                    S.op("dve", lambda e: e.reciprocal(out=rl[:], in_=py4[:, :, 64]), R=[PY], W=[rl])
                    S.op("dve", lambda e, s=s: e.tensor_tensor(out=omix[s][:, 0:256].rearrange("p (h d) -> p h d", h=4), in0=py4[:, :, 0:64],
                                                               in1=bc(rl[:].unsqueeze(2), [128, 4, 64]), op=ALU.mult), R=[PY, rl], W=[omix[s]])

                S.mark('================= C: gMLP')
                S.op("act", lambda e: e.copy(out=zc[:], in_=PC[:]), R=[PC], W=[zc])
                S.op("dve", lambda e: e.tensor_tensor(out=t5[:], in0=zc[:], in1=zc[:], op=ALU.mult), R=[zc], W=[t5])
                S.op("dve", lambda e: e.tensor_scalar(out=t5[:], in0=t5[:], scalar1=0.044715, scalar2=1.0, op0=ALU.mult, op1=ALU.add), R=[t5], W=[t5])
                S.op("dve", lambda e: e.tensor_tensor(out=t5[:], in0=t5[:], in1=zc[:], op=ALU.mult), R=[t5, zc], W=[t5])
                S.op("act", lambda e: e.activation(out=t5[:], in_=t5[:], func=AF.Sigmoid, scale=1.5957691216057308), R=[t5], W=[t5])
                S.op("dve", lambda e: e.tensor_tensor(out=zc[:], in0=zc[:], in1=t5[:], op=ALU.mult), R=[t5, zc], W=[zc])
                S.op("dve", lambda e: e.reduce_sum(out=ss[:, 0:1], in_=zc[:, 256:512], axis=AX.X), R=[zc], W=[ss])
                S.op("dve", lambda e: e.tensor_scalar(out=ss[:, 0:1], in0=ss[:, 0:1], scalar1=-1.0 / 256, scalar2=None, op0=ALU.mult), R=[ss], W=[ss])
                S.op("dve", lambda e: e.tensor_scalar(out=vn[:], in0=zc[:, 256:512], scalar1=ss[:, 0:1], scalar2=None, op0=ALU.add), R=[zc, ss], W=[vn])
                S.op("act", lambda e: e.activation(out=junk[:, 0:256], in_=vn[:], func=AF.Square), R=[vn], W=[junk])
                S.op("dve", lambda e: e.reduce_sum(out=ss[:, 1:2], in_=junk[:, 0:256], axis=AX.X), R=[junk], W=[ss])
                do_rsqrt(rstd, rstd[:, 1:2], ss, ss[:, 1:2], 1.0 / 256, LN_EPS)
                S.op("dve", lambda e: e.scalar_tensor_tensor(out=vn[:], in0=vn[:], scalar=rstd[:, 1:2], in1=VB("sgu_norm_g"),
                                                             op0=ALU.mult, op1=ALU.mult), R=[vn, rstd, vb], W=[vn])
                S.op("dve", lambda e: e.tensor_tensor(out=vn[:], in0=vn[:], in1=VB("sgu_norm_b"), op=ALU.add), R=[vn, vb], W=[vn])
                S.op("act", lambda e: e.copy(out=vn_bf[:], in_=vn[:]), R=[vn], W=[vn_bf])
                if kind == "s":
                    S.dma("sp", sgv_s_out[l], vn[:], R=[vn], track="o_misc")
                for h in range(4):
                    S.op("pe", lambda e, h=h: e.matmul(PC[:, h * 64:(h + 1) * 64], lhsT=(WsgT if kind == "p" else WsgTs)[:, h, :], rhs=vn_bf[:, h * 64:(h + 1) * 64],
                                                       start=True, stop=True), R=[WsgT, vn_bf, zc] + ([WsgTs] if kind == "s" else []), W=[PC])
                for h in range(4):
                    S.op("dve", lambda e, h=h, s=s: e.scalar_tensor_tensor(out=omix[s][:, 512 + h * 64:512 + (h + 1) * 64], in0=PC[:, h * 64:(h + 1) * 64],
                                                                          scalar=(VC("sgu_b") if kind == "p" else sgub_s)[:, h:h + 1], in1=zc[:, h * 64:(h + 1) * 64],
                                                                          op0=ALU.add, op1=ALU.mult), R=[PC, vc, zc] + ([sgub_s] if kind == "s" else []), W=[omix[s]])
                S.mark('================= B: conformer')
                S.op("act", lambda e: e.activation(out=sgm[:].rearrange("p h t -> p (h t)"), in_=PF[0][:, 256:512], func=AF.Sigmoid), R=[PF[0]], W=[sgm])
                cw = VC("conv_w").rearrange("p (h w) -> p h w", h=2)
                if kind == "p":
                    S.op("dve", lambda e, s=s: e.tensor_tensor(out=xin[s][:, :, 30:158], in0=PF[0][:, 0:256].rearrange("p (h t) -> p h t", h=2),
                                                               in1=sgm[:], op=ALU.mult), R=[PF[0], sgm], W=[xin[s]])
                    for hf in range(2):
                        eng = "dve"
                        S.op(eng, lambda e, hf=hf, s=s: e.tensor_scalar(out=cacc[:, hf, :], in0=xin[s][:, hf, 0:128], scalar1=cw[:, hf, 0:1],
                                                                        scalar2=VC("conv_b")[:, hf:hf + 1], op0=ALU.mult, op1=ALU.add),
                             R=[xin[s], vc], W=[cacc])
                        for w in range(1, CONV_W):
                            S.op(eng, lambda e, hf=hf, s=s, w=w: e.scalar_tensor_tensor(out=cacc[:, hf, :], in0=xin[s][:, hf, w:w + 128],
                                                                                       scalar=cw[:, hf, w:w + 1], in1=cacc[:, hf, :],
                                                                                       op0=ALU.mult, op1=ALU.add), R=[xin[s], vc, cacc], W=[cacc])
                else:
                    S.op("dve", lambda e: e.tensor_tensor(out=glu_c[:], in0=PF[0][:, 0:256].rearrange("p (h t) -> p h t", h=2),
                                                          in1=sgm[:], op=ALU.mult), R=[PF[0], sgm], W=[glu_c])
                    S.op("dve", lambda e: e.tensor_copy(out=xin_s[:, :, :, 30:38], in_=glu_c[:].rearrange("p h (b t) -> p h b t", t=8)),
                         R=[glu_c], W=[xin_s])
                    for hf in range(2):
                        c3 = cacc[:, hf, :].rearrange("p (b t) -> p b t", t=8)
                        S.op("dve", lambda e, hf=hf, c3=c3: e.tensor_scalar(out=c3, in0=xin_s[:, hf, :, 0:8], scalar1=cw[:, hf, 0:1],
                                                                          scalar2=VC("conv_b")[:, hf:hf + 1], op0=ALU.mult, op1=ALU.add),
                             R=[xin_s, vc], W=[cacc])
                        for w in range(1, CONV_W):
                            S.op("dve", lambda e, hf=hf, c3=c3, w=w: e.scalar_tensor_tensor(out=c3, in0=xin_s[:, hf, :, w:w + 8],
                                                                                          scalar=cw[:, hf, w:w + 1], in1=c3,
                                                                                          op0=ALU.mult, op1=ALU.add), R=[xin_s, vc, cacc], W=[cacc])
                    for hf in range(2):
                        S.op("pe", lambda e, hf=hf: e.transpose(out=PS_[:, hf * 128:(hf + 1) * 128], in_=glu_c[:, hf, :], identity=C("ident")),
                             R=[glu_c, cst], W=[PS_])
                    S.op("act", lambda e: e.copy(out=ctr_s[:], in_=PS_[:, 0:256]), R=[PS_], W=[ctr_s])
                    S.dma("sp", conv_new_out[l], ctr_s[:], R=[ctr_s], track="o_misc")
                    S.dma("sp", conv_old_out[l], state_conv[l, :, 8:30, :], track="o_misc2")
                if n == NT - 1 and kind == "p":
                    for hf in range(2):
                        S.op("pe", lambda e, hf=hf, s=s: e.transpose(out=PS_[0:30, hf * 128:(hf + 1) * 128], in_=xin[s][:, hf, 128:158],
                                                                    identity=C("ident")), R=[xin[s], cst], W=[PS_])
                    S.op("act", lambda e: e.copy(out=ctr[:], in_=PS_[0:30, 0:256]), R=[PS_], W=[ctr])
                    S.dma("sp", conv_prompt[l, s0 + s], ctr[:], R=[ctr], track="o_misc")
                if kind == "p":
                    S.op("dve", lambda e, s=s: e.tensor_copy(out=xin[s][:, :, 0:30], in_=xin[s][:, :, 128:158]), R=[xin[s]], W=[xin[s]])
                S.op("act", lambda e: e.activation(out=csq[:], in_=cacc[:], func=AF.Square), R=[cacc], W=[csq])
                for hf in range(2):
                    S.op("pe", lambda e, hf=hf: e.matmul(PS_[:, hf * 128:(hf + 1) * 128], lhsT=C("blkavg"), rhs=cacc[:, hf, :], start=True, stop=True),
                         R=[cst, cacc], W=[PS_])
                    S.op("pe", lambda e, hf=hf: e.matmul(PS_[:, 256 + hf * 128:256 + (hf + 1) * 128], lhsT=C("blkavg"), rhs=csq[:, hf, :],
                                                         start=True, stop=True), R=[cst, csq], W=[PS_])
                cm = cmean[:].rearrange("p h t -> p (h t)")
                cv = cvar[:].rearrange("p h t -> p (h t)")
                ca = cacc[:].rearrange("p h t -> p (h t)")
                S.op("act", lambda e: e.copy(out=cm, in_=PS_[:, 0:256]), R=[PS_], W=[cmean])
                S.op("dve", lambda e: e.tensor_tensor(out=cv, in0=cm, in1=cm, op=ALU.mult), R=[cmean], W=[cvar])
                S.op("dve", lambda e: e.tensor_tensor(out=cv, in0=PS_[:, 256:512], in1=cv, op=ALU.subtract), R=[PS_, cvar], W=[cvar])
                do_rsqrt(cvar, cv, cvar, cv, 1.0, LN_EPS)
                S.op("dve", lambda e: e.tensor_tensor(out=ca, in0=ca, in1=cm, op=ALU.subtract), R=[cacc, cmean], W=[cacc])
                S.op("dve", lambda e: e.tensor_tensor(out=ca, in0=ca, in1=cv, op=ALU.mult), R=[cacc, cvar], W=[cacc])
                for hf in range(2):
                    S.op("dve", lambda e, hf=hf: e.tensor_scalar(out=cacc[:, hf, :], in0=cacc[:, hf, :], scalar1=VC("conv_norm_g")[:, hf:hf + 1],
                                                                 scalar2=VC("conv_norm_b")[:, hf:hf + 1], op0=ALU.mult, op1=ALU.add),
                         R=[cacc, vc], W=[cacc])
                S.op("act", lambda e: e.activation(out=csl[:], in_=cacc[:], func=AF.Silu), R=[cacc], W=[csl])
                for hf in range(2):
                    S.op("pe", lambda e, hf=hf: e.matmul(PA[:, 0:256], lhsT=csl[:, hf, :], rhs=Wpw[:, hf, :], start=(hf == 0), stop=(hf == 1)),
                         R=[csl, Wpw, Vaug[s], qsb], W=[PA])
                S.op("act", lambda e, s=s: e.copy(out=omix[s][:, 256:512], in_=PA[:, 0:256]), R=[PA], W=[omix[s]])
                S.mark('================= D: RWKV-7 pre-scan')
                if kind == "p":
                    S.op("act", lambda e, s=s: e.copy(out=PD[s][:, 0:4, 1:129], in_=PF[1][:].rearrange("p (c t) -> p c t", c=4)), R=[PF[1]], W=[PD[s]])
                    S.op("act", lambda e, s=s: e.copy(out=PD[s][:, 4:8, 1:129], in_=PF[2][:].rearrange("p (c t) -> p c t", c=4)), R=[PF[2]], W=[PD[s]])
                else:
                    S.op("act", lambda e: e.copy(out=PDs[:, 0:4, :, 1:9], in_=PF[1][:].rearrange("p (c b t) -> p c b t", c=4, t=8)), R=[PF[1]], W=[PDs])
                    S.op("act", lambda e: e.copy(out=PDs[:, 4:8, :, 1:9], in_=PF[2][:].rearrange("p (c b t) -> p c b t", c=4, t=8)), R=[PF[2]], W=[PDs])
                    S.op("dve", lambda e: e.tensor_copy(out=lastc_s[:], in_=PDs[:, :, :, 8]), R=[PDs], W=[lastc_s])
                    for hh in range(2):
                        for c in range(4):
                            S.op("pe", lambda e, hh=hh, c=c: e.transpose(out=PS_[0:16, c * 128:(c + 1) * 128], in_=lastc_s[:, hh * 4 + c, :],
                                                                        identity=C("ident")), R=[lastc_s, cst], W=[PS_])
                        S.op("act", lambda e, hh=hh: e.copy(out=lastT_s[:, hh * 512:(hh + 1) * 512], in_=PS_[0:16, :]), R=[PS_], W=[lastT_s])
                    S.dma("sp", shift_s_out[l], lastT_s[:], R=[lastT_s], track="o_misc")
                    xs4 = xs[s][:].rearrange("p c (b t) -> p c b t", t=8)
                    S.op("dve", lambda e, xs4=xs4: e.tensor_tensor(out=xs4, in0=PDs[:, :, :, 0:8], in1=PDs[:, :, :, 1:9], op=ALU.subtract),
                         R=[PDs], W=[xs[s]])
                    S.op("dve", lambda e, s=s: e.tensor_tensor(out=xs[s][:], in0=xs[s][:], in1=bc(VC("rw_mu").unsqueeze(2), [128, 8, 128]), op=ALU.mult),
                         R=[xs[s], vc], W=[xs[s]])
                    S.op("dve", lambda e, xs4=xs4: e.tensor_tensor(out=xs4, in0=xs4, in1=PDs[:, :, :, 1:9], op=ALU.add), R=[xs[s], PDs], W=[xs[s]])
                if n == NT - 1 and kind == "p":
                    S.op("dve", lambda e, s=s: e.tensor_copy(out=lastc[:], in_=PD[s][:, :, 128]), R=[PD[s]], W=[lastc])
                    S.op("pe", lambda e: e.transpose(out=PS_[0:8, 0:128], in_=lastc[:], identity=C("ident")), R=[lastc, cst], W=[PS_])
                    S.op("act", lambda e: e.copy(out=lastT[:], in_=PS_[0:8, 0:128]), R=[PS_], W=[lastT])
                    S.dma("sp", shift_prompt[l, s0 + s].rearrange("(c p) -> c p", p=128), lastT[:], R=[lastT], track="o_misc")
                if kind == "p":
                    S.op("dve", lambda e, s=s: e.tensor_tensor(out=xs[s][:], in0=PD[s][:, :, 0:128], in1=PD[s][:, :, 1:129], op=ALU.subtract),
                         R=[PD[s]], W=[xs[s]])
                    S.op("dve", lambda e, s=s: e.tensor_tensor(out=xs[s][:], in0=xs[s][:], in1=bc(VC("rw_mu").unsqueeze(2), [128, 8, 128]), op=ALU.mult),
                         R=[xs[s], vc], W=[xs[s]])
                    S.op("dve", lambda e, s=s: e.tensor_tensor(out=xs[s][:], in0=xs[s][:], in1=PD[s][:, :, 1:129], op=ALU.add), R=[xs[s], PD[s]], W=[xs[s]])
                    S.op("dve", lambda e, s=s: e.tensor_copy(out=PD[s][:, :, 0:1], in_=PD[s][:, :, 128:129]), R=[PD[s], xs[s]], W=[PD[s]])
                S.op("act", lambda e, s=s: e.activation(out=tw[:], in_=xs[s][:, 6, :], func=AF.Tanh), R=[xs[s]], W=[tw])
                for cc in range(2):
                    S.op("pe", lambda e, cc=cc: e.matmul(PS_[:, cc * 128:(cc + 1) * 128], lhsT=W2w[:, cc * 128:(cc + 1) * 128], rhs=tw[:],
                                                         start=True, stop=True), R=[W2w, tw, cvar, cmean], W=[PS_])
                    S.op("pe", lambda e, cc=cc, s=s: e.matmul(PS_[:, 256 + cc * 128:256 + (cc + 1) * 128], lhsT=W2a[:, cc * 128:(cc + 1) * 128],
                                                              rhs=xs[s][:, 6, :], start=True, stop=True), R=[W2a, xs[s]], W=[PS_])
                for cc in range(2):
                    S.op("act", lambda e, cc=cc, s=s: e.activation(out=dec[s][:, cc, :], in_=PS_[:, cc * 128:(cc + 1) * 128], func=AF.Sigmoid,
                                                                  bias=VC("rw_w0")[:, cc:cc + 1]), R=[PS_, vc], W=[dec[s]])
                    S.op("act", lambda e, cc=cc: e.activation(out=aa[:, cc, :], in_=PS_[:, 256 + cc * 128:256 + (cc + 1) * 128], func=AF.Sigmoid,
                                                              bias=VC("rw_a0")[:, cc:cc + 1]), R=[PS_, vc], W=[aa])
                S.op("act", lambda e, s=s: e.activation(out=dec[s][:], in_=dec[s][:], func=AF.Exp, scale=-float(np.exp(-0.5))), R=[dec[s]], W=[dec[s]])
                S.op("act", lambda e, s=s: e.activation(out=sgg[:], in_=xs[s][:, 7, :], func=AF.Sigmoid), R=[xs[s]], W=[sgg])
                for cc in range(2):
                    S.op("pe", lambda e, cc=cc: e.matmul(PS_[:, cc * 128:(cc + 1) * 128], lhsT=G2[:, cc * 128:(cc + 1) * 128], rhs=sgg[:],
                                                         start=True, stop=True), R=[G2, sgg, dec[s], aa], W=[PS_])
                S.op("act", lambda e, s=s: e.copy(out=gT[s][:].rearrange("p c t -> p (c t)"), in_=PS_[:, 0:256]), R=[PS_], W=[gT[s]])
                S.op("dve", lambda e, s=s: e.tensor_tensor(out=kk[:], in0=xs[s][:, 2:4, :], in1=bc(VC("rw_kk").unsqueeze(2), [128, 2, 128]), op=ALU.mult),
                     R=[xs[s], vc], W=[kk])
                S.op("dve", lambda e: e.tensor_tensor(out=kk2[:], in0=kk[:], in1=kk[:], op=ALU.mult), R=[kk], W=[kk2])
                for cc in range(2):
                    S.op("pe", lambda e, cc=cc: e.matmul(PS_[:, cc * 128:(cc + 1) * 128], lhsT=C("blkone"), rhs=kk2[:, cc, :], start=True, stop=True),
                         R=[cst, kk2, gT[s]], W=[PS_])
                kk2v = kk2[:].rearrange("p c t -> p (c t)")
                do_rsqrt(kk2, kk2v, PS_, PS_[:, 0:256], 1.0, 1e-12)
                S.op("dve", lambda e, s=s: e.scalar_tensor_tensor(out=nkk[s][:], in0=kk[:], scalar=-1.0, in1=kk2[:], op0=ALU.mult, op1=ALU.mult),
                     R=[kk, kk2], W=[nkk[s]])
                S.op("dve", lambda e, s=s: e.scalar_tensor_tensor(out=kka[s][:], in0=nkk[s][:], scalar=-1.0, in1=aa[:], op0=ALU.mult, op1=ALU.mult),
                     R=[nkk[s], aa], W=[kka[s]])
                S.op("dve", lambda e: e.tensor_tensor(out=tk[:], in0=aa[:], in1=bc(VC("rw_ka").unsqueeze(2), [128, 2, 128]), op=ALU.mult), R=[aa, vc], W=[tk])
                S.op("dve", lambda e: e.tensor_tensor(out=tk[:], in0=tk[:], in1=bc(omka[:].unsqueeze(2), [128, 2, 128]), op=ALU.add), R=[tk, omka], W=[tk])
                S.op("dve", lambda e, s=s: e.tensor_tensor(out=kfin[s][:], in0=xs[s][:, 2:4, :], in1=tk[:], op=ALU.mult), R=[xs[s], tk], W=[kfin[s]])
                S.op("dve", lambda e, s=s: e.tensor_tensor(out=tk[:], in0=xs[s][:, 0:2, :], in1=kfin[s][:], op=ALU.mult), R=[xs[s], kfin[s]], W=[tk])
                S.op("dve", lambda e: e.tensor_tensor(out=tk[:], in0=tk[:], in1=bc(VC("rw_rk").unsqueeze(2), [128, 2, 128]), op=ALU.mult), R=[tk, vc], W=[tk])
                for cc in range(2):
                    S.op("pe", lambda e, cc=cc: e.matmul(PS_[:, cc * 128:(cc + 1) * 128], lhsT=C("blkone"), rhs=tk[:, cc, :], start=True, stop=True),
                         R=[cst, tk, kk2], W=[PS_])
                S.op("dve", lambda e, s=s: e.tensor_tensor(out=bon[s][:].rearrange("p c t -> p (c t)"), in0=PS_[:, 0:256],
                                                           in1=xs[s][:, 4:6, :].rearrange("p c t -> p (c t)"), op=ALU.mult), R=[PS_, xs[s]], W=[bon[s]])
                for cc in range(2):
                    S.op("pe", lambda e, cc=cc, s=s: e.transpose(out=PS_[:, 256 + cc * 128:256 + (cc + 1) * 128], in_=xs[s][:, 4 + cc, :], identity=C("ident")),
                         R=[xs[s], cst, bon[s]], W=[PS_])
                S.op("act", lambda e: e.copy(out=vtm[:], in_=PS_[:, 256:512]), R=[PS_], W=[vtm])
                S.dma("sp", v_scr[s], vtm[:], R=[vtm], W=[], track="vscr%d" % s)
            S.mark('---------------- scan over the 128 ste')
            vscr_tok = [S.track("vscr%d" % s)[2] for s in range(NSEQ)]
            if kind == "s":
                sample_attention()
                sample_scan()
            for c0 in (range(0, 128, CH) if kind == "p" else []):
                for s in range(NSEQ):
                    for p2 in range(2):
                        for cc in range(2):
                            hh = cc * 2 + p2
                            src = bc(v_scr[s, c0:c0 + CH, hh * 64:(hh + 1) * 64].unsqueeze(0), [64, CH, 64])
                            if vscr_tok[s] is not None and S.n < S.limit:
                                S._wait("sp", vscr_tok[s])
                            S.dma("sp", vbc[p2 * 64:(p2 + 1) * 64, :, 2 * s + cc, :], src, W=[vbc], track="vbc")
                for s in range(NSEQ):
                    kf = kfin[s][:, :, c0:c0 + CH].rearrange("p c t -> p t c")
                    S.op("pool", lambda e, s=s, kf=kf: e.tensor_tensor(out=KV[:, :, 2 * s:2 * s + 2, :], in0=vbc[:, :, 2 * s:2 * s + 2, :],
                                                                        in1=bc(kf.unsqueeze(3), [128, CH, 2, 64]), op=ALU.mult),
                         R=[vbc, kfin[s]], W=[KV])
                for tt in range(CH):
                    t = c0 + tt
                    for s in range(NSEQ):
                        for cc in range(2):
                            g = 2 * s + cc
                            for p2 in range(2):
                                pp = slice(p2 * 64, (p2 + 1) * 64)
                                S.op("pe", lambda e, s=s, cc=cc, g=g, pp=pp, t=t: e.matmul(
                                    PS_[pp, g * 64:(g + 1) * 64], lhsT=bc(nkk[s][pp, cc, t:t + 1], [64, 64]), rhs=Sst[pp, g, :],
                                    start=True, stop=True), R=[nkk[s], Sst, vtm], W=[PS_])
                    sa = PS_[:, 0:NG * 64].rearrange("p (g i) -> p g i", g=NG)
                    for s in range(NSEQ):
                        S.op("dve", lambda e, s=s, t=t: e.tensor_tensor(out=T1[:, 2 * s:2 * s + 2, :], in0=sa[:, 2 * s:2 * s + 2, :],
                                                                        in1=bc(kka[s][:, :, t:t + 1], [128, 2, 64]), op=ALU.mult),
                             R=[PS_, kka[s]], W=[T1])
                        S.op("dve", lambda e, s=s, t=t: e.tensor_tensor(out=Sst[:, 2 * s:2 * s + 2, :], in0=Sst[:, 2 * s:2 * s + 2, :],
                                                                        in1=bc(dec[s][:, :, t:t + 1], [128, 2, 64]), op=ALU.mult),
                             R=[Sst, dec[s]], W=[Sst])
                    S.op("dve", lambda e: e.tensor_tensor(out=Sst[:], in0=Sst[:], in1=T1[:], op=ALU.add), R=[Sst, T1], W=[Sst])
                    S.op("dve", lambda e, tt=tt: e.tensor_tensor(out=Sst[:], in0=Sst[:], in1=KV[:, tt, :, :], op=ALU.add), R=[Sst, KV], W=[Sst])
                    for s in range(NSEQ):
                        for cc in range(2):
                            g = 2 * s + cc
                            for p2 in range(2):
                                pp = slice(p2 * 64, (p2 + 1) * 64)
                                S.op("pe", lambda e, s=s, cc=cc, g=g, pp=pp, t=t: e.matmul(
                                    PY[pp, g * 128 + t:g * 128 + t + 1], lhsT=Sst[pp, g, :], rhs=xs[s][pp, cc, t:t + 1],
                                    start=True, stop=True), R=[Sst, xs[s], rl], W=[PY])
            S.mark('---------------- post-scan per sequenc')
            for s in range(NSEQ):
                xb = x_t[s]
                yv = ysb[s][:].rearrange("p c t -> p (c t)")
                S.op("act", lambda e, s=s, yv=yv: e.copy(out=yv, in_=PY[:, 2 * s * 128:(2 * s + 2) * 128]), R=[PY], W=[ysb[s]])
                S.op("act", lambda e, s=s: e.activation(out=ysq[:], in_=ysb[s][:], func=AF.Square), R=[ysb[s]], W=[ysq])
                for cc in range(2):
                    S.op("pe", lambda e, cc=cc, s=s: e.matmul(PS_[:, cc * 128:(cc + 1) * 128], lhsT=C("blkavg"), rhs=ysb[s][:, cc, :], start=True, stop=True),
                         R=[cst, ysb[s], T1], W=[PS_])
                    S.op("pe", lambda e, cc=cc: e.matmul(PS_[:, 256 + cc * 128:256 + (cc + 1) * 128], lhsT=C("blkavg"), rhs=ysq[:, cc, :],
                                                         start=True, stop=True), R=[cst, ysq], W=[PS_])
                cm = cmean[:].rearrange("p h t -> p (h t)")
                cv = cvar[:].rearrange("p h t -> p (h t)")
                S.op("act", lambda e: e.copy(out=cm, in_=PS_[:, 0:256]), R=[PS_], W=[cmean])
                S.op("dve", lambda e: e.tensor_tensor(out=cv, in0=cm, in1=cm, op=ALU.mult), R=[cmean], W=[cvar])
                S.op("dve", lambda e: e.tensor_tensor(out=cv, in0=PS_[:, 256:512], in1=cv, op=ALU.subtract), R=[PS_, cvar], W=[cvar])
                do_rsqrt(cvar, cv, cvar, cv, 1.0, RW_LN_EPS)
                S.op("dve", lambda e, yv=yv: e.tensor_tensor(out=yv, in0=yv, in1=cm, op=ALU.subtract), R=[ysb[s], cmean], W=[ysb[s]])
                S.op("dve", lambda e, yv=yv: e.tensor_tensor(out=yv, in0=yv, in1=cv, op=ALU.mult), R=[ysb[s], cvar], W=[ysb[s]])
                for cc in range(2):
                    S.op("dve", lambda e, cc=cc, s=s: e.tensor_scalar(out=ysb[s][:, cc, :], in0=ysb[s][:, cc, :], scalar1=VC("rw_ln_g")[:, cc:cc + 1],
                                                                      scalar2=VC("rw_ln_b")[:, cc:cc + 1], op0=ALU.mult, op1=ALU.add),
                         R=[ysb[s], vc], W=[ysb[s]])
                S.op("dve", lambda e, s=s: e.tensor_tensor(out=ysb[s][:], in0=ysb[s][:], in1=bon[s][:], op=ALU.add), R=[ysb[s], bon[s]], W=[ysb[s]])
                S.op("dve", lambda e, s=s: e.tensor_tensor(out=odT[:], in0=ysb[s][:], in1=gT[s][:], op=ALU.mult), R=[ysb[s], gT[s]], W=[odT])
                for cc in range(2):
                    S.op("pe", lambda e, cc=cc: e.transpose(out=PS_[:, cc * 128:(cc + 1) * 128], in_=odT[:, cc, :], identity=C("ident")),
                         R=[odT, cst, cmean, cvar], W=[PS_])
                S.op("act", lambda e, s=s: e.copy(out=omix[s][:, 768:1024], in_=PS_[:, 0:256]), R=[PS_], W=[omix[s]])
                S.op("act", lambda e, s=s: e.activation(out=junk[:], in_=omix[s][:], func=AF.Square), R=[omix[s]], W=[junk])
                S.op("dve", lambda e: e.reduce_sum(out=ss[:, 0:4], in_=junk[:].rearrange("p (g d) -> p g d", g=4), axis=AX.X), R=[junk], W=[ss])
                do_rsqrt(rstd, rstd[:, 0:4], ss, ss[:, 0:4], 1.0 / 256, EPS)
                S.op("dve", lambda e, s=s: e.tensor_tensor(out=omix[s][:].rearrange("p (g d) -> p g d", g=4), in0=omix[s][:].rearrange("p (g d) -> p g d", g=4),
                                                           in1=bc(rstd[:, 0:4].unsqueeze(2), [128, 4, 256]), op=ALU.mult), R=[omix[s], rstd], W=[omix[s]])
                S.op("dve", lambda e, s=s: e.tensor_tensor(out=on_bf[:], in0=omix[s][:], in1=VB("out_norm"), op=ALU.mult), R=[omix[s], vb], W=[on_bf])
                for kc in range(KC):
                    S.op("pe", lambda e, kc=kc: e.transpose(out=PT[:, kc * 128:(kc + 1) * 128], in_=on_bf[:, kc * 128:(kc + 1) * 128], identity=ident_bf[:]),
                         R=[on_bf, ident_bf], W=[PT])
                S.op("act", lambda e: e.copy(out=onT[:].rearrange("p k t -> p (k t)"), in_=PT[:]), R=[PT], W=[onT])
                for hb, pb in ((0, PA), (1, PC)):
                    for kc in range(KC):
                        S.op("pe", lambda e, hb=hb, pb=pb, kc=kc: e.matmul(pb[:], lhsT=onT[:, kc, :], rhs=Wout[:, kc, hb * 512:(hb + 1) * 512],
                                                                          start=(kc == 0), stop=(kc == KC - 1)), R=[onT, Wout, omix[s]], W=[pb])
                    S.op("dve", lambda e, hb=hb, pb=pb, xb=xb: e.tensor_tensor(out=xb[:, hb * 512:(hb + 1) * 512], in0=xb[:, hb * 512:(hb + 1) * 512],
                                                                              in1=pb[:], op=ALU.add), R=[xb, pb], W=[xb])
                S.dma("sp", (xdst[s0 + s, n * 128:(n + 1) * 128, :] if kind == "p" else xres_s), xb[:], R=[xb], track="x%d" % s)
        for g in (range(NG) if kind == "p" else []):
            for p2 in range(2):
                pp = slice(p2 * 64, (p2 + 1) * 64)
                S.op("pe", lambda e, g=g, pp=pp: e.matmul(PS_[pp, g * 64:(g + 1) * 64], lhsT=Sst[pp, g, :], rhs=C("ident")[pp, pp], start=True, stop=True),
                     R=[Sst, cst], W=[PS_])
        if kind == "p":
            S.op("act", lambda e: e.copy(out=wkv_nat[:].rearrange("p g j -> p (g j)"), in_=PS_[:, 0:NG * 64]), R=[PS_], W=[wkv_nat])
        for s in (range(NSEQ) if kind == "p" else []):
            for p2 in range(2):
                dst = wkv_prompt[l, s0 + s].rearrange("(cc q) i j -> q i cc j", q=2)[p2]
                S.dma("sp", dst, wkv_nat[p2 * 64:(p2 + 1) * 64, 2 * s:2 * s + 2, :], R=[wkv_nat], track="o_misc")
        S.barrier()

        S.barrier()
        S.emit()
        pes.close()
        cur[0] = es


    def common_tiles():
        t = {}
        t["vb"] = sb("vb", [128, NB])
        t["x_t"] = sb("x_t", [128, D])
        t["junk"] = sb("junk", [128, D])
        t["ss"] = sb("ss", [128, 8])
        t["rstd"] = sb("rstd", [128, 8])
        t["h_bf"] = sb("h_bf", [128, D], BF16)
        t["hT"] = sb("hT", [128, KC, 128], BF16)
        return t

    def mk_helpers(t):
        vb, junk, ss, rstd, h_bf, hT = t["vb"], t["junk"], t["ss"], t["rstd"], t["h_bf"], t["hT"]

        def VBx(name):
            o, n = VB_OFF[name]
            return vb[:, o:o + n]

        def do_rsqrt(dstb, dst, srcb, src, scale, eps):
            S.op("dve", lambda e: e.tensor_scalar(out=dst, in0=src, scalar1=scale, scalar2=eps, op0=ALU.mult, op1=ALU.add), R=[srcb], W=[dstb])
            S.op("act", lambda e: e.sqrt(out=dst, in_=dst), R=[dstb], W=[dstb])
            S.op("dve", lambda e: e.reciprocal(out=dst, in_=dst), R=[dstb], W=[dstb])

        def norm_T(xb, gname):
            S.op("act", lambda e: e.activation(out=junk[:], in_=xb[:], func=AF.Square), R=[xb], W=[junk])
            S.op("dve", lambda e: e.reduce_sum(out=ss[:, 0:1], in_=junk[:], axis=AX.X), R=[junk], W=[ss])
            do_rsqrt(rstd, rstd[:, 0:1], ss, ss[:, 0:1], 1.0 / D, EPS)
            S.op("dve", lambda e: e.scalar_tensor_tensor(out=h_bf[:], in0=xb[:], scalar=rstd[:, 0:1], in1=VBx(gname),
                                                         op0=ALU.mult, op1=ALU.mult), R=[xb, rstd, vb], W=[h_bf])
            for kc in range(KC):
                S.op("pe", lambda e, kc=kc: e.transpose(out=PT[:, kc * 128:(kc + 1) * 128], in_=h_bf[:, kc * 128:(kc + 1) * 128],
                                                        identity=ident_bf[:]), R=[h_bf, ident_bf], W=[PT])
            S.op("act", lambda e: e.copy(out=hT[:].rearrange("p k t -> p (k t)"), in_=PT[:]), R=[PT], W=[hT])

        def headnorm(pb, dstb, dst3, gname, nh, hd):
            S.op("act", lambda e: e.activation(out=junk[:, 0:nh * hd], in_=pb[:, 0:nh * hd], func=AF.Square), R=[pb], W=[junk])
            S.op("dve", lambda e: e.reduce_sum(out=ss[:, 0:nh], in_=junk[:, 0:nh * hd].rearrange("p (h d) -> p h d", h=nh), axis=AX.X),
                 R=[junk], W=[ss])
            do_rsqrt(rstd, rstd[:, 0:nh], ss, ss[:, 0:nh], 1.0 / hd, EPS)
            j3 = junk[:, 0:nh * hd].rearrange("p (h d) -> p h d", h=nh)
            S.op("dve", lambda e: e.tensor_tensor(out=j3, in0=pb[:, 0:nh * hd].rearrange("p (h d) -> p h d", h=nh),
                                                  in1=bc(rstd[:, 0:nh].unsqueeze(2), [128, nh, hd]), op=ALU.mult), R=[pb, rstd, junk], W=[junk])
            S.op("dve", lambda e: e.tensor_tensor(out=dst3, in0=j3, in1=bc(VBx(gname).unsqueeze(1), [128, nh, hd]), op=ALU.mult),
                 R=[junk, vb], W=[dstb])

        return VBx, do_rsqrt, norm_T, headnorm

    def xattn_phase(l):
        S.mark("xattn")
        pes = ExitStack()
        cur[0] = pes
        t = common_tiles()
        vb, x_t, junk, hT = t["vb"], t["x_t"], t["junk"], t["hT"]
        VBx, do_rsqrt, norm_T, headnorm = mk_helpers(t)
        Wq = sb("Wq", [128, KC, 512], BF16)
        Wk = sb("Wk", [128, KC, 512], BF16)
        Wv = sb("Wv", [128, KC, 512], BF16)
        Wo = sb("Wo", [128, 4, D], BF16)
        ones_bf = sb("ones_bf", [128, 2], BF16)
        kf = sb("kf", [128, 4, 128])
        k_bf = sb("k_bf", [128, 4, 128], BF16)
        vf = sb("vf", [128, 512])
        kTm = sb("kTm", [128, 4, 256], BF16)
        Vm = sb("Vm", [128, 2, 512], BF16)
        q_bf = sb("q_bf", [128, 4, 128], BF16)
        qTx = sb("qTx", [128, 4, 128], BF16)
        PTx = sb("PTx", [128, 2, 512], BF16)
        rl = sb("rlx", [128, 4])
        xo_bf = sb("xo_bf", [128, 4, 128], BF16)
        xoT = sb("xoT", [128, 4, 128], BF16)

        S.dma("sp", vb[:], bc(vb_d[l:l + 1, :], [128, NB]), W=[vb], track="par")
        S.dma("pool", Wq[:], wq_x[l].rearrange("(kc p) n -> p kc n", p=128), W=[Wq], track="w1")
        S.dma("pool", Wk[:], wk_x[l].rearrange("(kc p) n -> p kc n", p=128), W=[Wk], track="w1")
        S.dma("pool", Wv[:], wv_x[l].rearrange("(kc p) n -> p kc n", p=128), W=[Wv], track="w1")
        S.dma("pool", Wo[:], wo_x[l].rearrange("(kc p) n -> p kc n", p=128), W=[Wo], track="w1")
        S.op("pool", lambda e: e.memset(ones_bf[:], 1.0), W=[ones_bf])
        oq, nq = VB_OFF["xq_norm"]
        S.op("act", lambda e: e.mul(out=vb[:, oq:oq + nq], in_=vb[:, oq:oq + nq], mul=128.0 ** -0.5), R=[vb], W=[vb])


        def q_proj(src):
            S.dma("sp", x_t[:], src, W=[x_t], track="xx")
            norm_T(x_t, "norm_x")
            for kc in range(KC):
                S.op("pe", lambda e, kc=kc: e.matmul(PA[:], lhsT=hT[:, kc, :], rhs=Wq[:, kc, :], start=(kc == 0), stop=(kc == KC - 1)),
                     R=[hT, Wq], W=[PA])
            headnorm(PA, q_bf, q_bf[:], "xq_norm", 4, 128)
            for h in range(4):
                S.op("pe", lambda e, h=h: e.transpose(out=PT[:, h * 128:(h + 1) * 128], in_=q_bf[:, h, :], identity=ident_bf[:]),
                     R=[q_bf, ident_bf], W=[PT])
            S.op("act", lambda e: e.copy(out=qTx[:].rearrange("p h t -> p (h t)"), in_=PT[:, 0:512]), R=[PT], W=[qTx])

        def attn_epilogue(dst):
            S.op("dve", lambda e: e.reciprocal(out=rl[:], in_=PS_[:, 0:4]), R=[PS_], W=[rl])
            S.op("dve", lambda e: e.tensor_tensor(out=xo_bf[:], in0=PC[:].rearrange("p (h d) -> p h d", h=4),
                                                  in1=bc(rl[:].unsqueeze(2), [128, 4, 128]), op=ALU.mult), R=[PC, rl], W=[xo_bf])
            for h in range(4):
                S.op("pe", lambda e, h=h: e.transpose(out=PT[:, h * 128:(h + 1) * 128], in_=xo_bf[:, h, :], identity=ident_bf[:]),
                     R=[xo_bf, ident_bf], W=[PT])
            S.op("act", lambda e: e.copy(out=xoT[:].rearrange("p h t -> p (h t)"), in_=PT[:, 0:512]), R=[PT], W=[xoT])
            for hb, pb in ((0, PA), (1, PF[2])):
                for kc in range(4):
                    S.op("pe", lambda e, hb=hb, pb=pb, kc=kc: e.matmul(pb[:], lhsT=xoT[:, kc, :], rhs=Wo[:, kc, hb * 512:(hb + 1) * 512],
                                                                      start=(kc == 0), stop=(kc == 3)), R=[xoT, Wo], W=[pb])
                S.op("dve", lambda e, hb=hb, pb=pb: e.tensor_tensor(out=x_t[:, hb * 512:(hb + 1) * 512], in0=x_t[:, hb * 512:(hb + 1) * 512],
                                                                    in1=pb[:], op=ALU.add), R=[x_t, pb], W=[x_t])
            S.dma("sp", dst, x_t[:], R=[x_t], track="xx")

        for sg in range(NSEQ):
            for mt in range(2):
                S.dma("sp", x_t[:], mem_prompt[sg, mt * 128:(mt + 1) * 128, :], W=[x_t], track="xx")
                norm_T(x_t, "mem_norm")
                for kc in range(KC):
                    S.op("pe", lambda e, kc=kc: e.matmul(PA[:], lhsT=hT[:, kc, :], rhs=Wk[:, kc, :], start=(kc == 0), stop=(kc == KC - 1)),
                         R=[hT, Wk], W=[PA])
                for kc in range(KC):
                    S.op("pe", lambda e, kc=kc: e.matmul(PC[:], lhsT=hT[:, kc, :], rhs=Wv[:, kc, :], start=(kc == 0), stop=(kc == KC - 1)),
                         R=[hT, Wv], W=[PC])
                headnorm(PA, kf, kf[:], "xk_norm", 4, 128)
                S.dma("sp", mem_k_prompt[l, sg, mt * 128:(mt + 1) * 128, :], kf[:].rearrange("p h d -> p (h d)"), R=[kf], track="o_mk")
                S.op("act", lambda e: e.copy(out=k_bf[:], in_=kf[:]), R=[kf], W=[k_bf])
                for h in range(4):
                    S.op("pe", lambda e, h=h: e.transpose(out=PT[:, h * 128:(h + 1) * 128], in_=k_bf[:, h, :], identity=ident_bf[:]),
                         R=[k_bf, ident_bf], W=[PT])
                S.op("act", lambda e, mt=mt: e.copy(out=kTm[:, :, mt * 128:(mt + 1) * 128], in_=PT[:, 0:512].rearrange("p (h t) -> p h t", h=4)),
                     R=[PT], W=[kTm])
                S.op("act", lambda e: e.copy(out=vf[:], in_=PC[:]), R=[PC], W=[vf])
                S.dma("sp", mem_v_prompt[l, sg, mt * 128:(mt + 1) * 128, :], vf[:], R=[vf], track="o_mv")
                S.op("dve", lambda e, mt=mt: e.tensor_copy(out=Vm[:, mt, :], in_=vf[:]), R=[vf], W=[Vm])
            for n in range(NT):
                q_proj(xres[sg, n * 128:(n + 1) * 128, :])
                for mt in range(2):
                    pb = PF[mt]
                    for h in range(4):
                        S.op("pe", lambda e, pb=pb, mt=mt, h=h: e.matmul(pb[:, h * 128:(h + 1) * 128], lhsT=kTm[:, h, mt * 128:(mt + 1) * 128],
                                                                        rhs=qTx[:, h, :], start=True, stop=True), R=[kTm, qTx], W=[pb])
                    S.op("act", lambda e, pb=pb, mt=mt: e.activation(out=PTx[:, mt, :], in_=pb[:], func=AF.Exp), R=[pb], W=[PTx])
                first = True
                for mt in range(2):
                    for h in range(4):
                        S.op("pe", lambda e, mt=mt, h=h, first=first: e.matmul(PC[:, h * 128:(h + 1) * 128], lhsT=PTx[:, mt, h * 128:(h + 1) * 128],
                                                                              rhs=Vm[:, mt, h * 128:(h + 1) * 128], start=first,
                                                                              stop=(mt == 1 and h == 3)), R=[PTx, Vm], W=[PC])
                        first = False
                first = True
                for mt in range(2):
                    for h in range(4):
                        S.op("pe", lambda e, mt=mt, h=h, first=first: e.matmul(PS_[:, h:h + 1], lhsT=PTx[:, mt, h * 128:(h + 1) * 128],
                                                                              rhs=ones_bf[:, 0:1], start=first, stop=(mt == 1 and h == 3)),
                             R=[PTx, ones_bf], W=[PS_])
                        first = False
                attn_epilogue(xres[sg, n * 128:(n + 1) * 128, :])
        if SAMPLE:
            S.mark("xattn_sample")
            Ppad = sb("Ppad", [128, 2, 4, 128], BF16)
            kc_bf = sb("kc_bf", [128, 2, 512], BF16)
            S.op("pool", lambda e: e.memset(Ppad[:], 0.0), W=[Ppad])
            q_proj(xres_s)
            for b in range(16):
                S.dma("pool", kc_bf[:], cache_mem_k[l, b].rearrange("(mt p) c -> p mt c", p=128), W=[kc_bf], track="cmk")
                S.dma("pool", Vm[:], cache_mem_v[l, b].rearrange("(mt p) c -> p mt c", p=128), W=[Vm], track="cmv")
                for mt in range(2):
                    for h in range(4):
                        S.op("pe", lambda e, mt=mt, h=h: e.transpose(out=PT[:, h * 128:(h + 1) * 128], in_=kc_bf[:, mt, h * 128:(h + 1) * 128],
                                                                    identity=ident_bf[:]), R=[kc_bf, ident_bf], W=[PT])
                    S.op("act", lambda e, mt=mt: e.copy(out=kTm[:, :, mt * 128:(mt + 1) * 128], in_=PT[:, 0:512].rearrange("p (h t) -> p h t", h=4)),
                         R=[PT], W=[kTm])
                for mt in range(2):
                    pb = PF[mt]
                    for h in range(4):
                        S.op("pe", lambda e, pb=pb, mt=mt, h=h, b=b: e.matmul(pb[:, h * 8:(h + 1) * 8], lhsT=kTm[:, h, mt * 128:(mt + 1) * 128],
                                                                             rhs=qTx[:, h, b * 8:(b + 1) * 8], start=True, stop=True), R=[kTm, qTx], W=[pb])
                    S.op("act", lambda e, pb=pb, mt=mt, b=b: e.activation(out=Ppad[:, mt, :, b * 8:(b + 1) * 8],
                                                                         in_=pb[:, 0:32].rearrange("p (h q) -> p h q", h=4), func=AF.Exp), R=[pb], W=[Ppad])
                for mt in range(2):
                    for h in range(4):
                        fst = (b == 0 and mt == 0 and h == 0)
                        lst = (b == 15 and mt == 1 and h == 3)
                        S.op("pe", lambda e, mt=mt, h=h, fst=fst, lst=lst: e.matmul(PC[:, h * 128:(h + 1) * 128], lhsT=Ppad[:, mt, h, :],
                                                                                   rhs=Vm[:, mt, h * 128:(h + 1) * 128], start=fst, stop=lst), R=[Ppad, Vm], W=[PC])
                for mt in range(2):
                    for h in range(4):
                        fst = (b == 0 and mt == 0 and h == 0)
                        lst = (b == 15 and mt == 1 and h == 3)
                        S.op("pe", lambda e, mt=mt, h=h, fst=fst, lst=lst: e.matmul(PS_[:, h:h + 1], lhsT=Ppad[:, mt, h, :], rhs=ones_bf[:, 0:1],
                                                                                   start=fst, stop=lst), R=[Ppad, ones_bf], W=[PS_])
                S.op("pool", lambda e, b=b: e.memset(Ppad[:, :, :, b * 8:(b + 1) * 8], 0.0), R=[], W=[Ppad])
            attn_epilogue(xres_s)
        S.barrier()
        S.emit()
        pes.close()
        cur[0] = es

    def ffn_phase(l):
        S.mark("ffn")
        pes = ExitStack()
        cur[0] = pes
        t = common_tiles()
        vb, x_t, junk, hT = t["vb"], t["x_t"], t["junk"], t["hT"]
        VBx, do_rsqrt, norm_T, headnorm = mk_helpers(t)
        Wf1 = sb("Wf1", [128, KC, 2 * D_FF], BF16)
        Wf2 = sb("Wf2", [128, FC, D], BF16)
        sgf = sb("sgf", [128, 512])
        act = sb("act", [128, FC, 128], BF16)
        S.dma("sp", vb[:], bc(vb_d[l:l + 1, :], [128, NB]), W=[vb], track="par")
        w1 = w_ffn_in[l].rearrange("(kc p) n -> p kc n", p=128)
        for c0 in range(0, 2 * D_FF, 1408):
            S.dma("pool", Wf1[:, :, c0:c0 + 1408], w1[:, :, c0:c0 + 1408], W=[Wf1], track="w1")
        w2 = w_ffn_out[l].rearrange("(fc p) n -> p fc n", p=128)
        S.dma("pool", Wf2[:, 0:11, :], w2[:, 0:11, :], W=[Wf2], track="w1")
        S.dma("pool", Wf2[:, 11:22, :], w2[:, 11:22, :], W=[Wf2], track="w1")
        dst = y_prompt if l == L - 1 else xres
        tiles = [(xres[sg, n * 128:(n + 1) * 128, :], dst[sg, n * 128:(n + 1) * 128, :]) for sg in range(NSEQ) for n in range(NT)]
        if SAMPLE:
            tiles.append((xres_s, (y_sample if l == L - 1 else xres_s)))
        for (tsrc, tdst) in tiles:
            if True:
                S.dma("sp", x_t[:], tsrc, W=[x_t], track="xx")
                norm_T(x_t, "norm_ffn")
                gi = 0
                for fc0 in range(0, FC, 4):
                    nch = min(4, FC - fc0)
                    G = (PA, PF[0])[gi % 2]
                    U = (PC, PF[1])[gi % 2]
                    gi += 1
                    for j in range(nch):
                        for kc in range(KC):
                            S.op("pe", lambda e, G=G, j=j, kc=kc, fc0=fc0: e.matmul(G[:, j * 128:(j + 1) * 128], lhsT=Wf1[:, kc, (fc0 + j) * 128:(fc0 + j + 1) * 128],
                                                                                   rhs=hT[:, kc, :], start=(kc == 0), stop=(kc == KC - 1)), R=[Wf1, hT], W=[G])
                    for j in range(nch):
                        for kc in range(KC):
                            S.op("pe", lambda e, U=U, j=j, kc=kc, fc0=fc0: e.matmul(U[:, j * 128:(j + 1) * 128],
                                                                                   lhsT=Wf1[:, kc, D_FF + (fc0 + j) * 128:D_FF + (fc0 + j + 1) * 128],
                                                                                   rhs=hT[:, kc, :], start=(kc == 0), stop=(kc == KC - 1)), R=[Wf1, hT], W=[U])
                    S.op("act", lambda e, G=G, nch=nch: e.activation(out=sgf[:, 0:nch * 128], in_=G[:, 0:nch * 128], func=AF.Silu), R=[G], W=[sgf])
                    S.op("dve", lambda e, U=U, nch=nch, fc0=fc0: e.tensor_tensor(out=act[:, fc0:fc0 + nch, :].rearrange("p c t -> p (c t)"), in0=sgf[:, 0:nch * 128],
                                                                                  in1=U[:, 0:nch * 128], op=ALU.mult), R=[sgf, U], W=[act])
                for hb, pb in ((0, PS_), (1, PY)):
                    for fc in range(FC):
                        S.op("pe", lambda e, hb=hb, pb=pb, fc=fc: e.matmul(pb[:], lhsT=act[:, fc, :], rhs=Wf2[:, fc, hb * 512:(hb + 1) * 512],
                                                                          start=(fc == 0), stop=(fc == FC - 1)), R=[act, Wf2], W=[pb])
                    S.op("dve", lambda e, hb=hb, pb=pb: e.tensor_tensor(out=x_t[:, hb * 512:(hb + 1) * 512], in0=x_t[:, hb * 512:(hb + 1) * 512],
                                                                        in1=pb[:], op=ALU.add), R=[x_t, pb], W=[x_t])
                S.dma("sp", tdst, x_t[:], R=[x_t], track="xx")
        S.barrier()
        S.emit()
        pes.close()
        cur[0] = es

    SG = cfg.get('SG', 1)
    for l in range(L):
        for s0 in range(0, NSEQ, SG):
            mixer_phase(l, s0, SG)
        if SAMPLE:
            mixer_phase(l, 0, 1, kind="s")
        if cfg.get('PHASES', 3) >= 2:
            xattn_phase(l)
        if cfg.get('PHASES', 3) >= 3:
            ffn_phase(l)

    S.barrier()
    S.emit()
    es.close()
    if cfg.get('PRINT_ALLOC'):
        print(sorted(alloc_log, key=lambda x: -x[1])[:60])
    if cfg.get('PRINT_MARKS'):
        print(S.marks, S.n)
    return nc, carr


OUT_KEYS = ["y_prompt", "ckv_prompt", "kpe_prompt", "conv_prompt", "shift_prompt", "wkv_prompt", "mem_k_prompt", "mem_v_prompt",
            "y_sample", "ckv_s_out", "kpe_s_out", "conv_old_out", "conv_new_out", "shift_s_out", "wkv_s_out", "sgv_s_out"]


def core_inputs(inp, c, ncore, carr, L, ag, nch=10):
    B = inp["x_prompt"].shape[0]
    NSEQ = B // ncore
    Bs = inp["x_sample"].shape[0]
    nb = Bs // ncore
    f = np.float32
    vb, vc = pack_small(inp, L)
    m = {"consts": carr, "vb": vb, "vc": vc}
    for k in WEIGHT_KEYS:
        m[k] = np.ascontiguousarray(inp[k], dtype=f)
    m["x_prompt"] = np.ascontiguousarray(inp["x_prompt"][c * NSEQ:(c + 1) * NSEQ])
    m["mem_prompt"] = np.ascontiguousarray(inp["mem_prompt"][c * NSEQ:(c + 1) * NSEQ])
    m["x_sample"] = np.ascontiguousarray(inp["x_sample"][c * nb:(c + 1) * nb]).reshape(nb * 8, D)
    nph = inp["cache_ckv"].shape[0]
    if ag > 1:
        pc = nph // (ag * nch)
        ck = inp["cache_ckv"].reshape(nch, ag, pc, L, 128, 128)[:, c]
        kp = inp["cache_kpe"].reshape(nch, ag, pc, L, 128, 32)[:, c]
        m["cache_ckv"] = np.ascontiguousarray(ck).reshape(-1, 128)
        m["cache_kpe"] = np.ascontiguousarray(kp).reshape(-1, 32)
    else:
        m["cache_ckv"] = np.ascontiguousarray(inp["cache_ckv"]).reshape(-1, 128)
        m["cache_kpe"] = np.ascontiguousarray(inp["cache_kpe"]).reshape(-1, 32)
    m["cache_mem_k"] = np.ascontiguousarray(inp["cache_mem_k"][:, c * nb:(c + 1) * nb]).reshape(L, nb, 256, 512)
    m["cache_mem_v"] = np.ascontiguousarray(inp["cache_mem_v"][:, c * nb:(c + 1) * nb]).reshape(L, nb, 256, 512)
    m["state_conv"] = np.ascontiguousarray(inp["state_conv"][:, c * nb:(c + 1) * nb])
    m["state_shift"] = np.ascontiguousarray(inp["state_shift"][:, c * nb:(c + 1) * nb])
    m["state_wkv"] = np.ascontiguousarray(inp["state_wkv"][:, c * nb:(c + 1) * nb])
    m["page_table"] = np.ascontiguousarray(inp["page_table"][c * nb:(c + 1) * nb]).astype(np.int32)
    return m


def assemble(res, inp, L):
    f = np.float32
    B = inp["x_prompt"].shape[0]
    Bs, Ts, _ = inp["x_sample"].shape
    nb = Bs // len(res)
    cat = lambda k, ax: np.concatenate([r[k] for r in res], axis=ax)
    y_prompt = cat("y_prompt", 0)
    ckv_prompt = cat("ckv_prompt", 0)
    kpe_prompt = cat("kpe_prompt", 0)
    conv_prompt = cat("conv_prompt", 1)
    shift_prompt = cat("shift_prompt", 1)
    wkv_prompt = cat("wkv_prompt", 1)
    mem_k = cat("mem_k_prompt", 1).reshape(L, B, 256, 4, 128)
    mem_v = cat("mem_v_prompt", 1).reshape(L, B, 256, 4, 128)
    y_sample = cat("y_sample", 0).reshape(Bs, Ts, D)
    ckv_sample = np.concatenate([r["ckv_s_out"].reshape(L, nb, Ts, 128) for r in res], axis=1).transpose(1, 0, 2, 3)
    kpe_sample = np.concatenate([r["kpe_s_out"].reshape(L, nb, Ts, 32) for r in res], axis=1).transpose(1, 0, 2, 3)
    conv_sample = np.concatenate([np.concatenate([r["conv_old_out"], r["conv_new_out"].reshape(L, nb, Ts, 256)], axis=2) for r in res], axis=1)
    shift_sample = cat("shift_s_out", 1)
    wkv_sample = cat("wkv_s_out", 1)
    sgu_v = np.concatenate([r["sgv_s_out"].reshape(L, nb, Ts, 256) for r in res], axis=1)
    outs = (y_prompt, y_sample, ckv_prompt, kpe_prompt, ckv_sample, kpe_sample, mem_k, mem_v, conv_prompt, conv_sample,
            shift_prompt, shift_sample, wkv_prompt, wkv_sample, sgu_v)
    return tuple(np.ascontiguousarray(o, dtype=f) for o in outs)


def kernel(**inp):
    inp = {k: np.asarray(v) for k, v in inp.items()}
    B, T, _ = inp["x_prompt"].shape
    NCORE = 8
    L = inp["w_in"].shape[0]
    nph = inp["cache_ckv"].shape[0]
    npg = inp["page_table"].shape[1]
    nc, carr = build(dict(NSEQ=B // NCORE, NT=T // 128, L=L, SG=1, NPG=npg, NPH=nph, AG=1))
    in_maps = [core_inputs(inp, c, NCORE, carr, L, ag=1) for c in range(NCORE)]
    res = run_bass_kernel_spmd(nc, in_maps, core_ids=list(range(NCORE))).results
    return assemble(res, inp, L)
```

```python
import numpy as np
import concourse.bass as bass
import concourse.mybir as mybir
from concourse.bass_utils import run_bass_kernel_spmd
from contextlib import ExitStack

F32 = mybir.dt.float32
BF16 = mybir.dt.bfloat16
I32 = mybir.dt.int32
ALU = mybir.AluOpType
AF = mybir.ActivationFunctionType
AX = mybir.AxisListType

D = 1024
KC = 8
DEPTH = 2
EPS = 1e-6
LN_EPS = 1e-5
RW_LN_EPS = 64e-5
MLA_SCALE = 96.0 ** -0.5
CONV_W = 31
D_FF = 2816
FC = 22


WEIGHT_KEYS = ["w_in", "mla_w_uq", "mla_w_uk", "mla_w_uv", "conv_pw", "sgu_w", "rw_w2", "rw_a2", "rw_g2", "w_out",
               "wq_x", "wk_x", "wv_x", "wo_x", "w_ffn_in", "w_ffn_out"]


class Buf:
    def __init__(self, t, name):
        self.t = t
        self.name = name
        self.w = None
        self.r = []

    def __getitem__(self, k):
        return self.t[k]


class Tok:
    __slots__ = ("sem", "val", "closed", "idx", "dma")

    def __init__(self, sem, val, idx=0, dma=False):
        self.sem = sem
        self.val = val
        self.closed = False
        self.idx = idx
        self.dma = dma


class Sched:
    ENG = ("pe", "dve", "act", "pool", "sp")

    def __init__(self, nc, es):
        self.nc = nc
        self.es = es
        self.q = {e: [] for e in self.ENG}
        self.sem = {e: es.enter_context(nc.semaphore("s_" + e)) for e in self.ENG}
        self.cnt = {e: 0 for e in self.ENG}
        self.waited = {e: {} for e in self.ENG}
        self.tracks = {}
        self.nops = 0
        self.n = 0
        self.limit = 1 << 60
        self.marks = []
        self.own_sync = True

    def mark(self, name):
        self.marks.append((name, self.n))

    def track(self, name):
        if name not in self.tracks:
            self.tracks[name] = [self.es.enter_context(self.nc.semaphore("t_" + name)), 0, None, 0]
        return self.tracks[name]

    def _wait(self, eng, tok):
        key = id(tok.sem)
        if tok.dma:
            if self.waited[eng].get(key, -1) >= tok.idx:
                return
            tok.closed = True
            self.waited[eng][key] = tok.idx
        else:
            if self.waited[eng].get(key, 0) >= tok.val:
                return
            self.waited[eng][key] = tok.val
        self.q[eng].append(("w", tok))

    def _deps(self, eng, R, W, skip=None):
        own = self.sem[eng] if (eng == "pe" or (not self.own_sync and eng in ("dve", "act"))) else None
        for b in R:
            if b.w is not None and b.w.sem is not own and b.w is not skip:
                self._wait(eng, b.w)
        for b in W:
            if b.w is not None and b.w.sem is not own and b.w is not skip:
                self._wait(eng, b.w)
            for t in b.r:
                if t.sem is not own:
                    self._wait(eng, t)

    def _post(self, tok, R, W):
        for b in R:
            b.r.append(tok)
        for b in W:
            b.w = tok
            b.r = []

    def op(self, eng, fn, R=(), W=()):
        self.n += 1
        if self.n > self.limit:
            return None
        self._deps(eng, R, W)
        self.cnt[eng] += 1
        tok = Tok(self.sem[eng], self.cnt[eng])
        self.q[eng].append(("o", fn, self.sem[eng], 1))
        self._post(tok, R, W)
        self.nops += 1
        return tok

    def dma(self, eng, out, in_, R=(), W=(), track=None, **kw):
        self.n += 1
        if self.n > self.limit:
            return None
        tr = self.track(track or ("q_" + eng))
        if tr[2] is None or tr[2].closed:
            if tr[2] is not None:
                self._wait(eng, tr[2])
            tr[2] = Tok(tr[0], tr[1], idx=tr[3], dma=True)
            tr[3] += 1
        self._deps(eng, R, W, skip=tr[2])
        if tr[2].closed:
            self._wait(eng, tr[2])
            tr[2] = Tok(tr[0], tr[1], idx=tr[3], dma=True)
            tr[3] += 1
        tr[1] += 16
        tr[2].val = tr[1]
        tok = tr[2]
        self.q[eng].append(("o", (lambda e, o=out, i=in_, k=kw: e.dma_start(out=o, in_=i, **k)), tr[0], 16))
        self._post(tok, R, W)
        return tok

    def dmaf(self, eng, fn, R=(), W=(), track=None):
        self.n += 1
        if self.n > self.limit:
            return None
        tr = self.track(track or ("q_" + eng))
        if tr[2] is None or tr[2].closed:
            if tr[2] is not None:
                self._wait(eng, tr[2])
            tr[2] = Tok(tr[0], tr[1], idx=tr[3], dma=True)
            tr[3] += 1
        self._deps(eng, R, W, skip=tr[2])
        if tr[2].closed:
            self._wait(eng, tr[2])
            tr[2] = Tok(tr[0], tr[1], idx=tr[3], dma=True)
            tr[3] += 1
        tr[1] += 16
        tr[2].val = tr[1]
        tok = tr[2]
        self.q[eng].append(("o", fn, tr[0], 16))
        self._post(tok, R, W)
        return tok

    def barrier(self):
        toks = [Tok(self.sem[e], self.cnt[e]) for e in self.ENG if self.cnt[e] > 0]
        toks += [t[2] for t in self.tracks.values() if t[2] is not None]
        for e in self.ENG:
            for t in toks:
                self._wait(e, t)

    def emit(self):
        nc = self.nc
        q = self.q
        self.q = {e: [] for e in self.ENG}

        def run(e, items):
            for it in items:
                if it[0] == "w":
                    e.wait_ge(it[1].sem, it[1].val)
                else:
                    it[1](e).then_inc(it[2], it[3])

        with nc.Block() as block:
            @block.tensor
            def _(e):
                run(e, q["pe"])

            @block.vector
            def _(e):
                run(e, q["dve"])

            @block.scalar
            def _(e):
                run(e, q["act"])

            @block.gpsimd
            def _(e):
                run(e, q["pool"])

            @block.sync
            def _(e):
                run(e, q["sp"])


def bc(ap, shape):
    return ap.to_broadcast(list(shape))


VB_SPEC = [("norm_mix", 1024), ("mla_q_norm", 192), ("mla_kv_norm", 128), ("mla_gq_nope", 64),
           ("mla_gq_rope", 32), ("mla_gk_nope", 64), ("mla_gk_rope", 32), ("sgu_norm_g", 256),
           ("sgu_norm_b", 256), ("out_norm", 1024), ("norm_x", 1024), ("norm_ffn", 1024),
           ("mem_norm", 1024), ("xq_norm", 128), ("xk_norm", 128)]
VB_OFF = {}
_o = 0
for _n, _s in VB_SPEC:
    VB_OFF[_n] = (_o, _s)
    _o += _s
NB = _o

VC_SPEC = [("conv_w", 62), ("conv_b", 2), ("conv_norm_g", 2), ("conv_norm_b", 2), ("rw_mu", 8),
           ("rw_w0", 2), ("rw_a0", 2), ("rw_kk", 2), ("rw_ka", 2), ("rw_rk", 2), ("rw_ln_g", 2),
           ("rw_ln_b", 2), ("sgu_b", 4), ("gk_col", 1)]
VC_OFF = {}
_o = 0
for _n, _s in VC_SPEC:
    VC_OFF[_n] = (_o, _s)
    _o += _s
NCOL = _o


def pack_small(inp, L):
    vb = np.zeros((L, NB), np.float32)
    vc = np.zeros((L, 128, NCOL), np.float32)
    for l in range(L):
        for n, s in VB_SPEC:
            o = VB_OFF[n][0]
            vb[l, o:o + s] = np.asarray(inp[n][l]).reshape(-1)
        for n, s in VC_SPEC:
            o = VC_OFF[n][0]
            if n == "gk_col":
                vc[l, 0:64, o] = np.asarray(inp["mla_gk_nope"][l]).reshape(-1)
                continue
            a = np.asarray(inp[n][l])
            if n == "conv_w":
                vc[l, :, o:o + 62] = a.reshape(31, 2, 128).transpose(2, 1, 0).reshape(128, 62)
            elif n == "sgu_b":
                vc[l, :, o:o + 4] = a.T
            else:
                vc[l, :, o:o + s] = a.reshape(s, 128).T
    return vb, vc


def make_consts(NT, past=16384):
    c = {}
    c["ident"] = np.eye(128, dtype=np.float32)
    blk = np.kron(np.eye(2, dtype=np.float32), np.ones((64, 64), np.float32))
    c["blkone"] = blk
    c["blkavg"] = blk / 64.0
    p = np.arange(128)
    c["mask"] = (p[:, None] <= p[None, :]).astype(np.float32)
    inv = np.power(np.float32(10000.0), -np.arange(16, dtype=np.float32) / np.float32(16)).astype(np.float32)
    pos = (np.arange(NT * 128, dtype=np.float32)).reshape(NT, 128).T
    ang = (pos[:, :, None] * inv[None, None, :]).astype(np.float32)
    c["cosp"] = np.cos(ang).astype(np.float32).reshape(128, NT * 16)
    c["sinp"] = np.sin(ang).astype(np.float32).reshape(128, NT * 16)
    poss = (past + (p % 8)).astype(np.float32)
    angs = (poss[:, None] * inv[None, :]).astype(np.float32)
    c["coss"] = np.cos(angs).astype(np.float32)
    c["sins"] = np.sin(angs).astype(np.float32)
    c["pidx"] = p.astype(np.float32).reshape(128, 1)
    bq = p // 8
    tq = p % 8
    ms = np.zeros((128, 16, 4, 8), np.float32)
    for b in range(16):
        for t in range(8):
            ms[:, b, :, t] = ((bq == b) & (tq <= t)).astype(np.float32)[:, None]
    c["maskS"] = ms.reshape(128, 512)
    names = ["ident", "blkone", "blkavg", "mask", "cosp", "sinp", "coss", "sins", "pidx", "maskS"]
    offs = {}
    o = 0
    for n in names:
        offs[n] = (o, c[n].shape[1])
        o += c[n].shape[1]
    arr = np.concatenate([c[n] for n in names], axis=1).astype(np.float32)
    return arr, offs


def build(cfg):
    NSEQ = cfg["NSEQ"]
    NT = cfg["NT"]
    L = cfg.get("L", DEPTH)
    STAGE = cfg.get("STAGE", 99)
    T = NT * 128
    carr, COFF = make_consts(NT, cfg.get('NPG', 128) * 128)
    NCONST = carr.shape[1]

    nc = bass.Bass("TRN2", target_bir_lowering=False)
    es = ExitStack()
    S = Sched(nc, es)
    S.limit = cfg.get('MAXOPS', 1 << 60)
    S.own_sync = cfg.get('OWN_SYNC', True)

    def din(name, shape, dt=F32):
        return nc.dram_tensor(name, list(shape), dt, kind="ExternalInput").ap()

    def dout(name, shape, dt=F32):
        return nc.dram_tensor(name, list(shape), dt, kind="ExternalOutput").ap()

    def dscr(name, shape, dt=F32):
        return nc.dram_tensor(name, list(shape), dt, kind="Internal").ap()

    x_prompt = din("x_prompt", [NSEQ, T, D])
    consts_d = din("consts", [128, NCONST])
    vb_d = din("vb", [L, NB])
    vc_d = din("vc", [L, 128, NCOL])
    w_in = din("w_in", [L, D, 2400])
    w_uq = din("mla_w_uq", [L, 192, 384])
    w_uk = din("mla_w_uk", [L, 128, 256])
    w_uv = din("mla_w_uv", [L, 128, 256])
    conv_pw = din("conv_pw", [L, 256, 256])
    sgu_w = din("sgu_w", [L, 4, 128, 128])
    rw_w2 = din("rw_w2", [L, 64, 256])
    rw_a2 = din("rw_a2", [L, 64, 256])
    rw_g2 = din("rw_g2", [L, 128, 256])
    w_out = din("w_out", [L, D, D])
    mem_prompt = din("mem_prompt", [NSEQ, 256, D])
    wq_x = din("wq_x", [L, D, 512])
    wk_x = din("wk_x", [L, D, 512])
    wv_x = din("wv_x", [L, D, 512])
    wo_x = din("wo_x", [L, 512, D])
    w_ffn_in = din("w_ffn_in", [L, D, 2 * D_FF])
    w_ffn_out = din("w_ffn_out", [L, D_FF, D])
    mem_k_prompt = dout("mem_k_prompt", [L, NSEQ, 256, 512])
    mem_v_prompt = dout("mem_v_prompt", [L, NSEQ, 256, 512])

    y_prompt = dout("y_prompt", [NSEQ, T, D])
    ckv_prompt = dout("ckv_prompt", [NSEQ, NT, L, 128, 128])
    kpe_prompt = dout("kpe_prompt", [NSEQ, NT, L, 128, 32])
    conv_prompt = dout("conv_prompt", [L, NSEQ, 30, 256])
    shift_prompt = dout("shift_prompt", [L, NSEQ, 1024])
    wkv_prompt = dout("wkv_prompt", [L, NSEQ, 4, 64, 64])

    SAMPLE = cfg.get("SAMPLE", True)
    NPG = cfg.get("NPG", 128)
    NPH = cfg.get("NPH", 2560)
    AG = cfg.get("AG", 8)
    x_sample = din("x_sample", [128, D])
    cache_ckv_in = din("cache_ckv", [NPH * L * 128, 128])
    cache_kpe_in = din("cache_kpe", [NPH * L * 128, 32])
    cache_mem_k = din("cache_mem_k", [L, 16, 256, 512])
    cache_mem_v = din("cache_mem_v", [L, 16, 256, 512])
    state_conv = din("state_conv", [L, 16, 30, 256])
    state_shift = din("state_shift", [L, 16, 1024])
    state_wkv = din("state_wkv", [L, 16, 4, 64, 64])
    page_table = din("page_table", [16, NPG], I32)
    y_sample = dout("y_sample", [128, D])
    ckv_s_out = dout("ckv_s_out", [L, 128, 128])
    kpe_s_out = dout("kpe_s_out", [L, 128, 32])
    conv_old_out = dout("conv_old_out", [L, 16, 22, 256])
    conv_new_out = dout("conv_new_out", [L, 128, 256])
    shift_s_out = dout("shift_s_out", [L, 16, 1024])
    wkv_s_out = dout("wkv_s_out", [L, 16, 4, 64, 64])
    sgv_s_out = dout("sgv_s_out", [L, 128, 256])
    xres_s = dscr("xres_s", [128, D])
    if AG > 1:
        cck_loc = dscr("cck_loc", [NPH * L * 128, 128])
        ckp_loc = dscr("ckp_loc", [NPH * L * 128, 32])
        cache_ckv_f = dscr("cache_ckv_f", [AG * NPH * L * 128, 128])
        cache_kpe_f = dscr("cache_kpe_f", [AG * NPH * L * 128, 32])
    else:
        cache_ckv_f = cache_ckv_in
        cache_kpe_f = cache_kpe_in
    xres = dscr("xres", [NSEQ, T, D])
    v_scr = dscr("v_scr", [NSEQ, 128, 256])

    cur = [es]
    uniq = [0]

    alloc_log = []

    def sb(name, shape, dt=F32):
        uniq[0] += 1
        alloc_log.append((name, int(np.prod(shape[1:])) * (4 if dt in (F32, I32) else 2)))
        return Buf(cur[0].enter_context(nc.sbuf_tensor("sb%d_%s" % (uniq[0], name), list(shape), dt)), name)

    def ps(name, shape, dt=F32):
        return Buf(es.enter_context(nc.psum_tensor("ps_" + name, list(shape), dt)), name)

    cst = sb("cst", [128, NCONST])
    S.dma("sp", cst[:], consts_d, W=[cst], track="cst")

    def C(name):
        o, n = COFF[name]
        return cst[:, o:o + n]

    ident_bf = sb("ident_bf", [128, 128], BF16)
    mask_bf = sb("mask_bf", [128, 128], BF16)
    S.op("dve", lambda e: e.tensor_copy(out=ident_bf[:], in_=C("ident")), R=[cst], W=[ident_bf])
    S.op("dve", lambda e: e.tensor_copy(out=mask_bf[:], in_=C("mask")), R=[cst], W=[mask_bf])

    cfull = Buf(None, "cfull")
    cloc = Buf(None, "cloc")
    if SAMPLE and AG > 1:
        NCH = cfg.get("NCH", 10)
        rows_c = (NPH // NCH) * L * 128
        rg = [list(range(AG))]
        for ch in range(NCH):
            r0 = ch * rows_c
            for (src, loc, full, w) in ((cache_ckv_in, cck_loc, cache_ckv_f, 128), (cache_kpe_in, ckp_loc, cache_kpe_f, 32)):
                g = 2048 // w
                S.dma("sp", loc[r0:r0 + rows_c, :].rearrange("(a b) c -> a (b c)", b=g), src[r0:r0 + rows_c, :].rearrange("(a b) c -> a (b c)", b=g),
                      W=[cloc], track="agcp")
            for (loc, full) in ((cck_loc, cache_ckv_f), (ckp_loc, cache_kpe_f)):
                S.dmaf("pool", lambda e, loc=loc, full=full, r0=r0: e.collective_compute(
                    "AllGather", ALU.bypass, replica_groups=rg, ins=[loc[r0:r0 + rows_c, :]],
                    outs=[full[AG * r0:AG * (r0 + rows_c), :]]), R=[cloc], W=[cfull], track="ag")

    PA = ps("PA", [128, 512])
    PC = ps("PC", [128, 512])
    PF = [ps("PF%d" % i, [128, 512]) for i in range(3)]
    PT = ps("PT", [128, 1024], BF16)
    PS_ = ps("PSc", [128, 512])
    PY = ps("PY", [128, 512])

    def mixer_phase(l, s0, NSEQ, kind="p"):
        NTk = NT if kind == "p" else 1
        pes = ExitStack()
        cur[0] = pes
        NB1 = 3072
        vb = sb("vb", [128, NB1])
        vc = sb("vc", [128, NCOL])
        Win_tm = sb("Win_tm", [128, KC, 864], BF16)
        Win_fm = sb("Win_fm", [128, KC, 1536], BF16)
        Wuq = sb("Wuq", [128, 2, 384], BF16)
        Wuk = sb("Wuk", [128, 256], BF16)
        Wuv = sb("Wuv", [128, 256], BF16)
        Wpw = sb("Wpw", [128, 2, 256], BF16)
        Wsg_raw = sb("Wsg_raw", [128, 4, 128], BF16)
        WsgT = sb("WsgT", [128, 4, 128], BF16)
        W2w = sb("W2w", [128, 256])
        W2a = sb("W2a", [128, 256])
        G2 = sb("G2", [128, 256])
        Wout = sb("Wout", [128, KC, D], BF16) if kind == "p" else Win_fm
        omka = sb("omka", [128, 2])

        def VB(name):
            o, n = VB_OFF[name]
            return vb[:, o:o + n]

        def VC(name):
            o, n = VC_OFF[name]
            return vc[:, o:o + n]

        x_t = [sb("x_t%d" % s, [128, D]) for s in range(NSEQ)]
        junk = sb("junk", [128, D])
        ss = sb("ss", [128, 8])
        rstd = sb("rstd", [128, 8])
        h_bf = sb("h_bf", [128, D], BF16)
        hT = sb("hT", [128, KC, 128], BF16)
        pa = sb("pa", [128, 352])
        cqn = sb("cqn", [128, 192], BF16)
        cqT = sb("cqT", [128, 2, 128], BF16)
        qsb = sb("qsb", [128, 4, 96])
        qfull = sb("qfull", [128, 4, 96], BF16)
        qT = sb("qT", [96, 4, 128], BF16)
        tmpr = sb("tmpr", [128, 4, 32])
        tr1 = sb("tr1", [128, 4, 16])
        tr2 = sb("tr2", [128, 4, 16])
        ckv_sb = sb("ckv_sb", [128, 128])
        ckv_bf = sb("ckv_bf", [128, 128], BF16)
        ckvT = sb("ckvT", [128, 128], BF16)
        kfull = sb("kfull", [128, 4, 96], BF16)
        kpe_sb = sb("kpe_sb", [128, 32])
        kT = [sb("kT%d" % s, [96, 4, T if kind == "p" else 128], BF16) for s in range(NSEQ)]
        Vaug = [sb("Vaug%d" % s, [128, NT if kind == "p" else 1, 4, 65], BF16) for s in range(NSEQ)]
        PTs = sb("PTs", [128, 512], BF16)
        rl = sb("rl", [128, 4])
        omix = [sb("omix%d" % s, [128, D]) for s in range(NSEQ)]
        zc = sb("zc", [128, 512])
        t5 = sb("t5", [128, 512])
        vn = sb("vn", [128, 256])
        vn_bf = sb("vn_bf", [128, 256], BF16)
        xin = [sb("xin%d" % s, [128, 2, 158]) for s in range(NSEQ)]
        sgm = sb("sgm", [128, 2, 128])
        cacc = sb("cacc", [128, 2, 128])
        csq = sb("csq", [128, 2, 128])
        cmean = sb("cmean", [128, 2, 128])
        cvar = sb("cvar", [128, 2, 128])
        csl = sb("csl", [128, 2, 128], BF16)
        ctr = sb("ctr", [30, 256])
        PD = [sb("PD%d" % s, [128, 8, 129 if kind == "p" else 2]) for s in range(NSEQ)]
        xs = [sb("xs%d" % s, [128, 8, 128]) for s in range(NSEQ)]
        tw = sb("tw", [128, 128])
        lastc = sb("lastc", [128, 8])
        lastT = sb("lastT", [8, 128])
        dec = [sb("dec%d" % s, [128, 2, 128]) for s in range(NSEQ)]
        aa = sb("aa", [128, 2, 128])
        sgg = sb("sgg", [128, 128])
        gT = [sb("gT%d" % s, [128, 2, 128]) for s in range(NSEQ)]
        kk = sb("kk", [128, 2, 128])
        kk2 = sb("kk2", [128, 2, 128])
        nkk = [sb("nkk%d" % s, [128, 2, 128]) for s in range(NSEQ)]
        kka = [sb("kka%d" % s, [128, 2, 128]) for s in range(NSEQ)]
        kfin = [sb("kfin%d" % s, [128, 2, 128]) for s in range(NSEQ)]
        bon = [sb("bon%d" % s, [128, 2, 128]) for s in range(NSEQ)]
        tk = sb("tk", [128, 2, 128])
        vtm = sb("vtm", [128, 256])
        NG = 2 * NSEQ
        Sst = sb("Sst", [128, NG, 64])
        T1 = sb("T1", [128, NG, 64])
        Sst2 = sb("Sst2", [128, NG, 64])
        Abuf = sb("Abuf", [128, NG, 64])
        Sb = [Sst2, Sst]
        CH = 8
        vbc = sb("vbc", [128, CH if kind == "p" else 1, NG, 64])
        KV = sb("KV", [128, CH if kind == "p" else 1, NG, 64])
        ysb = [sb("ysb%d" % s, [128, 2, 128]) for s in range(NSEQ)]
        ysq = sb("ysq", [128, 2, 128])
        odT = sb("odT", [128, 2, 128])
        on_bf = sb("on_bf", [128, D], BF16)
        onT = sb("onT", [128, KC, 128], BF16)
        wkv_nat = sb("wkv_nat", [128, NG, 64])
        if kind == "s":
            WsgTs = sb("WsgTs", [128, 4, 128], BF16)
            sgub_s = sb("sgub_s", [128, 4])
            WukTg = sb("WukTg", [64, 4, 128], BF16)
            qabsT = sb("qabsT", [128, 4, 128], BF16)
            qpeT = sb("qpeT", [32, 4, 128], BF16)
            OT = sb("OT", [128, 4, 128], BF16)
            maskS_bf = sb("maskS_bf", [128, 512], BF16)
            pt_i = sb("pt_i", [128, 16 * NPG], I32)
            idx_i = pt_i
            GP = min(4, NPG)
            pg_c = [sb("pg_c%d" % i, [128, GP, 128]) for i in range(2)]
            pg_k = [sb("pg_k%d" % i, [128, GP, 32]) for i in range(2)]
            caug = sb("caug", [128, GP, 129], BF16)
            kpb = sb("kpb", [128, GP, 32], BF16)
            ckvTp = sb("ckvTp", [128, GP * 128], BF16)
            kpeTp = sb("kpeTp", [32, GP * 128], BF16)
            ssq = sb("ssq", [128, 16])
            rsk = sb("rsk", [128, 16])
            sc = sb("sc", [128, GP, 32])
            PTp = sb("PTp", [128, GP, 32], BF16)
            accs = sb("accs", [32, 129])
            ol_bf = sb("ol_bf", [32, 128], BF16)
            rls = sb("rls", [32, 1])
            xin_s = sb("xin_s", [128, 2, 16, 38])
            stc = sb("stc", [120, 256])
            glu_c = sb("glu_c", [128, 2, 128])
            ctr_s = sb("ctr_s", [128, 256])
            PDs = sb("PDs", [128, 8, 16, 9])
            sst = sb("sst", [16, 1024])
            lastc_s = sb("lastc_s", [128, 8, 16])
            lastT_s = sb("lastT_s", [16, 1024])
            Sst_s = sb("Sst_s", [128, 2, 16, 64])
            BQ = 2
            T1s = sb("T1s", [128, 2, BQ, 64])
            vbc_s = sb("vbc_s", [128, 2, BQ * 8, 64])

        def do_rsqrt(dstb, dst, srcb, src, scale, eps):
            S.op("dve", lambda e: e.tensor_scalar(out=dst, in0=src, scalar1=scale, scalar2=eps, op0=ALU.mult, op1=ALU.add), R=[srcb], W=[dstb])
            S.op("act", lambda e: e.sqrt(out=dst, in_=dst), R=[dstb], W=[dstb])
            S.op("dve", lambda e: e.reciprocal(out=dst, in_=dst), R=[dstb], W=[dstb])

        def rope(srcb, src4, dstb, dst4, cos, sin, nh):
            cb = bc(cos.unsqueeze(1), [128, nh, 16])
            sn = bc(sin.unsqueeze(1), [128, nh, 16])
            x1 = src4[:, :, 0:16]
            x2 = src4[:, :, 16:32]
            a1 = tr1[:, 0:nh, :]
            a2 = tr2[:, 0:nh, :]
            S.op("dve", lambda e: e.tensor_tensor(out=a1, in0=x1, in1=cb, op=ALU.mult), R=[srcb, cst], W=[tr1])
            S.op("dve", lambda e: e.tensor_tensor(out=a2, in0=x2, in1=sn, op=ALU.mult), R=[srcb, cst], W=[tr2])
            S.op("dve", lambda e: e.tensor_tensor(out=dst4[:, :, 0:16], in0=a1, in1=a2, op=ALU.subtract), R=[tr1, tr2], W=[dstb])
            S.op("dve", lambda e: e.tensor_tensor(out=a1, in0=x1, in1=sn, op=ALU.mult), R=[srcb, cst, dstb], W=[tr1])
            S.op("dve", lambda e: e.tensor_tensor(out=a2, in0=x2, in1=cb, op=ALU.mult), R=[srcb, cst, dstb], W=[tr2])
            S.op("dve", lambda e: e.tensor_tensor(out=dst4[:, :, 16:32], in0=a1, in1=a2, op=ALU.add), R=[tr1, tr2], W=[dstb])

        def norm_T(xb, gname):
            S.op("act", lambda e: e.activation(out=junk[:], in_=xb[:], func=AF.Square), R=[xb], W=[junk])
            S.op("dve", lambda e: e.reduce_sum(out=ss[:, 0:1], in_=junk[:], axis=AX.X), R=[junk], W=[ss])
            do_rsqrt(rstd, rstd[:, 0:1], ss, ss[:, 0:1], 1.0 / D, EPS)
            S.op("dve", lambda e: e.scalar_tensor_tensor(out=h_bf[:], in0=xb[:], scalar=rstd[:, 0:1], in1=VB(gname),
                                                         op0=ALU.mult, op1=ALU.mult), R=[xb, rstd, vb], W=[h_bf])
            for kc in range(KC):
                S.op("pe", lambda e, kc=kc: e.transpose(out=PT[:, kc * 128:(kc + 1) * 128], in_=h_bf[:, kc * 128:(kc + 1) * 128],
                                                        identity=ident_bf[:]), R=[h_bf, ident_bf], W=[PT])
            S.op("act", lambda e: e.copy(out=hT[:].rearrange("p k t -> p (k t)"), in_=PT[:]), R=[PT], W=[hT])

        S.mark('---------------- parameter loads')
        S.dma("sp", vb[:], bc(vb_d[l:l + 1, 0:NB1], [128, NB1]), W=[vb], track="par")
        S.dma("sp", vc[:], vc_d[l], W=[vc], track="par")
        wl = w_in[l].rearrange("(kc p) n -> p kc n", p=128)
        S.dma("pool", Win_tm[:, :, 0:352], wl[:, :, 0:352], W=[Win_tm], track="w1")
        S.dma("pool", Win_tm[:, :, 352:864], wl[:, :, 864:1376], W=[Win_tm], track="w1")
        S.dma("pool", Win_fm[:, :, 0:512], wl[:, :, 352:864], W=[Win_fm], track="w1")
        S.dma("pool", Win_fm[:, :, 512:1536], wl[:, :, 1376:2400], W=[Win_fm], track="w1")
        S.dma("pool", Wuq[:, 0, :], w_uq[l, 0:128, :], W=[Wuq], track="w1")
        S.dma("pool", Wuq[0:64, 1, :], w_uq[l, 128:192, :], W=[Wuq], track="w1")
        S.dma("pool", Wuk[:], w_uk[l], W=[Wuk], track="w1")
        S.dma("pool", Wuv[:], w_uv[l], W=[Wuv], track="w1")
        S.dma("pool", Wpw[:], conv_pw[l].rearrange("(h p) n -> p h n", p=128), W=[Wpw], track="w1")
        S.dma("pool", Wsg_raw[:], sgu_w[l].rearrange("h i j -> i h j"), W=[Wsg_raw], track="w1")
        S.op("pool", lambda e: e.memset(W2w[:], 0.0), W=[W2w])
        S.op("pool", lambda e: e.memset(W2a[:], 0.0), W=[W2a])
        S.dma("sp", W2w[0:64, :], rw_w2[l], W=[W2w], track="par2")
        S.dma("sp", W2a[64:128, :], rw_a2[l], W=[W2a], track="par2")
        S.dma("sp", G2[:], rw_g2[l], W=[G2], track="par")
        if kind == "p":
            S.dma("pool", Wout[:], w_out[l].rearrange("(kc p) n -> p kc n", p=128), W=[Wout], track="w1")
        o1, n1 = VB_OFF["mla_gq_nope"]
        S.op("act", lambda e: e.mul(out=vb[:, o1:o1 + 96], in_=vb[:, o1:o1 + 96], mul=MLA_SCALE), R=[vb], W=[vb])
        S.op("dve", lambda e: e.tensor_scalar(out=omka[:], in0=VC("rw_ka"), scalar1=-1.0, scalar2=1.0, op0=ALU.mult, op1=ALU.add),
             R=[vc], W=[omka])
        for h in range(4):
            S.op("pe", lambda e, h=h: e.transpose(out=PT[:, h * 128:(h + 1) * 128], in_=Wsg_raw[:, h, :], identity=ident_bf[:]),
                 R=[Wsg_raw, ident_bf], W=[PT])
        S.op("dve", lambda e: e.tensor_tensor(out=WsgT[:], in0=PT[:, 0:512].rearrange("p (h i) -> p h i", h=4),
                                              in1=bc(mask_bf[:].unsqueeze(1), [128, 4, 128]), op=ALU.mult),
             R=[PT, mask_bf], W=[WsgT])
        S.op("pool", lambda e: e.memset(Sst[:], 0.0), W=[Sst])
        for s in range(NSEQ):
            S.op("pool", lambda e, s=s: e.memset(xin[s][:], 0.0), W=[xin[s]])
            S.op("pool", lambda e, s=s: e.memset(PD[s][:], 0.0), W=[PD[s]])
            S.op("pool", lambda e, s=s: e.memset(Vaug[s][:], 1.0), W=[Vaug[s]])


        def sample_prep():
            S.mark("sample_prep")
            S.op("pool", lambda e: e.memset(WsgTs[:], 0.0), W=[WsgTs])
            for b in range(16):
                S.dma("sp", WsgTs[b * 8:(b + 1) * 8, :, b * 8:(b + 1) * 8], WsgT[0:8, :, 0:8], R=[WsgT], W=[WsgTs], track="wsg")
                S.dma("sp", sgub_s[b * 8:(b + 1) * 8, :], VC("sgu_b")[0:8, :], R=[vc], W=[sgub_s], track="wsg2")
            for h in range(4):
                S.op("pe", lambda e, h=h: e.transpose(out=PT[0:64, h * 128:(h + 1) * 128], in_=Wuk[:, h * 64:(h + 1) * 64], identity=ident_bf[:]),
                     R=[Wuk, ident_bf], W=[PT])
            S.op("dve", lambda e: e.tensor_scalar(out=WukTg[:].rearrange("p h l -> p (h l)"), in0=PT[0:64, 0:512], scalar1=VC("gk_col")[0:64, 0:1],
                                                  scalar2=None, op0=ALU.mult), R=[PT, vc], W=[WukTg])
            S.op("dve", lambda e: e.tensor_copy(out=maskS_bf[:], in_=C("maskS")), R=[cst], W=[maskS_bf])
            S.dma("sp", pt_i[:], bc(page_table.rearrange("b j -> (b j)").unsqueeze(0), [128, 16 * NPG]), W=[pt_i], track="par")
            ptf = pt_i[:].bitcast(F32)
            S.op("dve", lambda e: e.tensor_copy(out=ptf, in_=pt_i[:]), R=[pt_i], W=[pt_i])
            S.op("dve", lambda e: e.tensor_scalar(out=ptf, in0=ptf, scalar1=float(L * 128), scalar2=C("pidx")[:, 0:1], op0=ALU.mult, op1=ALU.add),
                 R=[pt_i, cst], W=[pt_i])
            S.op("dve", lambda e: e.tensor_scalar(out=ptf, in0=ptf, scalar1=float(l * 128), scalar2=None, op0=ALU.add), R=[pt_i], W=[pt_i])
            S.op("dve", lambda e: e.tensor_copy(out=pt_i[:], in_=ptf), R=[pt_i], W=[pt_i])
            for g4 in range(4):
                S.dma("sp", stc[:], state_conv[l, 4 * g4:4 * g4 + 4].rearrange("b w c -> (b w) c"), W=[stc], track="stc")
                for hf in range(2):
                    S.op("pe", lambda e, hf=hf: e.transpose(out=PS_[:, hf * 128:hf * 128 + 120], in_=stc[:, hf * 128:(hf + 1) * 128],
                                                            identity=C("ident")[0:120, 0:120]), R=[stc, cst], W=[PS_])
                for hf in range(2):
                    S.op("act", lambda e, hf=hf, g4=g4: e.copy(out=xin_s[:, hf, 4 * g4:4 * g4 + 4, 0:30],
                                                              in_=PS_[:, hf * 128:hf * 128 + 120].rearrange("p (b w) -> p b w", w=30)), R=[PS_], W=[xin_s])
            S.dma("sp", sst[:], state_shift[l], W=[sst], track="stc")
            for c in range(8):
                S.op("pe", lambda e, c=c: e.transpose(out=PS_[:, c * 16:(c + 1) * 16], in_=sst[:, c * 128:(c + 1) * 128], identity=C("ident")[0:16, 0:16]),
                     R=[sst, cst], W=[PS_])
            S.op("act", lambda e: e.copy(out=PDs[:, :, :, 0], in_=PS_[:, 0:128].rearrange("p (c b) -> p c b", c=8)), R=[PS_], W=[PDs])
            for p2 in range(2):
                for cc in range(2):
                    S.dma("sp", vbc_s[p2 * 64:(p2 + 1) * 64, cc, :, :], state_wkv[l, :, cc * 2 + p2].rearrange("b i j -> i b j"), W=[vbc_s], track="stw")
            for cc in range(2):
                for bh in range(2):
                    for bb in range(8):
                        for p2 in range(2):
                            pp = slice(p2 * 64, (p2 + 1) * 64)
                            S.op("pe", lambda e, cc=cc, bh=bh, bb=bb, pp=pp: e.matmul(PS_[pp, bb * 64:(bb + 1) * 64], lhsT=vbc_s[pp, cc, bh * 8 + bb, :],
                                                                                      rhs=C("ident")[pp, pp], start=True, stop=True), R=[vbc_s, cst], W=[PS_])
                    S.op("act", lambda e, cc=cc, bh=bh: e.copy(out=Sst_s[:, cc, bh * 8:(bh + 1) * 8, :], in_=PS_[:].rearrange("p (b i) -> p b i", i=64)),
                         R=[PS_], W=[Sst_s])

        def sample_attention():
            S.mark("sample_attention")
            ckv_rows = cache_ckv_f
            kpe_rows = cache_kpe_f
            for h in range(4):
                S.op("pe", lambda e, h=h: e.matmul(PF[0][:, h * 128:(h + 1) * 128], lhsT=WukTg[0:64, h, :], rhs=qT[0:64, h, :], start=True, stop=True),
                     R=[WukTg, qT], W=[PF[0]])
            S.op("act", lambda e: e.copy(out=qabsT[:].rearrange("p h t -> p (h t)"), in_=PF[0][:]), R=[PF[0]], W=[qabsT])
            for h in range(4):
                S.op("pe", lambda e, h=h: e.transpose(out=PT[0:32, h * 128:(h + 1) * 128], in_=qfull[:, h, 64:96], identity=ident_bf[:]),
                     R=[qfull, ident_bf], W=[PT])
            S.op("act", lambda e: e.copy(out=qpeT[:].rearrange("p h t -> p (h t)"), in_=PT[0:32, 0:512]), R=[PT], W=[qpeT])
            S.op("pool", lambda e: e.memset(caug[:], 1.0), W=[caug])

            def group(b, npg, cb, cap, kb, kap, masked, first, last):
                S.op("act", lambda e: e.copy(out=caug[:, 0:npg, 0:128], in_=cap), R=[cb], W=[caug])
                S.op("dve", lambda e: e.tensor_copy(out=kpb[:, 0:npg, :], in_=kap), R=[kb], W=[kpb])
                for pg in range(npg):
                    S.op("pe", lambda e, pg=pg: e.transpose(out=PT[:, pg * 128:(pg + 1) * 128], in_=caug[:, pg, 0:128], identity=ident_bf[:]),
                         R=[caug, ident_bf], W=[PT])
                    S.op("pe", lambda e, pg=pg: e.transpose(out=PT[0:32, 512 + pg * 128:512 + (pg + 1) * 128], in_=kpb[:, pg, :], identity=ident_bf[:]),
                         R=[kpb, ident_bf], W=[PT])
                S.op("act", lambda e: e.copy(out=ckvTp[:, 0:npg * 128], in_=PT[:, 0:npg * 128]), R=[PT], W=[ckvTp])
                S.op("act", lambda e: e.copy(out=kpeTp[:, 0:npg * 128], in_=PT[0:32, 512:512 + npg * 128]), R=[PT], W=[kpeTp])
                for pg in range(npg):
                    S.op("pe", lambda e, pg=pg: e.matmul(PF[pg // 2][:, (pg % 2) * 256:(pg % 2 + 1) * 256], lhsT=ckvTp[:, pg * 128:(pg + 1) * 128], rhs=Wuk[:],
                                                         start=True, stop=True), R=[ckvTp, Wuk], W=[PF[pg // 2]])
                for hb in range((npg + 1) // 2):
                    w = min(2, npg - 2 * hb) * 256
                    S.op("act", lambda e, hb=hb, w=w: e.activation(out=junk[:, hb * 512:hb * 512 + w], in_=PF[hb][:, 0:w], func=AF.Square), R=[PF[hb]], W=[junk])
                S.op("dve", lambda e: e.reduce_sum(out=ssq[:, 0:npg * 4], in_=junk[:, 0:npg * 256].rearrange("p (g d) -> p g d", d=64), axis=AX.X),
                     R=[junk], W=[ssq])
                do_rsqrt(rsk, rsk[:, 0:npg * 4], ssq, ssq[:, 0:npg * 4], 1.0 / 64, EPS)
                for pg in range(npg):
                    S.op("pe", lambda e, pg=pg: e.matmul(PS_[:, pg * 32:(pg + 1) * 32].rearrange("p (h q) -> p h q", h=4), lhsT=ckvTp[:, pg * 128:(pg + 1) * 128],
                                                         rhs=qabsT[:, :, b * 8:(b + 1) * 8], start=True, stop=True), R=[ckvTp, qabsT], W=[PS_])
                    S.op("pe", lambda e, pg=pg: e.matmul(PS_[:, 128 + pg * 32:128 + (pg + 1) * 32].rearrange("p (h q) -> p h q", h=4),
                                                         lhsT=kpeTp[0:32, pg * 128:(pg + 1) * 128], rhs=qpeT[0:32, :, b * 8:(b + 1) * 8], start=True, stop=True),
                         R=[kpeTp, qpeT], W=[PS_])
                scv = sc[:, 0:npg, :].rearrange("p g (h q) -> p (g h) q", h=4)
                S.op("dve", lambda e: e.tensor_tensor(out=scv, in0=PS_[:, 0:npg * 32].rearrange("p (g q) -> p g q", q=8),
                                                      in1=bc(rsk[:, 0:npg * 4].unsqueeze(2), [128, npg * 4, 8]), op=ALU.mult), R=[PS_, rsk], W=[sc])
                S.op("dve", lambda e: e.tensor_tensor(out=sc[:, 0:npg, :], in0=sc[:, 0:npg, :],
                                                      in1=PS_[:, 128:128 + npg * 32].rearrange("p (g q) -> p g q", q=32), op=ALU.add), R=[PS_, sc], W=[sc])
                S.op("act", lambda e: e.activation(out=PTp[:, 0:npg, :], in_=sc[:, 0:npg, :], func=AF.Exp), R=[sc], W=[PTp])
                if masked:
                    S.op("dve", lambda e: e.tensor_tensor(out=PTp[:, 0, :], in0=PTp[:, 0, :], in1=maskS_bf[:, b * 32:(b + 1) * 32], op=ALU.mult),
                         R=[PTp, maskS_bf], W=[PTp])
                for pg in range(npg):
                    S.op("pe", lambda e, pg=pg: e.matmul(PY[0:32, 0:129], lhsT=PTp[:, pg, :], rhs=caug[:, pg, :], start=(first and pg == 0),
                                                         stop=(last and pg == npg - 1)), R=[PTp, caug], W=[PY])

            gi = 0
            ngr = NPG // GP
            for b in range(16):
                group(b, 1, ckv_sb, ckv_sb[:].unsqueeze(1), kpe_sb, kpe_sb[:].unsqueeze(1), True, True, ngr == 0)
                for gq in range(ngr):
                    bufi = gi % 2
                    gi += 1
                    for pg in range(GP):
                        col = b * NPG + gq * GP + pg
                        S.dmaf("pool", lambda e, bufi=bufi, pg=pg, col=col: e.indirect_dma_start(
                            out=pg_c[bufi][:, pg, :], out_offset=None, in_=ckv_rows,
                            in_offset=bass.IndirectOffsetOnAxis(ap=idx_i[:, col:col + 1], axis=0)), R=[idx_i, cfull], W=[pg_c[bufi]], track="pgc%d" % bufi)
                        S.dmaf("pool", lambda e, bufi=bufi, pg=pg, col=col: e.indirect_dma_start(
                            out=pg_k[bufi][:, pg, :], out_offset=None, in_=kpe_rows,
                            in_offset=bass.IndirectOffsetOnAxis(ap=idx_i[:, col:col + 1], axis=0)), R=[idx_i, cfull], W=[pg_k[bufi]], track="pgk%d" % bufi)
                    group(b, GP, pg_c[bufi], pg_c[bufi][:], pg_k[bufi], pg_k[bufi][:], False, False, gq == ngr - 1)
                S.op("act", lambda e: e.copy(out=accs[:], in_=PY[0:32, 0:129]), R=[PY], W=[accs])
                S.op("dve", lambda e: e.reciprocal(out=rls[:], in_=accs[:, 128:129]), R=[accs], W=[rls])
                S.op("dve", lambda e: e.tensor_scalar(out=ol_bf[:], in0=accs[:, 0:128], scalar1=rls[:, 0:1], scalar2=None, op0=ALU.mult), R=[accs, rls], W=[ol_bf])
                S.op("pe", lambda e: e.transpose(out=PT[:, 0:32], in_=ol_bf[:], identity=ident_bf[0:32, 0:32]), R=[ol_bf, ident_bf], W=[PT])
                S.op("act", lambda e, b=b: e.copy(out=OT[:, :, b * 8:(b + 1) * 8], in_=PT[:, 0:32].rearrange("p (h q) -> p h q", h=4)), R=[PT], W=[OT])
            for h in range(4):
                S.op("pe", lambda e, h=h: e.matmul(PA[:, h * 64:(h + 1) * 64], lhsT=OT[:, h, :], rhs=Wuv[:, h * 64:(h + 1) * 64], start=True, stop=True),
                     R=[OT, Wuv], W=[PA])
            S.op("act", lambda e: e.copy(out=omix[0][:, 0:256], in_=PA[:, 0:256]), R=[PA], W=[omix[0]])

        def sample_scan():
            S.mark("sample_scan")
            vtok = S.track("vscr0")[2]
            kka4 = kka[0][:].rearrange("p c (b t) -> p c b t", t=8)
            dec4 = dec[0][:].rearrange("p c (b t) -> p c b t", t=8)
            for bq in range(16 // BQ):
                b0 = bq * BQ
                for p2 in range(2):
                    for cc in range(2):
                        hh = cc * 2 + p2
                        src = bc(v_scr[0, b0 * 8:(b0 + BQ) * 8, hh * 64:(hh + 1) * 64].unsqueeze(0), [64, BQ * 8, 64])
                        if vtok is not None and S.n < S.limit:
                            S._wait("sp", vtok)
                        S.dma("sp", vbc_s[p2 * 64:(p2 + 1) * 64, cc, :, :], src, W=[vbc_s], track="vbc")
                S.op("pool", lambda e, b0=b0: e.tensor_tensor(out=vbc_s[:], in0=vbc_s[:], in1=bc(kfin[0][:, :, b0 * 8:(b0 + BQ) * 8].unsqueeze(3), [128, 2, BQ * 8, 64]),
                                                              op=ALU.mult), R=[vbc_s, kfin[0]], W=[vbc_s])
                KV5 = vbc_s[:].rearrange("p c (b t) i -> p c b t i", t=8)
                Sv = Sst_s[:, :, b0:b0 + BQ, :]
                for t in range(8):
                    for cc in range(2):
                        for bb in range(BQ):
                            tok = (b0 + bb) * 8 + t
                            for p2 in range(2):
                                pp = slice(p2 * 64, (p2 + 1) * 64)
                                S.op("pe", lambda e, cc=cc, bb=bb, tok=tok, pp=pp, b0=b0: e.matmul(
                                    PS_[pp, (cc * BQ + bb) * 64:(cc * BQ + bb + 1) * 64], lhsT=bc(nkk[0][pp, cc, tok:tok + 1], [64, 64]),
                                    rhs=Sst_s[pp, cc, b0 + bb, :], start=True, stop=True), R=[nkk[0], Sst_s], W=[PS_])
                    sa = PS_[:, 0:2 * BQ * 64].rearrange("p (c b i) -> p c b i", c=2, b=BQ)
                    S.op("dve", lambda e, t=t, b0=b0, sa=sa: e.tensor_tensor(out=T1s[:], in0=sa, in1=bc(kka4[:, :, b0:b0 + BQ, t].unsqueeze(3), [128, 2, BQ, 64]),
                                                                             op=ALU.mult), R=[PS_, kka[0]], W=[T1s])
                    S.op("dve", lambda e, t=t, b0=b0, Sv=Sv: e.tensor_tensor(out=Sv, in0=Sv, in1=bc(dec4[:, :, b0:b0 + BQ, t].unsqueeze(3), [128, 2, BQ, 64]),
                                                                             op=ALU.mult), R=[Sst_s, dec[0]], W=[Sst_s])
                    S.op("dve", lambda e, Sv=Sv: e.tensor_tensor(out=Sv, in0=Sv, in1=T1s[:], op=ALU.add), R=[Sst_s, T1s], W=[Sst_s])
                    S.op("dve", lambda e, Sv=Sv, t=t, KV5=KV5: e.tensor_tensor(out=Sv, in0=Sv, in1=KV5[:, :, :, t, :], op=ALU.add), R=[Sst_s, vbc_s], W=[Sst_s])
                    for cc in range(2):
                        for bb in range(BQ):
                            tok = (b0 + bb) * 8 + t
                            for p2 in range(2):
                                pp = slice(p2 * 64, (p2 + 1) * 64)
                                S.op("pe", lambda e, cc=cc, bb=bb, tok=tok, pp=pp, b0=b0: e.matmul(
                                    PY[pp, cc * 128 + tok:cc * 128 + tok + 1], lhsT=Sst_s[pp, cc, b0 + bb, :], rhs=xs[0][pp, cc, tok:tok + 1],
                                    start=True, stop=True), R=[Sst_s, xs[0]], W=[PY])
            for cc in range(2):
                for bh in range(2):
                    for bb in range(8):
                        for p2 in range(2):
                            pp = slice(p2 * 64, (p2 + 1) * 64)
                            S.op("pe", lambda e, cc=cc, bh=bh, bb=bb, pp=pp: e.matmul(PS_[pp, bb * 64:(bb + 1) * 64], lhsT=Sst_s[pp, cc, bh * 8 + bb, :],
                                                                                      rhs=C("ident")[pp, pp], start=True, stop=True), R=[Sst_s, cst], W=[PS_])
                    S.op("act", lambda e, cc=cc, bh=bh: e.copy(out=vbc_s[:, cc, bh * 8:(bh + 1) * 8, :], in_=PS_[:].rearrange("p (b i) -> p b i", i=64)),
                         R=[PS_], W=[vbc_s])
            for p2 in range(2):
                for cc in range(2):
                    S.dma("sp", wkv_s_out[l, :, cc * 2 + p2].rearrange("b i j -> i b j"), vbc_s[p2 * 64:(p2 + 1) * 64, cc, :, :], R=[vbc_s], track="o_misc")

        xsrc = x_prompt if l == 0 else xres
        xdst = xres
        if kind == "s":
            sample_prep()

        for n in range(NTk):
            for s in range(NSEQ):
                xb = x_t[s]
                if kind == "p":
                    S.dma("sp", xb[:], xsrc[s0 + s, n * 128:(n + 1) * 128, :], W=[xb], track="x%d" % s)
                else:
                    S.dma("sp", xb[:], (x_sample if l == 0 else xres_s), W=[xb], track="x%d" % s)
                norm_T(xb, "norm_mix")
                for (pb, c0, nn) in ((PA, 0, 352), (PC, 352, 512)):
                    for kc in range(KC):
                        S.op("pe", lambda e, pb=pb, c0=c0, nn=nn, kc=kc: e.matmul(
                            pb[:, 0:nn], lhsT=hT[:, kc, :], rhs=Win_tm[:, kc, c0:c0 + nn], start=(kc == 0), stop=(kc == KC - 1)),
                            R=[hT, Win_tm], W=[pb])
                for ch in range(12):
                    pb = PF[ch // 4]
                    for kc in range(KC):
                        S.op("pe", lambda e, pb=pb, ch=ch, kc=kc: e.matmul(
                            pb[:, (ch % 4) * 128:(ch % 4 + 1) * 128], lhsT=Win_fm[:, kc, ch * 128:(ch + 1) * 128], rhs=hT[:, kc, :],
                            start=(kc == 0), stop=(kc == KC - 1)), R=[hT, Win_fm], W=[pb])
                S.mark('================= A: MLA')
                if kind == "s":
                    S.dma("pool", Win_fm[:, :, 0:D], w_out[l].rearrange("(kc p) n -> p kc n", p=128), R=[], W=[Win_fm], track="w2")
                S.op("act", lambda e: e.copy(out=pa[:], in_=PA[:, 0:352]), R=[PA], W=[pa])
                S.op("act", lambda e: e.activation(out=junk[:, 0:352], in_=pa[:], func=AF.Square), R=[pa], W=[junk])
                S.op("dve", lambda e: e.reduce_sum(out=ss[:, 1:2], in_=junk[:, 0:192], axis=AX.X), R=[junk], W=[ss])
                S.op("dve", lambda e: e.reduce_sum(out=ss[:, 2:3], in_=junk[:, 192:320], axis=AX.X), R=[junk], W=[ss])
                S.op("dve", lambda e: e.reduce_sum(out=ss[:, 3:4], in_=junk[:, 320:352], axis=AX.X), R=[junk], W=[ss])
                do_rsqrt(rstd, rstd[:, 1:2], ss, ss[:, 1:2], 1.0 / 192, EPS)
                do_rsqrt(rstd, rstd[:, 2:3], ss, ss[:, 2:3], 1.0 / 128, EPS)
                do_rsqrt(rstd, rstd[:, 3:4], ss, ss[:, 3:4], 1.0 / 32, EPS)
                S.op("dve", lambda e: e.scalar_tensor_tensor(out=cqn[:], in0=pa[:, 0:192], scalar=rstd[:, 1:2], in1=VB("mla_q_norm"),
                                                             op0=ALU.mult, op1=ALU.mult), R=[pa, rstd, vb], W=[cqn])
                S.op("dve", lambda e: e.scalar_tensor_tensor(out=ckv_sb[:], in0=pa[:, 192:320], scalar=rstd[:, 2:3], in1=VB("mla_kv_norm"),
                                                             op0=ALU.mult, op1=ALU.mult), R=[pa, rstd, vb], W=[ckv_sb])
                S.dma("sp", (ckv_prompt[s0 + s, n, l] if kind == "p" else ckv_s_out[l]), ckv_sb[:], R=[ckv_sb], track="o_ckv")
                S.op("act", lambda e: e.copy(out=ckv_bf[:], in_=ckv_sb[:]), R=[ckv_sb], W=[ckv_bf])
                S.op("dve", lambda e: e.scalar_tensor_tensor(out=tmpr[:, 0, :], in0=pa[:, 320:352], scalar=rstd[:, 3:4], in1=VB("mla_gk_rope"),
                                                             op0=ALU.mult, op1=ALU.mult), R=[pa, rstd, vb], W=[tmpr])
                cosn = C("cosp")[:, n * 16:(n + 1) * 16] if kind == "p" else C("coss")
                sinn = C("sinp")[:, n * 16:(n + 1) * 16] if kind == "p" else C("sins")
                rope(tmpr, tmpr[:, 0:1, :], kpe_sb, kpe_sb[:].unsqueeze(1), cosn, sinn, 1)
                S.dma("sp", (kpe_prompt[s0 + s, n, l] if kind == "p" else kpe_s_out[l]), kpe_sb[:], R=[kpe_sb], track="o_kpe")
                S.op("dve", lambda e: e.tensor_copy(out=kfull[:, :, 64:96], in_=bc(kpe_sb[:].unsqueeze(1), [128, 4, 32])),
                     R=[kpe_sb], W=[kfull])
                S.op("pe", lambda e: e.transpose(out=PT[:, 0:128], in_=cqn[:, 0:128], identity=ident_bf[:]), R=[cqn, ident_bf], W=[PT])
                S.op("pe", lambda e: e.transpose(out=PT[0:64, 128:256], in_=cqn[:, 128:192], identity=ident_bf[:]), R=[cqn, ident_bf], W=[PT])
                S.op("pe", lambda e: e.transpose(out=PT[:, 256:384], in_=ckv_bf[:], identity=ident_bf[:]), R=[ckv_bf, ident_bf], W=[PT])
                S.op("act", lambda e: e.copy(out=cqT[:, 0, :], in_=PT[:, 0:128]), R=[PT], W=[cqT])
                S.op("act", lambda e: e.copy(out=cqT[0:64, 1, :], in_=PT[0:64, 128:256]), R=[PT], W=[cqT])
                S.op("act", lambda e: e.copy(out=ckvT[:], in_=PT[:, 256:384]), R=[PT], W=[ckvT])
                S.op("pe", lambda e: e.matmul(PA[:, 0:384], lhsT=cqT[:, 0, :], rhs=Wuq[:, 0, :], start=True, stop=False), R=[cqT, Wuq], W=[PA])
                S.op("pe", lambda e: e.matmul(PA[:, 0:384], lhsT=cqT[0:64, 1, :], rhs=Wuq[0:64, 1, :], start=False, stop=True), R=[cqT, Wuq], W=[PA])
                qv = qsb[:].rearrange("p h d -> p (h d)")
                S.op("act", lambda e: e.copy(out=qv, in_=PA[:, 0:384]), R=[PA], W=[qsb])
                S.op("act", lambda e: e.activation(out=junk[:, 0:384], in_=qv, func=AF.Square), R=[qsb], W=[junk])
                j4 = junk[:, 0:384].rearrange("p (h d) -> p h d", h=4)
                S.op("dve", lambda e: e.reduce_sum(out=ss[:, 4:8], in_=j4[:, :, 0:64], axis=AX.X), R=[junk], W=[ss])
                do_rsqrt(rstd, rstd[:, 4:8], ss, ss[:, 4:8], 1.0 / 64, EPS)
                S.op("dve", lambda e: e.tensor_tensor(out=qsb[:, :, 0:64], in0=qsb[:, :, 0:64], in1=bc(rstd[:, 4:8].unsqueeze(2), [128, 4, 64]),
                                                      op=ALU.mult), R=[qsb, rstd], W=[qsb])
                S.op("dve", lambda e: e.tensor_tensor(out=qfull[:, :, 0:64], in0=qsb[:, :, 0:64],
                                                      in1=bc(VB("mla_gq_nope").unsqueeze(1), [128, 4, 64]), op=ALU.mult),
                     R=[qsb, vb], W=[qfull])
                S.op("dve", lambda e: e.reduce_sum(out=ss[:, 4:8], in_=j4[:, :, 64:96], axis=AX.X), R=[junk], W=[ss])
                do_rsqrt(rstd, rstd[:, 4:8], ss, ss[:, 4:8], 1.0 / 32, EPS)
                S.op("dve", lambda e: e.tensor_tensor(out=tmpr[:], in0=qsb[:, :, 64:96], in1=bc(rstd[:, 4:8].unsqueeze(2), [128, 4, 32]),
                                                      op=ALU.mult), R=[qsb, rstd], W=[tmpr])
                S.op("dve", lambda e: e.tensor_tensor(out=tmpr[:], in0=tmpr[:], in1=bc(VB("mla_gq_rope").unsqueeze(1), [128, 4, 32]),
                                                      op=ALU.mult), R=[tmpr, vb], W=[tmpr])
                rope(tmpr, tmpr[:], qfull, qfull[:, :, 64:96], cosn, sinn, 4)
                S.op("pe", lambda e: e.matmul(PA[:, 0:256], lhsT=ckvT[:], rhs=Wuk[:], start=True, stop=True), R=[ckvT, Wuk, qsb], W=[PA])
                S.op("pe", lambda e: e.matmul(PA[:, 256:512], lhsT=ckvT[:], rhs=Wuv[:], start=True, stop=True), R=[ckvT, Wuv], W=[PA])
                S.op("act", lambda e: e.activation(out=junk[:, 0:256], in_=PA[:, 0:256], func=AF.Square), R=[PA], W=[junk])
                S.op("dve", lambda e: e.reduce_sum(out=ss[:, 4:8], in_=junk[:, 0:256].rearrange("p (h d) -> p h d", h=4), axis=AX.X),
                     R=[junk], W=[ss])
                do_rsqrt(rstd, rstd[:, 4:8], ss, ss[:, 4:8], 1.0 / 64, EPS)
                S.op("dve", lambda e: e.tensor_tensor(out=qsb[:, :, 0:64], in0=PA[:, 0:256].rearrange("p (h d) -> p h d", h=4),
                                                      in1=bc(rstd[:, 4:8].unsqueeze(2), [128, 4, 64]), op=ALU.mult),
                     R=[PA, rstd, qfull], W=[qsb])
                S.op("dve", lambda e: e.tensor_tensor(out=kfull[:, :, 0:64], in0=qsb[:, :, 0:64],
                                                      in1=bc(VB("mla_gk_nope").unsqueeze(1), [128, 4, 64]), op=ALU.mult),
                     R=[qsb, vb], W=[kfull])
                S.op("act", lambda e, s=s, n=n: e.copy(out=Vaug[s][:, n, :, 0:64], in_=PA[:, 256:512].rearrange("p (h d) -> p h d", h=4)),
                     R=[PA], W=[Vaug[s]])
                for h in range(4):
                    S.op("pe", lambda e, h=h: e.transpose(out=PT[0:96, h * 128:(h + 1) * 128], in_=qfull[:, h, :], identity=ident_bf[:]),
                         R=[qfull, ident_bf], W=[PT])
                    S.op("pe", lambda e, h=h: e.transpose(out=PT[0:96, 512 + h * 128:512 + (h + 1) * 128], in_=kfull[:, h, :], identity=ident_bf[:]),
                         R=[kfull, ident_bf], W=[PT])
                S.op("act", lambda e: e.copy(out=qT[:].rearrange("p h t -> p (h t)"), in_=PT[0:96, 0:512]), R=[PT], W=[qT])
                S.op("act", lambda e, s=s, n=n: e.copy(out=kT[s][:, :, n * 128:(n + 1) * 128],
                                                      in_=PT[0:96, 512:1024].rearrange("p (h t) -> p h t", h=4)), R=[PT], W=[kT[s]])
                for kt in (range(n + 1) if kind == "p" else []):
                    for h in range(4):
                        S.op("pe", lambda e, h=h, kt=kt, s=s: e.matmul(PS_[:, h * 128:(h + 1) * 128], lhsT=kT[s][:, h, kt * 128:(kt + 1) * 128],
                                                                      rhs=qT[:, h, :], start=True, stop=True), R=[kT[s], qT], W=[PS_])
                    S.op("act", lambda e: e.activation(out=PTs[:], in_=PS_[:], func=AF.Exp), R=[PS_], W=[PTs])
                    if kt == n:
                        S.op("dve", lambda e: e.tensor_tensor(out=PTs[:].rearrange("p (h t) -> p h t", h=4),
                                                              in0=PTs[:].rearrange("p (h t) -> p h t", h=4),
                                                              in1=bc(mask_bf[:].unsqueeze(1), [128, 4, 128]), op=ALU.mult),
                             R=[PTs, mask_bf], W=[PTs])
                    for h in range(4):
                        S.op("pe", lambda e, h=h, kt=kt, s=s, n=n: e.matmul(PY[:, h * 65:(h + 1) * 65], lhsT=PTs[:, h * 128:(h + 1) * 128],
                                                                           rhs=Vaug[s][:, kt, h, :], start=(kt == 0 and h == 0), stop=(kt == n and h == 3)),
                             R=[PTs, Vaug[s]], W=[PY])
                py4 = PY[:, 0:260].rearrange("p (h d) -> p h d", h=4)
                if kind == "p":
                    S.op("dve", lambda e: e.reciprocal(out=rl[:], in_=py4[:, :, 64]), R=[PY], W=[rl])
                    S.op("dve", lambda e, s=s: e.tensor_tensor(out=omix[s][:, 0:256].rearrange("p (h d) -> p h d", h=4), in0=py4[:, :, 0:64],
                                                               in1=bc(rl[:].unsqueeze(2), [128, 4, 64]), op=ALU.mult), R=[PY, rl], W=[omix[s]])

                S.mark('================= C: gMLP')
                S.op("act", lambda e: e.copy(out=zc[:], in_=PC[:]), R=[PC], W=[zc])
                S.op("dve", lambda e: e.tensor_tensor(out=t5[:], in0=zc[:], in1=zc[:], op=ALU.mult), R=[zc], W=[t5])
                S.op("dve", lambda e: e.tensor_scalar(out=t5[:], in0=t5[:], scalar1=0.044715, scalar2=1.0, op0=ALU.mult, op1=ALU.add), R=[t5], W=[t5])
                S.op("dve", lambda e: e.tensor_tensor(out=t5[:], in0=t5[:], in1=zc[:], op=ALU.mult), R=[t5, zc], W=[t5])
                S.op("act", lambda e: e.activation(out=t5[:], in_=t5[:], func=AF.Sigmoid, scale=1.5957691216057308), R=[t5], W=[t5])
                S.op("dve", lambda e: e.tensor_tensor(out=zc[:], in0=zc[:], in1=t5[:], op=ALU.mult), R=[t5, zc], W=[zc])
                S.op("dve", lambda e: e.reduce_sum(out=ss[:, 0:1], in_=zc[:, 256:512], axis=AX.X), R=[zc], W=[ss])
                S.op("dve", lambda e: e.tensor_scalar(out=ss[:, 0:1], in0=ss[:, 0:1], scalar1=-1.0 / 256, scalar2=None, op0=ALU.mult), R=[ss], W=[ss])
                S.op("dve", lambda e: e.tensor_scalar(out=vn[:], in0=zc[:, 256:512], scalar1=ss[:, 0:1], scalar2=None, op0=ALU.add), R=[zc, ss], W=[vn])
                S.op("act", lambda e: e.activation(out=junk[:, 0:256], in_=vn[:], func=AF.Square), R=[vn], W=[junk])
                S.op("dve", lambda e: e.reduce_sum(out=ss[:, 1:2], in_=junk[:, 0:256], axis=AX.X), R=[junk], W=[ss])
                do_rsqrt(rstd, rstd[:, 1:2], ss, ss[:, 1:2], 1.0 / 256, LN_EPS)
                S.op("dve", lambda e: e.scalar_tensor_tensor(out=vn[:], in0=vn[:], scalar=rstd[:, 1:2], in1=VB("sgu_norm_g"),
                                                             op0=ALU.mult, op1=ALU.mult), R=[vn, rstd, vb], W=[vn])
                S.op("dve", lambda e: e.tensor_tensor(out=vn[:], in0=vn[:], in1=VB("sgu_norm_b"), op=ALU.add), R=[vn, vb], W=[vn])
                S.op("act", lambda e: e.copy(out=vn_bf[:], in_=vn[:]), R=[vn], W=[vn_bf])
                if kind == "s":
                    S.dma("sp", sgv_s_out[l], vn[:], R=[vn], track="o_misc")
                for h in range(4):
                    S.op("pe", lambda e, h=h: e.matmul(PC[:, h * 64:(h + 1) * 64], lhsT=(WsgT if kind == "p" else WsgTs)[:, h, :], rhs=vn_bf[:, h * 64:(h + 1) * 64],
                                                       start=True, stop=True), R=[WsgT, vn_bf, zc] + ([WsgTs] if kind == "s" else []), W=[PC])
                for h in range(4):
                    S.op("dve", lambda e, h=h, s=s: e.scalar_tensor_tensor(out=omix[s][:, 512 + h * 64:512 + (h + 1) * 64], in0=PC[:, h * 64:(h + 1) * 64],
                                                                          scalar=(VC("sgu_b") if kind == "p" else sgub_s)[:, h:h + 1], in1=zc[:, h * 64:(h + 1) * 64],
                                                                          op0=ALU.add, op1=ALU.mult), R=[PC, vc, zc] + ([sgub_s] if kind == "s" else []), W=[omix[s]])
                S.mark('================= B: conformer')
                S.op("act", lambda e: e.activation(out=sgm[:].rearrange("p h t -> p (h t)"), in_=PF[0][:, 256:512], func=AF.Sigmoid), R=[PF[0]], W=[sgm])
                cw = VC("conv_w").rearrange("p (h w) -> p h w", h=2)
                if kind == "p":
                    S.op("dve", lambda e, s=s: e.tensor_tensor(out=xin[s][:, :, 30:158], in0=PF[0][:, 0:256].rearrange("p (h t) -> p h t", h=2),
                                                               in1=sgm[:], op=ALU.mult), R=[PF[0], sgm], W=[xin[s]])
                    for hf in range(2):
                        eng = "dve"
                        S.op(eng, lambda e, hf=hf, s=s: e.tensor_scalar(out=cacc[:, hf, :], in0=xin[s][:, hf, 0:128], scalar1=cw[:, hf, 0:1],
                                                                        scalar2=VC("conv_b")[:, hf:hf + 1], op0=ALU.mult, op1=ALU.add),
                             R=[xin[s], vc], W=[cacc])
                        for w in range(1, CONV_W):
                            S.op(eng, lambda e, hf=hf, s=s, w=w: e.scalar_tensor_tensor(out=cacc[:, hf, :], in0=xin[s][:, hf, w:w + 128],
                                                                                       scalar=cw[:, hf, w:w + 1], in1=cacc[:, hf, :],
                                                                                       op0=ALU.mult, op1=ALU.add), R=[xin[s], vc, cacc], W=[cacc])
                else:
                    S.op("dve", lambda e: e.tensor_tensor(out=glu_c[:], in0=PF[0][:, 0:256].rearrange("p (h t) -> p h t", h=2),
                                                          in1=sgm[:], op=ALU.mult), R=[PF[0], sgm], W=[glu_c])
                    S.op("dve", lambda e: e.tensor_copy(out=xin_s[:, :, :, 30:38], in_=glu_c[:].rearrange("p h (b t) -> p h b t", t=8)),
                         R=[glu_c], W=[xin_s])
                    for hf in range(2):
                        c3 = cacc[:, hf, :].rearrange("p (b t) -> p b t", t=8)
                        S.op("dve", lambda e, hf=hf, c3=c3: e.tensor_scalar(out=c3, in0=xin_s[:, hf, :, 0:8], scalar1=cw[:, hf, 0:1],
                                                                          scalar2=VC("conv_b")[:, hf:hf + 1], op0=ALU.mult, op1=ALU.add),
                             R=[xin_s, vc], W=[cacc])
                        for w in range(1, CONV_W):
                            S.op("dve", lambda e, hf=hf, c3=c3, w=w: e.scalar_tensor_tensor(out=c3, in0=xin_s[:, hf, :, w:w + 8],
                                                                                          scalar=cw[:, hf, w:w + 1], in1=c3,
                                                                                          op0=ALU.mult, op1=ALU.add), R=[xin_s, vc, cacc], W=[cacc])
                    for hf in range(2):
                        S.op("pe", lambda e, hf=hf: e.transpose(out=PS_[:, hf * 128:(hf + 1) * 128], in_=glu_c[:, hf, :], identity=C("ident")),
                             R=[glu_c, cst], W=[PS_])
                    S.op("act", lambda e: e.copy(out=ctr_s[:], in_=PS_[:, 0:256]), R=[PS_], W=[ctr_s])
                    S.dma("sp", conv_new_out[l], ctr_s[:], R=[ctr_s], track="o_misc")
                    S.dma("sp", conv_old_out[l], state_conv[l, :, 8:30, :], track="o_misc2")
                if n == NT - 1 and kind == "p":
                    for hf in range(2):
                        S.op("pe", lambda e, hf=hf, s=s: e.transpose(out=PS_[0:30, hf * 128:(hf + 1) * 128], in_=xin[s][:, hf, 128:158],
                                                                    identity=C("ident")), R=[xin[s], cst], W=[PS_])
                    S.op("act", lambda e: e.copy(out=ctr[:], in_=PS_[0:30, 0:256]), R=[PS_], W=[ctr])
                    S.dma("sp", conv_prompt[l, s0 + s], ctr[:], R=[ctr], track="o_misc")
                if kind == "p":
                    S.op("dve", lambda e, s=s: e.tensor_copy(out=xin[s][:, :, 0:30], in_=xin[s][:, :, 128:158]), R=[xin[s]], W=[xin[s]])
                S.op("act", lambda e: e.activation(out=csq[:], in_=cacc[:], func=AF.Square), R=[cacc], W=[csq])
                for hf in range(2):
                    S.op("pe", lambda e, hf=hf: e.matmul(PS_[:, hf * 128:(hf + 1) * 128], lhsT=C("blkavg"), rhs=cacc[:, hf, :], start=True, stop=True),
                         R=[cst, cacc], W=[PS_])
                    S.op("pe", lambda e, hf=hf: e.matmul(PS_[:, 256 + hf * 128:256 + (hf + 1) * 128], lhsT=C("blkavg"), rhs=csq[:, hf, :],
                                                         start=True, stop=True), R=[cst, csq], W=[PS_])
                cm = cmean[:].rearrange("p h t -> p (h t)")
                cv = cvar[:].rearrange("p h t -> p (h t)")
                ca = cacc[:].rearrange("p h t -> p (h t)")
                S.op("act", lambda e: e.copy(out=cm, in_=PS_[:, 0:256]), R=[PS_], W=[cmean])
                S.op("dve", lambda e: e.tensor_tensor(out=cv, in0=cm, in1=cm, op=ALU.mult), R=[cmean], W=[cvar])
                S.op("dve", lambda e: e.tensor_tensor(out=cv, in0=PS_[:, 256:512], in1=cv, op=ALU.subtract), R=[PS_, cvar], W=[cvar])
                do_rsqrt(cvar, cv, cvar, cv, 1.0, LN_EPS)
                S.op("dve", lambda e: e.tensor_tensor(out=ca, in0=ca, in1=cm, op=ALU.subtract), R=[cacc, cmean], W=[cacc])
                S.op("dve", lambda e: e.tensor_tensor(out=ca, in0=ca, in1=cv, op=ALU.mult), R=[cacc, cvar], W=[cacc])
                for hf in range(2):
                    S.op("dve", lambda e, hf=hf: e.tensor_scalar(out=cacc[:, hf, :], in0=cacc[:, hf, :], scalar1=VC("conv_norm_g")[:, hf:hf + 1],
                                                                 scalar2=VC("conv_norm_b")[:, hf:hf + 1], op0=ALU.mult, op1=ALU.add),
                         R=[cacc, vc], W=[cacc])
                S.op("act", lambda e: e.activation(out=csl[:], in_=cacc[:], func=AF.Silu), R=[cacc], W=[csl])
                for hf in range(2):
                    S.op("pe", lambda e, hf=hf: e.matmul(PA[:, 0:256], lhsT=csl[:, hf, :], rhs=Wpw[:, hf, :], start=(hf == 0), stop=(hf == 1)),
                         R=[csl, Wpw, Vaug[s], qsb], W=[PA])
                S.op("act", lambda e, s=s: e.copy(out=omix[s][:, 256:512], in_=PA[:, 0:256]), R=[PA], W=[omix[s]])
                S.mark('================= D: RWKV-7 pre-scan')
                if kind == "p":
                    S.op("act", lambda e, s=s: e.copy(out=PD[s][:, 0:4, 1:129], in_=PF[1][:].rearrange("p (c t) -> p c t", c=4)), R=[PF[1]], W=[PD[s]])
                    S.op("act", lambda e, s=s: e.copy(out=PD[s][:, 4:8, 1:129], in_=PF[2][:].rearrange("p (c t) -> p c t", c=4)), R=[PF[2]], W=[PD[s]])
                else:
                    S.op("act", lambda e: e.copy(out=PDs[:, 0:4, :, 1:9], in_=PF[1][:].rearrange("p (c b t) -> p c b t", c=4, t=8)), R=[PF[1]], W=[PDs])
                    S.op("act", lambda e: e.copy(out=PDs[:, 4:8, :, 1:9], in_=PF[2][:].rearrange("p (c b t) -> p c b t", c=4, t=8)), R=[PF[2]], W=[PDs])
                    S.op("dve", lambda e: e.tensor_copy(out=lastc_s[:], in_=PDs[:, :, :, 8]), R=[PDs], W=[lastc_s])
                    for hh in range(2):
                        for c in range(4):
                            S.op("pe", lambda e, hh=hh, c=c: e.transpose(out=PS_[0:16, c * 128:(c + 1) * 128], in_=lastc_s[:, hh * 4 + c, :],
                                                                        identity=C("ident")), R=[lastc_s, cst], W=[PS_])
                        S.op("act", lambda e, hh=hh: e.copy(out=lastT_s[:, hh * 512:(hh + 1) * 512], in_=PS_[0:16, :]), R=[PS_], W=[lastT_s])
                    S.dma("sp", shift_s_out[l], lastT_s[:], R=[lastT_s], track="o_misc")
                    xs4 = xs[s][:].rearrange("p c (b t) -> p c b t", t=8)
                    S.op("dve", lambda e, xs4=xs4: e.tensor_tensor(out=xs4, in0=PDs[:, :, :, 0:8], in1=PDs[:, :, :, 1:9], op=ALU.subtract),
                         R=[PDs], W=[xs[s]])
                    S.op("dve", lambda e, s=s: e.tensor_tensor(out=xs[s][:], in0=xs[s][:], in1=bc(VC("rw_mu").unsqueeze(2), [128, 8, 128]), op=ALU.mult),
                         R=[xs[s], vc], W=[xs[s]])
                    S.op("dve", lambda e, xs4=xs4: e.tensor_tensor(out=xs4, in0=xs4, in1=PDs[:, :, :, 1:9], op=ALU.add), R=[xs[s], PDs], W=[xs[s]])
                if n == NT - 1 and kind == "p":
                    S.op("dve", lambda e, s=s: e.tensor_copy(out=lastc[:], in_=PD[s][:, :, 128]), R=[PD[s]], W=[lastc])
                    S.op("pe", lambda e: e.transpose(out=PS_[0:8, 0:128], in_=lastc[:], identity=C("ident")), R=[lastc, cst], W=[PS_])
                    S.op("act", lambda e: e.copy(out=lastT[:], in_=PS_[0:8, 0:128]), R=[PS_], W=[lastT])
                    S.dma("sp", shift_prompt[l, s0 + s].rearrange("(c p) -> c p", p=128), lastT[:], R=[lastT], track="o_misc")
                if kind == "p":
                    S.op("dve", lambda e, s=s: e.tensor_tensor(out=xs[s][:], in0=PD[s][:, :, 0:128], in1=PD[s][:, :, 1:129], op=ALU.subtract),
                         R=[PD[s]], W=[xs[s]])
                    S.op("dve", lambda e, s=s: e.tensor_tensor(out=xs[s][:], in0=xs[s][:], in1=bc(VC("rw_mu").unsqueeze(2), [128, 8, 128]), op=ALU.mult),
                         R=[xs[s], vc], W=[xs[s]])
                    S.op("dve", lambda e, s=s: e.tensor_tensor(out=xs[s][:], in0=xs[s][:], in1=PD[s][:, :, 1:129], op=ALU.add), R=[xs[s], PD[s]], W=[xs[s]])
                    S.op("dve", lambda e, s=s: e.tensor_copy(out=PD[s][:, :, 0:1], in_=PD[s][:, :, 128:129]), R=[PD[s], xs[s]], W=[PD[s]])
                S.op("act", lambda e, s=s: e.activation(out=tw[:], in_=xs[s][:, 6, :], func=AF.Tanh), R=[xs[s]], W=[tw])
                for cc in range(2):
                    S.op("pe", lambda e, cc=cc: e.matmul(PS_[:, cc * 128:(cc + 1) * 128], lhsT=W2w[:, cc * 128:(cc + 1) * 128], rhs=tw[:],
                                                         start=True, stop=True), R=[W2w, tw, cvar, cmean], W=[PS_])
                    S.op("pe", lambda e, cc=cc, s=s: e.matmul(PS_[:, 256 + cc * 128:256 + (cc + 1) * 128], lhsT=W2a[:, cc * 128:(cc + 1) * 128],
                                                              rhs=xs[s][:, 6, :], start=True, stop=True), R=[W2a, xs[s]], W=[PS_])
                for cc in range(2):
                    S.op("act", lambda e, cc=cc, s=s: e.activation(out=dec[s][:, cc, :], in_=PS_[:, cc * 128:(cc + 1) * 128], func=AF.Sigmoid,
                                                                  bias=VC("rw_w0")[:, cc:cc + 1]), R=[PS_, vc], W=[dec[s]])
                    S.op("act", lambda e, cc=cc: e.activation(out=aa[:, cc, :], in_=PS_[:, 256 + cc * 128:256 + (cc + 1) * 128], func=AF.Sigmoid,
                                                              bias=VC("rw_a0")[:, cc:cc + 1]), R=[PS_, vc], W=[aa])
                S.op("act", lambda e, s=s: e.activation(out=dec[s][:], in_=dec[s][:], func=AF.Exp, scale=-float(np.exp(-0.5))), R=[dec[s]], W=[dec[s]])
                S.op("act", lambda e, s=s: e.activation(out=sgg[:], in_=xs[s][:, 7, :], func=AF.Sigmoid), R=[xs[s]], W=[sgg])
                for cc in range(2):
                    S.op("pe", lambda e, cc=cc: e.matmul(PS_[:, cc * 128:(cc + 1) * 128], lhsT=G2[:, cc * 128:(cc + 1) * 128], rhs=sgg[:],
                                                         start=True, stop=True), R=[G2, sgg, dec[s], aa], W=[PS_])
                S.op("act", lambda e, s=s: e.copy(out=gT[s][:].rearrange("p c t -> p (c t)"), in_=PS_[:, 0:256]), R=[PS_], W=[gT[s]])
                S.op("dve", lambda e, s=s: e.tensor_tensor(out=kk[:], in0=xs[s][:, 2:4, :], in1=bc(VC("rw_kk").unsqueeze(2), [128, 2, 128]), op=ALU.mult),
                     R=[xs[s], vc], W=[kk])
                S.op("dve", lambda e: e.tensor_tensor(out=kk2[:], in0=kk[:], in1=kk[:], op=ALU.mult), R=[kk], W=[kk2])
                for cc in range(2):
                    S.op("pe", lambda e, cc=cc: e.matmul(PS_[:, cc * 128:(cc + 1) * 128], lhsT=C("blkone"), rhs=kk2[:, cc, :], start=True, stop=True),
                         R=[cst, kk2, gT[s]], W=[PS_])
                kk2v = kk2[:].rearrange("p c t -> p (c t)")
                do_rsqrt(kk2, kk2v, PS_, PS_[:, 0:256], 1.0, 1e-12)
                S.op("dve", lambda e, s=s: e.scalar_tensor_tensor(out=nkk[s][:], in0=kk[:], scalar=-1.0, in1=kk2[:], op0=ALU.mult, op1=ALU.mult),
                     R=[kk, kk2], W=[nkk[s]])
                S.op("dve", lambda e, s=s: e.scalar_tensor_tensor(out=kka[s][:], in0=nkk[s][:], scalar=-1.0, in1=aa[:], op0=ALU.mult, op1=ALU.mult),
                     R=[nkk[s], aa], W=[kka[s]])
                S.op("dve", lambda e: e.tensor_tensor(out=tk[:], in0=aa[:], in1=bc(VC("rw_ka").unsqueeze(2), [128, 2, 128]), op=ALU.mult), R=[aa, vc], W=[tk])
                S.op("dve", lambda e: e.tensor_tensor(out=tk[:], in0=tk[:], in1=bc(omka[:].unsqueeze(2), [128, 2, 128]), op=ALU.add), R=[tk, omka], W=[tk])
                S.op("dve", lambda e, s=s: e.tensor_tensor(out=kfin[s][:], in0=xs[s][:, 2:4, :], in1=tk[:], op=ALU.mult), R=[xs[s], tk], W=[kfin[s]])
                S.op("dve", lambda e, s=s: e.tensor_tensor(out=tk[:], in0=xs[s][:, 0:2, :], in1=kfin[s][:], op=ALU.mult), R=[xs[s], kfin[s]], W=[tk])
                S.op("dve", lambda e: e.tensor_tensor(out=tk[:], in0=tk[:], in1=bc(VC("rw_rk").unsqueeze(2), [128, 2, 128]), op=ALU.mult), R=[tk, vc], W=[tk])
                for cc in range(2):
                    S.op("pe", lambda e, cc=cc: e.matmul(PS_[:, cc * 128:(cc + 1) * 128], lhsT=C("blkone"), rhs=tk[:, cc, :], start=True, stop=True),
                         R=[cst, tk, kk2], W=[PS_])
                S.op("dve", lambda e, s=s: e.tensor_tensor(out=bon[s][:].rearrange("p c t -> p (c t)"), in0=PS_[:, 0:256],
                                                           in1=xs[s][:, 4:6, :].rearrange("p c t -> p (c t)"), op=ALU.mult), R=[PS_, xs[s]], W=[bon[s]])
                for cc in range(2):
                    S.op("pe", lambda e, cc=cc, s=s: e.transpose(out=PS_[:, 256 + cc * 128:256 + (cc + 1) * 128], in_=xs[s][:, 4 + cc, :], identity=C("ident")),
                         R=[xs[s], cst, bon[s]], W=[PS_])
                S.op("act", lambda e: e.copy(out=vtm[:], in_=PS_[:, 256:512]), R=[PS_], W=[vtm])
                S.dma("sp", v_scr[s], vtm[:], R=[vtm], W=[], track="vscr%d" % s)
            S.mark('---------------- scan over the 128 ste')
            vscr_tok = [S.track("vscr%d" % s)[2] for s in range(NSEQ)]

            def y_mm(t, Sbuf):
                for s in range(NSEQ):
                    for cc in range(2):
                        g = 2 * s + cc
                        for p2 in range(2):
                            pp = slice(p2 * 64, (p2 + 1) * 64)
                            S.op("pe", lambda e, s=s, cc=cc, g=g, pp=pp, t=t, Sbuf=Sbuf: e.matmul(
                                PY[pp, g * 128 + t:g * 128 + t + 1], lhsT=Sbuf[pp, g, :], rhs=xs[s][pp, cc, t:t + 1],
                                start=True, stop=True), R=[Sbuf, xs[s], rl], W=[PY])
            if kind == "s":
                sample_attention()
                sample_scan()
            for c0 in (range(0, 128, CH) if kind == "p" else []):
                for s in range(NSEQ):
                    for p2 in range(2):
                        for cc in range(2):
                            hh = cc * 2 + p2
                            src = bc(v_scr[s, c0:c0 + CH, hh * 64:(hh + 1) * 64].unsqueeze(0), [64, CH, 64])
                            if vscr_tok[s] is not None and S.n < S.limit:
                                S._wait("sp", vscr_tok[s])
                            S.dma("sp", vbc[p2 * 64:(p2 + 1) * 64, :, 2 * s + cc, :], src, W=[vbc], track="vbc")
                for s in range(NSEQ):
                    kf = kfin[s][:, :, c0:c0 + CH].rearrange("p c t -> p t c")
                    S.op("pool", lambda e, s=s, kf=kf: e.tensor_tensor(out=KV[:, :, 2 * s:2 * s + 2, :], in0=vbc[:, :, 2 * s:2 * s + 2, :],
                                                                        in1=bc(kf.unsqueeze(3), [128, CH, 2, 64]), op=ALU.mult),
                         R=[vbc, kfin[s]], W=[KV])
                for tt in range(CH):
                    t = c0 + tt
                    prev = Sb[(t + 1) % 2]
                    new = Sb[t % 2]
                    for s in range(NSEQ):
                        for cc in range(2):
                            g = 2 * s + cc
                            for p2 in range(2):
                                pp = slice(p2 * 64, (p2 + 1) * 64)
                                S.op("pe", lambda e, s=s, cc=cc, g=g, pp=pp, t=t, prev=prev: e.matmul(
                                    PS_[pp, g * 64:(g + 1) * 64], lhsT=bc(nkk[s][pp, cc, t:t + 1], [64, 64]), rhs=prev[pp, g, :],
                                    start=True, stop=True), R=[nkk[s], prev, vtm], W=[PS_])
                    if t > 0:
                        y_mm(t - 1, prev)
                    for s in range(NSEQ):
                        S.op("dve", lambda e, s=s, t=t, prev=prev: e.tensor_tensor(out=Abuf[:, 2 * s:2 * s + 2, :], in0=prev[:, 2 * s:2 * s + 2, :],
                                                                                   in1=bc(dec[s][:, :, t:t + 1], [128, 2, 64]), op=ALU.mult),
                             R=[prev, dec[s]], W=[Abuf])
                    S.op("dve", lambda e, tt=tt: e.tensor_tensor(out=Abuf[:], in0=Abuf[:], in1=KV[:, tt, :, :], op=ALU.add), R=[Abuf, KV], W=[Abuf])
                    sa = PS_[:, 0:NG * 64].rearrange("p (g i) -> p g i", g=NG)
                    for s in range(NSEQ):
                        S.op("dve", lambda e, s=s, t=t: e.tensor_tensor(out=T1[:, 2 * s:2 * s + 2, :], in0=sa[:, 2 * s:2 * s + 2, :],
                                                                        in1=bc(kka[s][:, :, t:t + 1], [128, 2, 64]), op=ALU.mult),
                             R=[PS_, kka[s]], W=[T1])
                    S.op("dve", lambda e, new=new: e.tensor_tensor(out=new[:], in0=Abuf[:], in1=T1[:], op=ALU.add), R=[Abuf, T1], W=[new])
            if kind == "p":
                y_mm(127, Sb[1])
            S.mark('---------------- post-scan per sequenc')
            for s in range(NSEQ):
                xb = x_t[s]
                yv = ysb[s][:].rearrange("p c t -> p (c t)")
                S.op("act", lambda e, s=s, yv=yv: e.copy(out=yv, in_=PY[:, 2 * s * 128:(2 * s + 2) * 128]), R=[PY], W=[ysb[s]])
                S.op("act", lambda e, s=s: e.activation(out=ysq[:], in_=ysb[s][:], func=AF.Square), R=[ysb[s]], W=[ysq])
                for cc in range(2):
                    S.op("pe", lambda e, cc=cc, s=s: e.matmul(PS_[:, cc * 128:(cc + 1) * 128], lhsT=C("blkavg"), rhs=ysb[s][:, cc, :], start=True, stop=True),
                         R=[cst, ysb[s], T1], W=[PS_])
                    S.op("pe", lambda e, cc=cc: e.matmul(PS_[:, 256 + cc * 128:256 + (cc + 1) * 128], lhsT=C("blkavg"), rhs=ysq[:, cc, :],
                                                         start=True, stop=True), R=[cst, ysq], W=[PS_])
                cm = cmean[:].rearrange("p h t -> p (h t)")
                cv = cvar[:].rearrange("p h t -> p (h t)")
                S.op("act", lambda e: e.copy(out=cm, in_=PS_[:, 0:256]), R=[PS_], W=[cmean])
                S.op("dve", lambda e: e.tensor_tensor(out=cv, in0=cm, in1=cm, op=ALU.mult), R=[cmean], W=[cvar])
                S.op("dve", lambda e: e.tensor_tensor(out=cv, in0=PS_[:, 256:512], in1=cv, op=ALU.subtract), R=[PS_, cvar], W=[cvar])
                do_rsqrt(cvar, cv, cvar, cv, 1.0, RW_LN_EPS)
                S.op("dve", lambda e, yv=yv: e.tensor_tensor(out=yv, in0=yv, in1=cm, op=ALU.subtract), R=[ysb[s], cmean], W=[ysb[s]])
                S.op("dve", lambda e, yv=yv: e.tensor_tensor(out=yv, in0=yv, in1=cv, op=ALU.mult), R=[ysb[s], cvar], W=[ysb[s]])
                for cc in range(2):
                    S.op("dve", lambda e, cc=cc, s=s: e.tensor_scalar(out=ysb[s][:, cc, :], in0=ysb[s][:, cc, :], scalar1=VC("rw_ln_g")[:, cc:cc + 1],
                                                                      scalar2=VC("rw_ln_b")[:, cc:cc + 1], op0=ALU.mult, op1=ALU.add),
                         R=[ysb[s], vc], W=[ysb[s]])
                S.op("dve", lambda e, s=s: e.tensor_tensor(out=ysb[s][:], in0=ysb[s][:], in1=bon[s][:], op=ALU.add), R=[ysb[s], bon[s]], W=[ysb[s]])
                S.op("dve", lambda e, s=s: e.tensor_tensor(out=odT[:], in0=ysb[s][:], in1=gT[s][:], op=ALU.mult), R=[ysb[s], gT[s]], W=[odT])
                for cc in range(2):
                    S.op("pe", lambda e, cc=cc: e.transpose(out=PS_[:, cc * 128:(cc + 1) * 128], in_=odT[:, cc, :], identity=C("ident")),
                         R=[odT, cst, cmean, cvar], W=[PS_])
                S.op("act", lambda e, s=s: e.copy(out=omix[s][:, 768:1024], in_=PS_[:, 0:256]), R=[PS_], W=[omix[s]])
                S.op("act", lambda e, s=s: e.activation(out=junk[:], in_=omix[s][:], func=AF.Square), R=[omix[s]], W=[junk])
                S.op("dve", lambda e: e.reduce_sum(out=ss[:, 0:4], in_=junk[:].rearrange("p (g d) -> p g d", g=4), axis=AX.X), R=[junk], W=[ss])
                do_rsqrt(rstd, rstd[:, 0:4], ss, ss[:, 0:4], 1.0 / 256, EPS)
                S.op("dve", lambda e, s=s: e.tensor_tensor(out=omix[s][:].rearrange("p (g d) -> p g d", g=4), in0=omix[s][:].rearrange("p (g d) -> p g d", g=4),
                                                           in1=bc(rstd[:, 0:4].unsqueeze(2), [128, 4, 256]), op=ALU.mult), R=[omix[s], rstd], W=[omix[s]])
                S.op("dve", lambda e, s=s: e.tensor_tensor(out=on_bf[:], in0=omix[s][:], in1=VB("out_norm"), op=ALU.mult), R=[omix[s], vb], W=[on_bf])
                for kc in range(KC):
                    S.op("pe", lambda e, kc=kc: e.transpose(out=PT[:, kc * 128:(kc + 1) * 128], in_=on_bf[:, kc * 128:(kc + 1) * 128], identity=ident_bf[:]),
                         R=[on_bf, ident_bf], W=[PT])
                S.op("act", lambda e: e.copy(out=onT[:].rearrange("p k t -> p (k t)"), in_=PT[:]), R=[PT], W=[onT])
                for hb, pb in ((0, PA), (1, PC)):
                    for kc in range(KC):
                        S.op("pe", lambda e, hb=hb, pb=pb, kc=kc: e.matmul(pb[:], lhsT=onT[:, kc, :], rhs=Wout[:, kc, hb * 512:(hb + 1) * 512],
                                                                          start=(kc == 0), stop=(kc == KC - 1)), R=[onT, Wout, omix[s]], W=[pb])
                    S.op("dve", lambda e, hb=hb, pb=pb, xb=xb: e.tensor_tensor(out=xb[:, hb * 512:(hb + 1) * 512], in0=xb[:, hb * 512:(hb + 1) * 512],
                                                                              in1=pb[:], op=ALU.add), R=[xb, pb], W=[xb])
                S.dma("sp", (xdst[s0 + s, n * 128:(n + 1) * 128, :] if kind == "p" else xres_s), xb[:], R=[xb], track="x%d" % s)
        for g in (range(NG) if kind == "p" else []):
            for p2 in range(2):
                pp = slice(p2 * 64, (p2 + 1) * 64)
                S.op("pe", lambda e, g=g, pp=pp: e.matmul(PS_[pp, g * 64:(g + 1) * 64], lhsT=Sst[pp, g, :], rhs=C("ident")[pp, pp], start=True, stop=True),
                     R=[Sst, cst], W=[PS_])
        if kind == "p":
            S.op("act", lambda e: e.copy(out=wkv_nat[:].rearrange("p g j -> p (g j)"), in_=PS_[:, 0:NG * 64]), R=[PS_], W=[wkv_nat])
        for s in (range(NSEQ) if kind == "p" else []):
            for p2 in range(2):
                dst = wkv_prompt[l, s0 + s].rearrange("(cc q) i j -> q i cc j", q=2)[p2]
                S.dma("sp", dst, wkv_nat[p2 * 64:(p2 + 1) * 64, 2 * s:2 * s + 2, :], R=[wkv_nat], track="o_misc")
        S.barrier()

        S.barrier()
        S.emit()
        pes.close()
        cur[0] = es


    def common_tiles():
        t = {}
        t["vb"] = sb("vb", [128, NB])
        t["x_t"] = sb("x_t", [128, D])
        t["junk"] = sb("junk", [128, D])
        t["ss"] = sb("ss", [128, 8])
        t["rstd"] = sb("rstd", [128, 8])
        t["h_bf"] = sb("h_bf", [128, D], BF16)
        t["hT"] = sb("hT", [128, KC, 128], BF16)
        return t

    def mk_helpers(t):
        vb, junk, ss, rstd, h_bf, hT = t["vb"], t["junk"], t["ss"], t["rstd"], t["h_bf"], t["hT"]

        def VBx(name):
            o, n = VB_OFF[name]
            return vb[:, o:o + n]

        def do_rsqrt(dstb, dst, srcb, src, scale, eps):
            S.op("dve", lambda e: e.tensor_scalar(out=dst, in0=src, scalar1=scale, scalar2=eps, op0=ALU.mult, op1=ALU.add), R=[srcb], W=[dstb])
            S.op("act", lambda e: e.sqrt(out=dst, in_=dst), R=[dstb], W=[dstb])
            S.op("dve", lambda e: e.reciprocal(out=dst, in_=dst), R=[dstb], W=[dstb])

        def norm_T(xb, gname):
            S.op("act", lambda e: e.activation(out=junk[:], in_=xb[:], func=AF.Square), R=[xb], W=[junk])
            S.op("dve", lambda e: e.reduce_sum(out=ss[:, 0:1], in_=junk[:], axis=AX.X), R=[junk], W=[ss])
            do_rsqrt(rstd, rstd[:, 0:1], ss, ss[:, 0:1], 1.0 / D, EPS)
            S.op("dve", lambda e: e.scalar_tensor_tensor(out=h_bf[:], in0=xb[:], scalar=rstd[:, 0:1], in1=VBx(gname),
                                                         op0=ALU.mult, op1=ALU.mult), R=[xb, rstd, vb], W=[h_bf])
            for kc in range(KC):
                S.op("pe", lambda e, kc=kc: e.transpose(out=PT[:, kc * 128:(kc + 1) * 128], in_=h_bf[:, kc * 128:(kc + 1) * 128],
                                                        identity=ident_bf[:]), R=[h_bf, ident_bf], W=[PT])
            S.op("act", lambda e: e.copy(out=hT[:].rearrange("p k t -> p (k t)"), in_=PT[:]), R=[PT], W=[hT])

        def headnorm(pb, dstb, dst3, gname, nh, hd):
            S.op("act", lambda e: e.activation(out=junk[:, 0:nh * hd], in_=pb[:, 0:nh * hd], func=AF.Square), R=[pb], W=[junk])
            S.op("dve", lambda e: e.reduce_sum(out=ss[:, 0:nh], in_=junk[:, 0:nh * hd].rearrange("p (h d) -> p h d", h=nh), axis=AX.X),
                 R=[junk], W=[ss])
            do_rsqrt(rstd, rstd[:, 0:nh], ss, ss[:, 0:nh], 1.0 / hd, EPS)
            j3 = junk[:, 0:nh * hd].rearrange("p (h d) -> p h d", h=nh)
            S.op("dve", lambda e: e.tensor_tensor(out=j3, in0=pb[:, 0:nh * hd].rearrange("p (h d) -> p h d", h=nh),
                                                  in1=bc(rstd[:, 0:nh].unsqueeze(2), [128, nh, hd]), op=ALU.mult), R=[pb, rstd, junk], W=[junk])
            S.op("dve", lambda e: e.tensor_tensor(out=dst3, in0=j3, in1=bc(VBx(gname).unsqueeze(1), [128, nh, hd]), op=ALU.mult),
                 R=[junk, vb], W=[dstb])

        return VBx, do_rsqrt, norm_T, headnorm

    def xattn_phase(l):
        S.mark("xattn")
        pes = ExitStack()
        cur[0] = pes
        t = common_tiles()
        vb, x_t, junk, hT = t["vb"], t["x_t"], t["junk"], t["hT"]
        VBx, do_rsqrt, norm_T, headnorm = mk_helpers(t)
        Wq = sb("Wq", [128, KC, 512], BF16)
        Wk = sb("Wk", [128, KC, 512], BF16)
        Wv = sb("Wv", [128, KC, 512], BF16)
        Wo = sb("Wo", [128, 4, D], BF16)
        ones_bf = sb("ones_bf", [128, 2], BF16)
        kf = sb("kf", [128, 4, 128])
        k_bf = sb("k_bf", [128, 4, 128], BF16)
        vf = sb("vf", [128, 512])
        kTm = sb("kTm", [128, 4, 256], BF16)
        Vm = sb("Vm", [128, 2, 512], BF16)
        q_bf = sb("q_bf", [128, 4, 128], BF16)
        qTx = sb("qTx", [128, 4, 128], BF16)
        PTx = sb("PTx", [128, 2, 512], BF16)
        rl = sb("rlx", [128, 4])
        xo_bf = sb("xo_bf", [128, 4, 128], BF16)
        xoT = sb("xoT", [128, 4, 128], BF16)

        S.dma("sp", vb[:], bc(vb_d[l:l + 1, :], [128, NB]), W=[vb], track="par")
        S.dma("pool", Wq[:], wq_x[l].rearrange("(kc p) n -> p kc n", p=128), W=[Wq], track="w1")
        S.dma("pool", Wk[:], wk_x[l].rearrange("(kc p) n -> p kc n", p=128), W=[Wk], track="w1")
        S.dma("pool", Wv[:], wv_x[l].rearrange("(kc p) n -> p kc n", p=128), W=[Wv], track="w1")
        S.dma("pool", Wo[:], wo_x[l].rearrange("(kc p) n -> p kc n", p=128), W=[Wo], track="w1")
        S.op("pool", lambda e: e.memset(ones_bf[:], 1.0), W=[ones_bf])
        oq, nq = VB_OFF["xq_norm"]
        S.op("act", lambda e: e.mul(out=vb[:, oq:oq + nq], in_=vb[:, oq:oq + nq], mul=128.0 ** -0.5), R=[vb], W=[vb])


        def q_proj(src):
            S.dma("sp", x_t[:], src, W=[x_t], track="xx")
            norm_T(x_t, "norm_x")
            for kc in range(KC):
                S.op("pe", lambda e, kc=kc: e.matmul(PA[:], lhsT=hT[:, kc, :], rhs=Wq[:, kc, :], start=(kc == 0), stop=(kc == KC - 1)),
                     R=[hT, Wq], W=[PA])
            headnorm(PA, q_bf, q_bf[:], "xq_norm", 4, 128)
            for h in range(4):
                S.op("pe", lambda e, h=h: e.transpose(out=PT[:, h * 128:(h + 1) * 128], in_=q_bf[:, h, :], identity=ident_bf[:]),
                     R=[q_bf, ident_bf], W=[PT])
            S.op("act", lambda e: e.copy(out=qTx[:].rearrange("p h t -> p (h t)"), in_=PT[:, 0:512]), R=[PT], W=[qTx])

        def attn_epilogue(dst):
            S.op("dve", lambda e: e.reciprocal(out=rl[:], in_=PS_[:, 0:4]), R=[PS_], W=[rl])
            S.op("dve", lambda e: e.tensor_tensor(out=xo_bf[:], in0=PC[:].rearrange("p (h d) -> p h d", h=4),
                                                  in1=bc(rl[:].unsqueeze(2), [128, 4, 128]), op=ALU.mult), R=[PC, rl], W=[xo_bf])
            for h in range(4):
                S.op("pe", lambda e, h=h: e.transpose(out=PT[:, h * 128:(h + 1) * 128], in_=xo_bf[:, h, :], identity=ident_bf[:]),
                     R=[xo_bf, ident_bf], W=[PT])
            S.op("act", lambda e: e.copy(out=xoT[:].rearrange("p h t -> p (h t)"), in_=PT[:, 0:512]), R=[PT], W=[xoT])
            for hb, pb in ((0, PA), (1, PF[2])):
                for kc in range(4):
                    S.op("pe", lambda e, hb=hb, pb=pb, kc=kc: e.matmul(pb[:], lhsT=xoT[:, kc, :], rhs=Wo[:, kc, hb * 512:(hb + 1) * 512],
                                                                      start=(kc == 0), stop=(kc == 3)), R=[xoT, Wo], W=[pb])
                S.op("dve", lambda e, hb=hb, pb=pb: e.tensor_tensor(out=x_t[:, hb * 512:(hb + 1) * 512], in0=x_t[:, hb * 512:(hb + 1) * 512],
                                                                    in1=pb[:], op=ALU.add), R=[x_t, pb], W=[x_t])
            S.dma("sp", dst, x_t[:], R=[x_t], track="xx")

        for sg in range(NSEQ):
            for mt in range(2):
                S.dma("sp", x_t[:], mem_prompt[sg, mt * 128:(mt + 1) * 128, :], W=[x_t], track="xx")
                norm_T(x_t, "mem_norm")
                for kc in range(KC):
                    S.op("pe", lambda e, kc=kc: e.matmul(PA[:], lhsT=hT[:, kc, :], rhs=Wk[:, kc, :], start=(kc == 0), stop=(kc == KC - 1)),
                         R=[hT, Wk], W=[PA])
                for kc in range(KC):
                    S.op("pe", lambda e, kc=kc: e.matmul(PC[:], lhsT=hT[:, kc, :], rhs=Wv[:, kc, :], start=(kc == 0), stop=(kc == KC - 1)),
                         R=[hT, Wv], W=[PC])
                headnorm(PA, kf, kf[:], "xk_norm", 4, 128)
                S.dma("sp", mem_k_prompt[l, sg, mt * 128:(mt + 1) * 128, :], kf[:].rearrange("p h d -> p (h d)"), R=[kf], track="o_mk")
                S.op("act", lambda e: e.copy(out=k_bf[:], in_=kf[:]), R=[kf], W=[k_bf])
                for h in range(4):
                    S.op("pe", lambda e, h=h: e.transpose(out=PT[:, h * 128:(h + 1) * 128], in_=k_bf[:, h, :], identity=ident_bf[:]),
                         R=[k_bf, ident_bf], W=[PT])
                S.op("act", lambda e, mt=mt: e.copy(out=kTm[:, :, mt * 128:(mt + 1) * 128], in_=PT[:, 0:512].rearrange("p (h t) -> p h t", h=4)),
                     R=[PT], W=[kTm])
                S.op("act", lambda e: e.copy(out=vf[:], in_=PC[:]), R=[PC], W=[vf])
                S.dma("sp", mem_v_prompt[l, sg, mt * 128:(mt + 1) * 128, :], vf[:], R=[vf], track="o_mv")
                S.op("dve", lambda e, mt=mt: e.tensor_copy(out=Vm[:, mt, :], in_=vf[:]), R=[vf], W=[Vm])
            for n in range(NT):
                q_proj(xres[sg, n * 128:(n + 1) * 128, :])
                for mt in range(2):
                    pb = PF[mt]
                    for h in range(4):
                        S.op("pe", lambda e, pb=pb, mt=mt, h=h: e.matmul(pb[:, h * 128:(h + 1) * 128], lhsT=kTm[:, h, mt * 128:(mt + 1) * 128],
                                                                        rhs=qTx[:, h, :], start=True, stop=True), R=[kTm, qTx], W=[pb])
                    S.op("act", lambda e, pb=pb, mt=mt: e.activation(out=PTx[:, mt, :], in_=pb[:], func=AF.Exp), R=[pb], W=[PTx])
                first = True
                for mt in range(2):
                    for h in range(4):
                        S.op("pe", lambda e, mt=mt, h=h, first=first: e.matmul(PC[:, h * 128:(h + 1) * 128], lhsT=PTx[:, mt, h * 128:(h + 1) * 128],
                                                                              rhs=Vm[:, mt, h * 128:(h + 1) * 128], start=first,
                                                                              stop=(mt == 1 and h == 3)), R=[PTx, Vm], W=[PC])
                        first = False
                first = True
                for mt in range(2):
                    for h in range(4):
                        S.op("pe", lambda e, mt=mt, h=h, first=first: e.matmul(PS_[:, h:h + 1], lhsT=PTx[:, mt, h * 128:(h + 1) * 128],
                                                                              rhs=ones_bf[:, 0:1], start=first, stop=(mt == 1 and h == 3)),
                             R=[PTx, ones_bf], W=[PS_])
                        first = False
                attn_epilogue(xres[sg, n * 128:(n + 1) * 128, :])
        if SAMPLE:
            S.mark("xattn_sample")
            Ppad = sb("Ppad", [128, 2, 4, 128], BF16)
            kc_bf = sb("kc_bf", [128, 2, 512], BF16)
            S.op("pool", lambda e: e.memset(Ppad[:], 0.0), W=[Ppad])
            q_proj(xres_s)
            for b in range(16):
                S.dma("pool", kc_bf[:], cache_mem_k[l, b].rearrange("(mt p) c -> p mt c", p=128), W=[kc_bf], track="cmk")
                S.dma("pool", Vm[:], cache_mem_v[l, b].rearrange("(mt p) c -> p mt c", p=128), W=[Vm], track="cmv")
                for mt in range(2):
                    for h in range(4):
                        S.op("pe", lambda e, mt=mt, h=h: e.transpose(out=PT[:, h * 128:(h + 1) * 128], in_=kc_bf[:, mt, h * 128:(h + 1) * 128],
                                                                    identity=ident_bf[:]), R=[kc_bf, ident_bf], W=[PT])
                    S.op("act", lambda e, mt=mt: e.copy(out=kTm[:, :, mt * 128:(mt + 1) * 128], in_=PT[:, 0:512].rearrange("p (h t) -> p h t", h=4)),
                         R=[PT], W=[kTm])
                for mt in range(2):
                    pb = PF[mt]
                    for h in range(4):
                        S.op("pe", lambda e, pb=pb, mt=mt, h=h, b=b: e.matmul(pb[:, h * 8:(h + 1) * 8], lhsT=kTm[:, h, mt * 128:(mt + 1) * 128],
                                                                             rhs=qTx[:, h, b * 8:(b + 1) * 8], start=True, stop=True), R=[kTm, qTx], W=[pb])
                    S.op("act", lambda e, pb=pb, mt=mt, b=b: e.activation(out=Ppad[:, mt, :, b * 8:(b + 1) * 8],
                                                                         in_=pb[:, 0:32].rearrange("p (h q) -> p h q", h=4), func=AF.Exp), R=[pb], W=[Ppad])
                for mt in range(2):
                    for h in range(4):
                        fst = (b == 0 and mt == 0 and h == 0)
                        lst = (b == 15 and mt == 1 and h == 3)
                        S.op("pe", lambda e, mt=mt, h=h, fst=fst, lst=lst: e.matmul(PC[:, h * 128:(h + 1) * 128], lhsT=Ppad[:, mt, h, :],
                                                                                   rhs=Vm[:, mt, h * 128:(h + 1) * 128], start=fst, stop=lst), R=[Ppad, Vm], W=[PC])
                for mt in range(2):
                    for h in range(4):
                        fst = (b == 0 and mt == 0 and h == 0)
                        lst = (b == 15 and mt == 1 and h == 3)
                        S.op("pe", lambda e, mt=mt, h=h, fst=fst, lst=lst: e.matmul(PS_[:, h:h + 1], lhsT=Ppad[:, mt, h, :], rhs=ones_bf[:, 0:1],
                                                                                   start=fst, stop=lst), R=[Ppad, ones_bf], W=[PS_])
                S.op("pool", lambda e, b=b: e.memset(Ppad[:, :, :, b * 8:(b + 1) * 8], 0.0), R=[], W=[Ppad])
            attn_epilogue(xres_s)
        S.barrier()
        S.emit()
        pes.close()
        cur[0] = es

    def ffn_phase(l):
        S.mark("ffn")
        pes = ExitStack()
        cur[0] = pes
        t = common_tiles()
        vb, x_t, junk, hT = t["vb"], t["x_t"], t["junk"], t["hT"]
        VBx, do_rsqrt, norm_T, headnorm = mk_helpers(t)
        Wf1 = sb("Wf1", [128, KC, 2 * D_FF], BF16)
        Wf2 = sb("Wf2", [128, FC, D], BF16)
        sgf = sb("sgf", [128, 512])
        act = sb("act", [128, FC, 128], BF16)
        S.dma("sp", vb[:], bc(vb_d[l:l + 1, :], [128, NB]), W=[vb], track="par")
        w1 = w_ffn_in[l].rearrange("(kc p) n -> p kc n", p=128)
        for c0 in range(0, 2 * D_FF, 1408):
            S.dma("pool", Wf1[:, :, c0:c0 + 1408], w1[:, :, c0:c0 + 1408], W=[Wf1], track="w1")
        w2 = w_ffn_out[l].rearrange("(fc p) n -> p fc n", p=128)
        S.dma("pool", Wf2[:, 0:11, :], w2[:, 0:11, :], W=[Wf2], track="w1")
        S.dma("pool", Wf2[:, 11:22, :], w2[:, 11:22, :], W=[Wf2], track="w1")
        dst = y_prompt if l == L - 1 else xres
        tiles = [(xres[sg, n * 128:(n + 1) * 128, :], dst[sg, n * 128:(n + 1) * 128, :]) for sg in range(NSEQ) for n in range(NT)]
        if SAMPLE:
            tiles.append((xres_s, (y_sample if l == L - 1 else xres_s)))
        for (tsrc, tdst) in tiles:
            if True:
                S.dma("sp", x_t[:], tsrc, W=[x_t], track="xx")
                norm_T(x_t, "norm_ffn")
                gi = 0
                for fc0 in range(0, FC, 4):
                    nch = min(4, FC - fc0)
                    G = (PA, PF[0])[gi % 2]
                    U = (PC, PF[1])[gi % 2]
                    gi += 1
                    for j in range(nch):
                        for kc in range(KC):
                            S.op("pe", lambda e, G=G, j=j, kc=kc, fc0=fc0: e.matmul(G[:, j * 128:(j + 1) * 128], lhsT=Wf1[:, kc, (fc0 + j) * 128:(fc0 + j + 1) * 128],
                                                                                   rhs=hT[:, kc, :], start=(kc == 0), stop=(kc == KC - 1)), R=[Wf1, hT], W=[G])
                    for j in range(nch):
                        for kc in range(KC):
                            S.op("pe", lambda e, U=U, j=j, kc=kc, fc0=fc0: e.matmul(U[:, j * 128:(j + 1) * 128],
                                                                                   lhsT=Wf1[:, kc, D_FF + (fc0 + j) * 128:D_FF + (fc0 + j + 1) * 128],
                                                                                   rhs=hT[:, kc, :], start=(kc == 0), stop=(kc == KC - 1)), R=[Wf1, hT], W=[U])
                    S.op("act", lambda e, G=G, nch=nch: e.activation(out=sgf[:, 0:nch * 128], in_=G[:, 0:nch * 128], func=AF.Silu), R=[G], W=[sgf])
                    S.op("dve", lambda e, U=U, nch=nch, fc0=fc0: e.tensor_tensor(out=act[:, fc0:fc0 + nch, :].rearrange("p c t -> p (c t)"), in0=sgf[:, 0:nch * 128],
                                                                                  in1=U[:, 0:nch * 128], op=ALU.mult), R=[sgf, U], W=[act])
                for hb, pb in ((0, PS_), (1, PY)):
                    for fc in range(FC):
                        S.op("pe", lambda e, hb=hb, pb=pb, fc=fc: e.matmul(pb[:], lhsT=act[:, fc, :], rhs=Wf2[:, fc, hb * 512:(hb + 1) * 512],
                                                                          start=(fc == 0), stop=(fc == FC - 1)), R=[act, Wf2], W=[pb])
                    S.op("dve", lambda e, hb=hb, pb=pb: e.tensor_tensor(out=x_t[:, hb * 512:(hb + 1) * 512], in0=x_t[:, hb * 512:(hb + 1) * 512],
                                                                        in1=pb[:], op=ALU.add), R=[x_t, pb], W=[x_t])
                S.dma("sp", tdst, x_t[:], R=[x_t], track="xx")
        S.barrier()
        S.emit()
        pes.close()
        cur[0] = es

    SG = cfg.get('SG', 1)
    for l in range(L):
        for s0 in range(0, NSEQ, SG):
            mixer_phase(l, s0, SG)
        if SAMPLE:
            mixer_phase(l, 0, 1, kind="s")
        if cfg.get('PHASES', 3) >= 2:
            xattn_phase(l)
        if cfg.get('PHASES', 3) >= 3:
            ffn_phase(l)

    S.barrier()
    S.emit()
    es.close()
    if cfg.get('PRINT_ALLOC'):
        print(sorted(alloc_log, key=lambda x: -x[1])[:60])
    if cfg.get('PRINT_MARKS'):
        print(S.marks, S.n)
    return nc, carr


OUT_KEYS = ["y_prompt", "ckv_prompt", "kpe_prompt", "conv_prompt", "shift_prompt", "wkv_prompt", "mem_k_prompt", "mem_v_prompt",
            "y_sample", "ckv_s_out", "kpe_s_out", "conv_old_out", "conv_new_out", "shift_s_out", "wkv_s_out", "sgv_s_out"]


def core_inputs(inp, c, ncore, carr, L, ag, nch=10):
    B = inp["x_prompt"].shape[0]
    NSEQ = B // ncore
    Bs = inp["x_sample"].shape[0]
    nb = Bs // ncore
    f = np.float32
    vb, vc = pack_small(inp, L)
    m = {"consts": carr, "vb": vb, "vc": vc}
    for k in WEIGHT_KEYS:
        m[k] = np.ascontiguousarray(inp[k], dtype=f)
    m["x_prompt"] = np.ascontiguousarray(inp["x_prompt"][c * NSEQ:(c + 1) * NSEQ])
    m["mem_prompt"] = np.ascontiguousarray(inp["mem_prompt"][c * NSEQ:(c + 1) * NSEQ])
    m["x_sample"] = np.ascontiguousarray(inp["x_sample"][c * nb:(c + 1) * nb]).reshape(nb * 8, D)
    nph = inp["cache_ckv"].shape[0]
    if ag > 1:
        pc = nph // (ag * nch)
        ck = inp["cache_ckv"].reshape(nch, ag, pc, L, 128, 128)[:, c]
        kp = inp["cache_kpe"].reshape(nch, ag, pc, L, 128, 32)[:, c]
        m["cache_ckv"] = np.ascontiguousarray(ck).reshape(-1, 128)
        m["cache_kpe"] = np.ascontiguousarray(kp).reshape(-1, 32)
    else:
        m["cache_ckv"] = np.ascontiguousarray(inp["cache_ckv"]).reshape(-1, 128)
        m["cache_kpe"] = np.ascontiguousarray(inp["cache_kpe"]).reshape(-1, 32)
    m["cache_mem_k"] = np.ascontiguousarray(inp["cache_mem_k"][:, c * nb:(c + 1) * nb]).reshape(L, nb, 256, 512)
    m["cache_mem_v"] = np.ascontiguousarray(inp["cache_mem_v"][:, c * nb:(c + 1) * nb]).reshape(L, nb, 256, 512)
    m["state_conv"] = np.ascontiguousarray(inp["state_conv"][:, c * nb:(c + 1) * nb])
    m["state_shift"] = np.ascontiguousarray(inp["state_shift"][:, c * nb:(c + 1) * nb])
    m["state_wkv"] = np.ascontiguousarray(inp["state_wkv"][:, c * nb:(c + 1) * nb])
    m["page_table"] = np.ascontiguousarray(inp["page_table"][c * nb:(c + 1) * nb]).astype(np.int32)
    return m


def assemble(res, inp, L):
    f = np.float32
    B = inp["x_prompt"].shape[0]
    Bs, Ts, _ = inp["x_sample"].shape
    nb = Bs // len(res)
    cat = lambda k, ax: np.concatenate([r[k] for r in res], axis=ax)
    y_prompt = cat("y_prompt", 0)
    ckv_prompt = cat("ckv_prompt", 0)
    kpe_prompt = cat("kpe_prompt", 0)
    conv_prompt = cat("conv_prompt", 1)
    shift_prompt = cat("shift_prompt", 1)
    wkv_prompt = cat("wkv_prompt", 1)
    mem_k = cat("mem_k_prompt", 1).reshape(L, B, 256, 4, 128)
    mem_v = cat("mem_v_prompt", 1).reshape(L, B, 256, 4, 128)
    y_sample = cat("y_sample", 0).reshape(Bs, Ts, D)
    ckv_sample = np.concatenate([r["ckv_s_out"].reshape(L, nb, Ts, 128) for r in res], axis=1).transpose(1, 0, 2, 3)
    kpe_sample = np.concatenate([r["kpe_s_out"].reshape(L, nb, Ts, 32) for r in res], axis=1).transpose(1, 0, 2, 3)
    conv_sample = np.concatenate([np.concatenate([r["conv_old_out"], r["conv_new_out"].reshape(L, nb, Ts, 256)], axis=2) for r in res], axis=1)
    shift_sample = cat("shift_s_out", 1)
    wkv_sample = cat("wkv_s_out", 1)
    sgu_v = np.concatenate([r["sgv_s_out"].reshape(L, nb, Ts, 256) for r in res], axis=1)
    outs = (y_prompt, y_sample, ckv_prompt, kpe_prompt, ckv_sample, kpe_sample, mem_k, mem_v, conv_prompt, conv_sample,
            shift_prompt, shift_sample, wkv_prompt, wkv_sample, sgu_v)
    return tuple(np.ascontiguousarray(o, dtype=f) for o in outs)


def kernel(**inp):
    inp = {k: np.asarray(v) for k, v in inp.items()}
    B, T, _ = inp["x_prompt"].shape
    NCORE = 8
    L = inp["w_in"].shape[0]
    nph = inp["cache_ckv"].shape[0]
    npg = inp["page_table"].shape[1]
    nc, carr = build(dict(NSEQ=B // NCORE, NT=T // 128, L=L, SG=1, NPG=npg, NPH=nph, AG=1))
    in_maps = [core_inputs(inp, c, NCORE, carr, L, ag=1) for c in range(NCORE)]
    res = run_bass_kernel_spmd(nc, in_maps, core_ids=list(range(NCORE))).results
    return assemble(res, inp, L)
```
